# Optimizing a Trainium2 kernel written in Bass

```python
import jax, jax.numpy as jnp
from jax import lax
import numpy as np

D_MODEL = 2048
BATCH = 8
SEQ = 2048
DEPTH = 1

MIX_WIDTH = D_MODEL
FOURIER_WIDTH = D_MODEL // 4
FOURIER_GROUPS = 4
FOURIER_GROUP_DIM = FOURIER_WIDTH // FOURIER_GROUPS
RWKV_WIDTH = MIX_WIDTH - FOURIER_WIDTH
RWKV_HEAD_DIM = 64
RWKV_HEADS = RWKV_WIDTH // RWKV_HEAD_DIM
DECAY_LORA = 64
ICLR_LORA = 64
GATE_LORA = 224
D_FF = -(-8 * D_MODEL // (3 * 256)) * 256
ALPHA = (2.0 * DEPTH) ** 0.25
BETA = (8.0 * DEPTH) ** -0.25
LN_EPS = 1e-5
GN_EPS = 64e-5

F_OFF = 0
R_OFF = F_OFF + FOURIER_WIDTH
K_OFF = R_OFF + RWKV_WIDTH
V_OFF = K_OFF + RWKV_WIDTH
G_OFF = V_OFF + RWKV_WIDTH
WF_OFF = G_OFF + GATE_LORA
WB_OFF = WF_OFF + DECAY_LORA
AF_OFF = WB_OFF + DECAY_LORA
AB_OFF = AF_OFF + ICLR_LORA
IN_COLS = AB_OFF + ICLR_LORA
SHIFT_COLS = IN_COLS - R_OFF

kernel_name = "fnet_rwkv7_hybrid_deepnorm_block"


def _layernorm(x, g, b):
    xf = x.astype(jnp.float32)
    mu = jnp.mean(xf, -1, keepdims=True)
    var = jnp.mean(jnp.square(xf - mu), -1, keepdims=True)
    return ((xf - mu) * lax.rsqrt(var + LN_EPS)).astype(x.dtype) * g + b


def _centred_shift(p):
    z = jnp.zeros_like(p[:, :1])
    prev = jnp.concatenate([z, p[:, :-1]], axis=1)
    nxt = jnp.concatenate([p[:, 1:], z], axis=1)
    return 0.5 * (prev + nxt)


def _fourier_mixer(u):
    B, S, _ = u.shape
    ug = u.reshape(B, S, FOURIER_GROUPS, FOURIER_GROUP_DIM).astype(jnp.float32)
    f = jnp.fft.fft2(ug, axes=(1, 3), norm="ortho").real
    return f.reshape(B, S, FOURIER_WIDTH).astype(u.dtype)


def _wkv_scan(r, w, k, v, a, b):
    Bn, H2, N = r.shape[1:]

    def step(state, inp):
        r_t, w_t, k_t, v_t, a_t, b_t = inp
        sa = jnp.einsum('bhij,bhj->bhi', state, a_t)
        state = (state * w_t[:, :, None, :] + sa[..., None] * b_t[:, :, None, :]
                 + v_t[..., None] * k_t[:, :, None, :])
        y_t = jnp.einsum('bhij,bhj->bhi', state, r_t)
        return state, y_t

    s0 = jnp.zeros((Bn, H2, N, N), jnp.float32)
    _, y = lax.scan(step, s0, (r, w, k, v, a, b))
    return y


def _rwkv7_bidir(p, mu_shift, w_up_f, w_up_b, w0_f, w0_b, a_up_f, a_up_b, a0_f, a0_b,
                 g_up, k_k, k_a, r_k, lnx_g, lnx_b):
    B, S, _ = p.shape
    H, N = RWKV_HEADS, RWKV_HEAD_DIM
    p = p + (_centred_shift(p) - p) * mu_shift
    o = lambda off, n: p[..., off - R_OFF: off - R_OFF + n]
    r, k, v = o(R_OFF, RWKV_WIDTH), o(K_OFF, RWKV_WIDTH), o(V_OFF, RWKV_WIDTH)
    g = jax.nn.sigmoid(o(G_OFF, GATE_LORA)) @ g_up

    def decay(wd, up, w0):
        wl = -jax.nn.softplus(-(w0 + jnp.tanh(wd) @ up)) - 0.5
        return jnp.exp(-jnp.exp(wl.astype(jnp.float32)))

    w_f = decay(o(WF_OFF, DECAY_LORA), w_up_f, w0_f)
    w_b = decay(o(WB_OFF, DECAY_LORA), w_up_b, w0_b)
    a_f = jax.nn.sigmoid(a0_f + o(AF_OFF, ICLR_LORA) @ a_up_f)
    a_b = jax.nn.sigmoid(a0_b + o(AB_OFF, ICLR_LORA) @ a_up_b)

    heads = lambda t: t.reshape(B, S, H, N)
    kk = heads(k * k_k).astype(jnp.float32)
    kk = kk / jnp.maximum(jnp.linalg.norm(kk, axis=-1, keepdims=True), 1e-12)
    k_f = k * (1.0 + (a_f - 1.0) * k_a)
    k_b = k * (1.0 + (a_b - 1.0) * k_a)

    flip = lambda t: jnp.flip(t, axis=1)
    f32 = lambda t: heads(t).astype(jnp.float32)

    def both(tf, tb):
        return jnp.swapaxes(jnp.concatenate([tf, flip(tb)], axis=2), 0, 1)

    y = _wkv_scan(both(f32(r), f32(r)), both(f32(w_f), f32(w_b)), both(f32(k_f), f32(k_b)),
                  both(f32(v), f32(v)), both(-kk, -kk),
                  both(kk * f32(a_f), kk * f32(a_b)))
    y = jnp.swapaxes(y, 0, 1)
    y = y[:, :, :H] + flip(y[:, :, H:])
    mu = jnp.mean(y, -1, keepdims=True)
    var = jnp.mean(jnp.square(y - mu), -1, keepdims=True)
    y = ((y - mu) * lax.rsqrt(var + GN_EPS)).reshape(B, S, RWKV_WIDTH).astype(p.dtype)
    y = y * lnx_g + lnx_b
    bonus = jnp.sum(heads(r) * heads(k_f + k_b) * r_k, -1, keepdims=True) * heads(v)
    return (y + bonus.reshape(B, S, RWKV_WIDTH)) * g


def setup_inputs(seed: int = 0) -> dict:
    key = jax.random.key(seed)
    ks = iter(jax.random.split(key, 32))
    L, D, RW = DEPTH, D_MODEL, RWKV_WIDTH
    nrm = lambda shape, scale: jax.random.normal(next(ks), shape, jnp.float32) * scale
    uni = lambda shape, lo, hi: jax.random.uniform(next(ks), shape, jnp.float32, lo, hi)
    x = nrm((BATCH, SEQ, D), 1.0)
    w_in = nrm((L, D, IN_COLS), D ** -0.5)
    w_in = w_in.at[:, :, V_OFF:V_OFF + RW].multiply(BETA)
    return {
        "x": x,
        "w_in": w_in,
        "mu_shift": uni((L, SHIFT_COLS), 0.0, 1.0),
        "w_up_f": nrm((L, DECAY_LORA, RW), 0.1 * DECAY_LORA ** -0.5),
        "w_up_b": nrm((L, DECAY_LORA, RW), 0.1 * DECAY_LORA ** -0.5),
        "w0_f": uni((L, RW), -5.0, 0.0),
        "w0_b": uni((L, RW), -5.0, 0.0),
        "a_up_f": nrm((L, ICLR_LORA, RW), ICLR_LORA ** -0.5),
        "a_up_b": nrm((L, ICLR_LORA, RW), ICLR_LORA ** -0.5),
        "a0_f": nrm((L, RW), 0.1),
        "a0_b": nrm((L, RW), 0.1),
        "g_up": nrm((L, GATE_LORA, RW), GATE_LORA ** -0.5),
        "k_k": 0.85 + nrm((L, RW), 0.02),
        "k_a": 1.0 + nrm((L, RW), 0.02),
        "r_k": nrm((L, RWKV_HEADS, RWKV_HEAD_DIM), 0.1),
        "lnx_g": 1.0 + nrm((L, RW), 0.02),
        "lnx_b": nrm((L, RW), 0.02),
        "w_out": nrm((L, MIX_WIDTH, D), BETA * MIX_WIDTH ** -0.5),
        "ln1_g": 1.0 + nrm((L, D), 0.02),
        "ln1_b": nrm((L, D), 0.02),
        "w_ffn_gate": nrm((L, D, D_FF), BETA * D ** -0.5),
        "w_ffn_up": nrm((L, D, D_FF), BETA * D ** -0.5),
        "w_ffn_down": nrm((L, D_FF, D), BETA * D_FF ** -0.5),
        "ln2_g": 1.0 + nrm((L, D), 0.02),
        "ln2_b": nrm((L, D), 0.02),
    }


def reference(x, w_in, mu_shift, w_up_f, w_up_b, w0_f, w0_b, a_up_f, a_up_b, a0_f, a0_b,
              g_up, k_k, k_a, r_k, lnx_g, lnx_b, w_out, ln1_g, ln1_b,
              w_ffn_gate, w_ffn_up, w_ffn_down, ln2_g, ln2_b):
    for l in range(DEPTH):
        proj = jnp.einsum('bsd,dc->bsc', x, w_in[l])
        y_fourier = _fourier_mixer(proj[..., F_OFF:R_OFF])
        y_rwkv = _rwkv7_bidir(proj[..., R_OFF:], mu_shift[l], w_up_f[l], w_up_b[l],
                              w0_f[l], w0_b[l], a_up_f[l], a_up_b[l], a0_f[l], a0_b[l],
                              g_up[l], k_k[l], k_a[l], r_k[l], lnx_g[l], lnx_b[l])
        mix = jnp.einsum('bsc,cd->bsd', jnp.concatenate([y_fourier, y_rwkv], -1), w_out[l])
        h = _layernorm(ALPHA * x + mix, ln1_g[l], ln1_b[l])
        ff = jax.nn.silu(h @ w_ffn_gate[l]) * (h @ w_ffn_up[l])
        x = _layernorm(ALPHA * h + ff @ w_ffn_down[l], ln2_g[l], ln2_b[l])
    return x
```

```python
import contextlib
import numpy as np
import ml_dtypes
import concourse.bass as bass
import concourse.mybir as mybir
from concourse.bass_utils import run_bass_kernel_spmd

F32 = mybir.dt.float32
BF16 = mybir.dt.bfloat16
AF = mybir.ActivationFunctionType
ALU = mybir.AluOpType
AX = mybir.AxisListType

D = 2048
S = 2048
FW = 512
RW = 1536
NH = 24
N = 64
R_OFF = 512
K_OFF = R_OFF + RW
V_OFF = K_OFF + RW
G_OFF = V_OFF + RW
WF_OFF = G_OFF + 224
AF_OFF = WF_OFF + 128
IN_COLS = 5600
DFF = 5632
ALPHA = 2.0 ** 0.25
LN_EPS = 1e-5
GN_EPS = 64e-5
CDEC = float(np.exp(-0.5))
NHP = 12
NCP = 124

ENGS = ("pe", "act", "dve", "pool", "sp")
EPOCH = 12000
NDSEM = 16


class Op:
    __slots__ = ("eng", "idx", "fn", "deps", "is_dma", "needs_inc", "semval", "dma_slot", "dma_val")

    def __init__(self, eng, idx, fn, is_dma):
        self.eng = eng
        self.idx = idx
        self.fn = fn
        self.deps = []
        self.is_dma = is_dma
        self.needs_inc = False
        self.semval = None
        self.dma_slot = None
        self.dma_val = None


class Prog:
    def __init__(self, nc):
        self.nc = nc
        self.ops = {e: [] for e in ENGS}
        self.last_w = {}
        self.readers = {}
        self.ndma = {e: 0 for e in ENGS}
        self.dma_ring = {e: [None] * NDSEM for e in ENGS}
        self.fence_ops = []

    def _add(self, eng, fn, reads, writes, is_dma):
        op = Op(eng, len(self.ops[eng]), fn, is_dma)
        deps = list(self.fence_ops)
        xr = [k for k in reads if len(k) == 2 and k[0] == "B"]
        if xr:
            reads = [k for k in reads if k not in xr]
            writes = list(writes) + xr
        for k in reads:
            w = self.last_w.get(k)
            if w is not None:
                deps.append(w)
        for k in writes:
            w = self.last_w.get(k)
            if w is not None:
                deps.append(w)
            deps.extend(self.readers.get(k, ()))
        if is_dma:
            slot = self.ndma[eng] % NDSEM
            prev = self.dma_ring[eng][slot]
            if prev is not None:
                deps.append(prev)
            op.dma_slot = slot
            op.dma_val = 16 * (self.ndma[eng] // NDSEM + 1)
            self.dma_ring[eng][slot] = op
            self.ndma[eng] += 1
        op.deps = deps
        for k in reads:
            self.readers.setdefault(k, []).append(op)
        for k in writes:
            self.last_w[k] = op
            self.readers[k] = []
        self.ops[eng].append(op)
        return op

    def op(self, eng, fn, reads=(), writes=()):
        return self._add(eng, fn, reads, writes, False)

    def dma(self, eng, out, in_, reads=(), writes=()):
        def fn(e):
            return e.dma_start(out=out, in_=in_)
        return self._add(eng, fn, reads, writes, True)

    def fence(self):
        f = []
        for e in ENGS:
            if self.ops[e]:
                last_c = None
                for o in reversed(self.ops[e]):
                    if not o.is_dma:
                        last_c = o
                        break
                if last_c is not None:
                    f.append(last_c)
            for o in self.dma_ring[e]:
                if o is not None:
                    f.append(o)
        self.fence_ops = f
        self.last_w = {}
        self.readers = {}

    def emit(self, final_wait_ops=()):
        nc = self.nc
        for e in ENGS:
            for op in self.ops[e]:
                nd = []
                seen = set()
                for d in op.deps:
                    if d is op or id(d) in seen:
                        continue
                    seen.add(id(d))
                    if d.is_dma:
                        nd.append(d)
                        continue
                    if d.eng == op.eng and not op.is_dma:
                        if d.eng in ("pe", "sp"):
                            continue
                        if d.idx < op.idx - 2:
                            continue
                    d.needs_inc = True
                    nd.append(d)
                op.deps = nd
        for o in final_wait_ops:
            if not o.is_dma:
                o.needs_inc = True
        nep = {}
        for e in ENGS:
            c = 0
            for op in self.ops[e]:
                if not op.is_dma and op.needs_inc:
                    op.semval = c
                    c += 1
            nep[e] = max(1, -(-c // EPOCH))
        with contextlib.ExitStack() as st:
            csem = {e: [st.enter_context(nc.semaphore(f"c_{e}_{i}")) for i in range(nep[e])] for e in ENGS}
            dsem = {e: [st.enter_context(nc.semaphore(f"d_{e}_{i}")) for i in range(NDSEM)]
                    for e in ENGS if self.ndma[e] > 0}
            block = st.enter_context(nc.Block())

            def run(e, eng):
                waited = {}
                for op in self.ops[e]:
                    for d in op.deps:
                        if d.is_dma:
                            key = ("d", d.eng, d.dma_slot)
                            val = d.dma_val
                            sem = dsem[d.eng][d.dma_slot]
                        else:
                            ep, v = divmod(d.semval, EPOCH)
                            key = ("c", d.eng, ep)
                            val = v + 1
                            sem = csem[d.eng][ep]
                        if waited.get(key, 0) >= val:
                            continue
                        waited[key] = val
                        eng.wait_ge(sem, val)
                    inst = op.fn(eng)
                    if op.is_dma:
                        inst.then_inc(dsem[e][op.dma_slot], 16)
                    elif op.needs_inc:
                        ep, v = divmod(op.semval, EPOCH)
                        inst.then_inc(csem[e][ep], 1)
                if e == "sp":
                    for o in final_wait_ops:
                        if o.is_dma:
                            eng.wait_ge(dsem[o.eng][o.dma_slot], o.dma_val)
                        else:
                            ep, v = divmod(o.semval, EPOCH)
                            eng.wait_ge(csem[o.eng][ep], v + 1)

            @block.tensor
            def _(eng):
                run("pe", eng)

            @block.scalar
            def _(eng):
                run("act", eng)

            @block.vector
            def _(eng):
                run("dve", eng)

            @block.gpsimd
            def _(eng):
                run("pool", eng)

            @block.sync
            def _(eng):
                run("sp", eng)


class Arena:
    def __init__(self, ap, ncols):
        self.ap = ap
        self.n = ncols
        self.top = 0
        self.peak = 0

    def alloc(self, cols, dtype=F32):
        a = self.ap[:, self.top:self.top + cols]
        self.top += cols
        self.peak = max(self.peak, self.top)
        assert self.top <= self.n, f"SBUF arena overflow {self.top} > {self.n}"
        if dtype == BF16:
            a = a.bitcast(BF16)
        return a

    def mark(self):
        return self.top

    def release(self, m):
        self.top = m


C_M4F = 0
C_M4B = 512
C_MTF = 1024
C_MTB = 1152
C_ID = 1280
C_BO = 1408
C_BI = 1536
NCONST = 1538


def _host_consts():
    idx = np.arange(128)
    sf = (idx[:, None] < idx[None, :]).astype(np.float32)
    inf_ = (idx[:, None] <= idx[None, :]).astype(np.float32)
    sb = (idx[:, None] > idx[None, :]).astype(np.float32)
    inb = (idx[:, None] >= idx[None, :]).astype(np.float32)
    c = np.zeros((128, NCONST), np.float32)
    c[:, C_M4F:C_M4F + 512] = np.concatenate([sf, inf_, sf, inf_], 1)
    c[:, C_M4B:C_M4B + 512] = np.concatenate([sb, inb, sb, inb], 1)
    c[:, C_MTF:C_MTF + 128] = sf.T
    c[:, C_MTB:C_MTB + 128] = sb.T
    c[:, C_ID:C_ID + 128] = np.eye(128, dtype=np.float32)
    bo = np.zeros((128, 128), np.float32)
    bo[:64, :64] = 1
    bo[64:, 64:] = 1
    c[:, C_BO:C_BO + 128] = bo
    c[:64, C_BI] = 1
    c[64:, C_BI + 1] = 1
    return c


def _host_dft():
    t = np.arange(S, dtype=np.int64)
    m = (t[:, None] * t[None, :]) % S
    ang = 2.0 * np.pi * m.astype(np.float64) / S
    cs = np.cos(ang).astype(ml_dtypes.bfloat16)
    ss = np.sin(ang).astype(ml_dtypes.bfloat16)
    n = np.arange(128, dtype=np.int64)
    mn = (n[:, None] * n[None, :]) % 128
    an = 2.0 * np.pi * mn.astype(np.float64) / 128
    norm = 1.0 / np.sqrt(S * 128.0)
    dn = np.concatenate([np.cos(an) * norm, -np.sin(an) * norm], 1).astype(ml_dtypes.bfloat16)
    return cs, ss, dn


class _Stop(Exception):
    pass


def build_program(taps=None, stop_after=None):
    try:
        return _build_program(taps, stop_after)
    except _Stop as e:
        return e.args[0]


def _build_program(taps=None, stop_after=None):
    taps = taps or {}
    nc = bass.Bass("TRN2", target_bir_lowering=False)
    din = lambda n, s, d=F32: nc.dram_tensor(n, list(s), d, kind="ExternalInput").ap()
    xT_d = din("xT", [D, S])
    x_d = din("x", [S, D])
    win_d = din("w_in", [D, IN_COLS])
    wout_d = din("w_out", [D, D])
    wg_d = din("w_gate", [D, DFF])
    wu_d = din("w_up", [D, DFF])
    wd_d = din("w_down", [DFF, D])
    wupfb_d = din("wupfb", [128, RW])
    aupfb_d = din("aupfb", [128, RW])
    gup_d = din("g_up", [224, RW])
    cp_d = din("cp", [128, NCP])
    lnx_d = din("lnx", [2, RW])
    ln12_d = din("ln12", [4, D])
    consts_d = din("consts", [128, NCONST])
    dftc_d = din("dftC", [S, S], BF16)
    dfts_d = din("dftS", [S, S], BF16)
    dftn_d = din("dftN", [128, 256], BF16)
    out_d = nc.dram_tensor("out", [S, D], F32, kind="ExternalOutput").ap()
    ycat_d = nc.dram_tensor("ycat_scr", [16, 128, S], BF16).ap()
    tap_out = {}

    with contextlib.ExitStack() as top:
        ARENA_COLS = 52600
        arena_t = top.enter_context(nc.sbuf_tensor("arena", [128, ARENA_COLS], F32))
        A = Arena(arena_t, ARENA_COLS)
        banks = [top.enter_context(nc.psum_tensor(f"bank{i}", [128, 512], F32)) for i in range(8)]
        P = Prog(nc)
        finals = []

        def tap(name, ap, keys, shape, dtype=F32):
            if name not in taps:
                return
            t = nc.dram_tensor("tap_" + name, list(shape), dtype, kind="ExternalOutput").ap()
            tap_out[name] = t
            sel = taps[name]
            finals.append(P.dma("sp", sel(t) if callable(sel) else t, ap, reads=keys))

        def cut(name):
            if stop_after == name:
                for e_ in ("pe", "act", "dve", "pool"):
                    for o_ in reversed(P.ops[e_]):
                        if not o_.is_dma:
                            finals.append(o_)
                            break
                P.emit(finals)
                raise _Stop((nc, tap_out))

        def mm(out, lhsT, rhs, start=True, stop=True, reads=(), writes=()):
            return P.op("pe", lambda e: e.matmul(out, lhsT, rhs, start=start, stop=stop), reads, writes)

        def tr(out, in_, ident, reads=(), writes=()):
            return P.op("pe", lambda e: e.transpose(out, in_, ident), reads, writes)

        def act(out, in_, func, bias=None, scale=1.0, reads=(), writes=(), accum_out=None):
            kw = {}
            if bias is not None:
                kw["bias"] = bias
            if accum_out is not None:
                kw["accum_out"] = accum_out
            return P.op("act", lambda e: e.activation(out=out, in_=in_, func=func, scale=scale, **kw), reads, writes)

        def tt(eng, out, in0, in1, op, reads=(), writes=()):
            return P.op(eng, lambda e: e.tensor_tensor(out=out, in0=in0, in1=in1, op=op), reads, writes)

        def ts(eng, out, in0, s1, op0, s2=None, op1=None, reads=(), writes=()):
            if op1 is None:
                return P.op(eng, lambda e: e.tensor_scalar(out=out, in0=in0, scalar1=s1, scalar2=None, op0=op0), reads, writes)
            return P.op(eng, lambda e: e.tensor_scalar(out=out, in0=in0, scalar1=s1, scalar2=s2, op0=op0, op1=op1), reads, writes)

        def stt(eng, out, in0, scalar, in1, op0, op1, reads=(), writes=()):
            return P.op(eng, lambda e: e.scalar_tensor_tensor(out=out, in0=in0, scalar=scalar, in1=in1, op0=op0, op1=op1), reads, writes)

        def cp_act(out, in_, reads=(), writes=()):
            return P.op("act", lambda e: e.copy(out, in_), reads, writes)

        def cp_dve(out, in_, reads=(), writes=()):
            return P.op("dve", lambda e: e.tensor_copy(out, in_), reads, writes)

        CT = A.alloc(NCONST)
        CPt = A.alloc(NCP)
        OMt = A.alloc(NCP)
        HMt = A.alloc(NCP)
        ident = CT[:, C_ID:C_ID + 128]
        blockones = CT[:, C_BO:C_BO + 128]
        blockind = CT[:, C_BI:C_BI + 2]
        P.dma("sp", CT, consts_d, writes=["CT"])
        P.dma("sp", CPt, cp_d, writes=["CP"])
        ts("dve", OMt, CPt, -1.0, ALU.mult, 1.0, ALU.add, reads=["CP"], writes=["OM"])
        ts("dve", HMt, CPt, 0.5, ALU.mult, reads=["CP"], writes=["HM"])
        ones128 = A.alloc(128)
        P.op("pool", lambda e: e.memset(ones128, 1.0), writes=["ones"])
        tiny_col = A.alloc(1)
        P.op("pool", lambda e: e.memset(tiny_col, 1e-24), writes=["tiny"])

        mixer_mark = A.mark()
        xTs = A.alloc(16 * S // 2, BF16).rearrange("p (k t) -> p k t", k=16)
        for kc in range(16):
            P.dma("pool", xTs[:, kc, :], xT_d[kc * 128:(kc + 1) * 128, :], writes=[f"xT{kc}"])
        XK = [f"xT{kc}" for kc in range(16)]

        NW = 2
        Wb = [A.alloc(16 * 128 // 2, BF16).rearrange("p (k c) -> p k c", k=16) for _ in range(NW)]
        wctr = [0]
        pbank = [0]

        def prefetch_W(c0, ncol):
            wi = wctr[0] % NW
            wctr[0] += 1
            W = Wb[wi]
            wk = f"W{wi}"
            P.dma("pool", W[:, :, 0:ncol], win_d[:, c0:c0 + ncol].rearrange("(k p) c -> p k c", p=128), writes=[wk])
            return (W, wk)

        def proj_cols(c0, ncol, evac, pre=None):
            W, wk = pre if pre is not None else prefetch_W(c0, ncol)
            for tb in range(4):
                bi = pbank[0] % 2
                pbank[0] += 1
                bk = f"B{bi}"
                ps = banks[bi][0:ncol, :]
                for kc in range(16):
                    mm(ps, W[:, kc, 0:ncol], xTs[:, kc, tb * 512:(tb + 1) * 512], start=(kc == 0), stop=(kc == 15),
                       reads=[wk, f"xT{kc}"], writes=[bk])
                evac(tb, ps, bk)

        def token_shift(raw, rk, npart, col, tmp, tk):
            tt("dve", tmp[0:npart, :], raw[0:npart, 0:S], raw[0:npart, 2:S + 2], ALU.add, reads=[rk], writes=[tk])
            act(raw[0:npart, 1:S + 1], raw[0:npart, 1:S + 1], AF.Identity, scale=OMt[0:npart, col:col + 1], reads=[rk, "OM"], writes=[rk])
            stt("dve", raw[0:npart, 1:S + 1], tmp[0:npart, :], HMt[0:npart, col:col + 1], raw[0:npart, 1:S + 1],
                ALU.mult, ALU.add, reads=[rk, tk, "HM"], writes=[rk])

        sg0 = A.alloc(S // 2, BF16)
        sg1 = A.alloc(S // 2, BF16)
        twfb = A.alloc(S // 2, BF16)
        alfb = A.alloc(S // 2, BF16)

        a2_mark = A.mark()
        uT = A.alloc(S // 2, BF16)
        Acs = [A.alloc(16 * 256 // 2, BF16).rearrange("p (c n) -> p c n", c=16) for _ in range(4)]
        dftN = A.alloc(128, BF16)
        P.dma("sp", dftN, dftn_d, writes=["dftN"])
        yfo = [A.alloc(S // 2, BF16) for _ in range(4)]
        for g in range(4):
            def evac_u(tb, ps, bk):
                cp_act(uT[:, tb * 512:(tb + 1) * 512], ps, reads=[bk], writes=["uT"])
            proj_cols(g * 128, 128, evac_u)
            for tcp in range(8):
                bi = pbank[0] % 2
                pbank[0] += 1
                bk = f"B{bi}"
                for j in range(2):
                    tc = tcp * 2 + j
                    mm(banks[bi][:, j * 256:(j + 1) * 256], uT[:, tc * 128:(tc + 1) * 128], dftN,
                       reads=["uT", "dftN"], writes=[bk])
                cp_dve(Acs[g][:, tcp * 2:tcp * 2 + 2, :], banks[bi][:, :].rearrange("p (c n) -> p c n", c=2),
                       reads=[bk], writes=[f"Acs{g}"])
        tap("acs0", Acs[0], ["Acs0"], [128, 16, 256], BF16)
        dpc = [A.alloc(4 * 512 // 2, BF16).rearrange("p (j t) -> p j t", j=4) for _ in range(2)]
        dps = [A.alloc(4 * 512 // 2, BF16).rearrange("p (j t) -> p j t", j=4) for _ in range(2)]
        pc = 0
        for tpb in range(4):
            for q in range(4):
                bsel = pc % 2
                pc += 1
                P.dma("sp", dpc[bsel], dftc_d[q * 512:(q + 1) * 512, tpb * 512:(tpb + 1) * 512].rearrange("(j p) t -> p j t", p=128),
                      writes=[f"dpc{bsel}"])
                P.dma("sp", dps[bsel], dfts_d[q * 512:(q + 1) * 512, tpb * 512:(tpb + 1) * 512].rearrange("(j p) t -> p j t", p=128),
                      writes=[f"dps{bsel}"])
                for g in range(4):
                    for j in range(4):
                        tc = q * 4 + j
                        mm(banks[2 + g][:, :], Acs[g][:, tc, 0:128], dpc[bsel][:, j, :], start=(tc == 0), stop=False,
                           reads=[f"Acs{g}", f"dpc{bsel}"], writes=[f"B{2 + g}"])
                        mm(banks[2 + g][:, :], Acs[g][:, tc, 128:256], dps[bsel][:, j, :], start=False, stop=(tc == 15),
                           reads=[f"Acs{g}", f"dps{bsel}"], writes=[f"B{2 + g}"])
            for g in range(4):
                if g % 2 == 0:
                    cp_act(yfo[g][:, tpb * 512:(tpb + 1) * 512], banks[2 + g][:, :], reads=[f"B{2 + g}"], writes=[f"yfo{g}"])
                else:
                    cp_dve(yfo[g][:, tpb * 512:(tpb + 1) * 512], banks[2 + g][:, :], reads=[f"B{2 + g}"], writes=[f"yfo{g}"])
        for g in range(4):
            P.dma("sp", ycat_d[g], yfo[g], reads=[f"yfo{g}"], writes=[f"ycat{g}"])
        tap("yfo0", yfo[0], ["yfo0"], [128, S], BF16)

        lraw = A.alloc(S + 2)
        ltmp = A.alloc(S)
        P.op("pool", lambda e: e.memset(lraw[:, 0:1], 0.0), writes=["lraw"])
        P.op("pool", lambda e: e.memset(lraw[:, S + 1:S + 2], 0.0), writes=["lraw"])
        lora_specs = [(G_OFF, 128, sg0, "sg0", AF.Sigmoid, 120), (G_OFF + 128, 96, sg1, "sg1", AF.Sigmoid, 121),
                      (WF_OFF, 128, twfb, "twfb", AF.Tanh, 122), (AF_OFF, 128, alfb, "alfb", AF.Identity, 123)]
        for (c0, ncol, dst, dk, fn, col) in lora_specs:
            def evac_l(tb, ps, bk, ncol=ncol):
                cp_act(lraw[0:ncol, 1 + tb * 512:1 + (tb + 1) * 512], ps, reads=[bk], writes=["lraw"])
            proj_cols(c0, ncol, evac_l)
            token_shift(lraw, "lraw", ncol, col, ltmp, "ltmp")
            act(dst[0:ncol, :], lraw[0:ncol, 1:S + 1], fn, reads=["lraw"], writes=[dk])
        tap("twfb", twfb, ["twfb"], [128, S], BF16)
        tap("sg1", sg1, ["sg1"], [128, S], BF16)

        if stop_after == "A1":
            P.emit(finals)
            return nc, tap_out

        P.fence()
        A.release(a2_mark)
        rawr = A.alloc(S + 2)
        rawk = A.alloc(S + 2)
        rawv = A.alloc(S + 2)
        for rw_, k_ in ((rawr, "rawr"), (rawk, "rawk"), (rawv, "rawv")):
            P.op("pool", lambda e, rw_=rw_: e.memset(rw_[:, 0:1], 0.0), writes=[k_])
            P.op("pool", lambda e, rw_=rw_: e.memset(rw_[:, S + 1:S + 2], 0.0), writes=[k_])
        VT = A.alloc(S // 2, BF16).rearrange("p (c n) -> p c n", c=16)
        YF = A.alloc(S).rearrange("p (c n) -> p c n", c=16)
        YFflat = YF.rearrange("p c n -> p (c n)")
        Sd = [A.alloc(32).rearrange("p (c h) -> p c h", c=16) for _ in range(2)]
        WCt = A.alloc(32).rearrange("p (d c) -> p d c", d=2)
        lw_hp = A.alloc(4 * 64, BF16).rearrange("p (w c) -> p w c", w=4)
        lnxb = A.alloc(256).rearrange("p (w c) -> p w c", w=2)
        ych = rawk[:, 2:2 + S // 2].bitcast(BF16)
        AR = [[A.alloc(512, BF16).rearrange("p (c n) -> p c n", c=4) for _ in range(2)] for _ in range(2)]
        BKb = [[A.alloc(512, BF16).rearrange("p (c n) -> p c n", c=4) for _ in range(2)] for _ in range(2)]
        BK = [A.alloc(1024).rearrange("p (c n) -> p c n", c=4) for _ in range(2)]
        BKT = [[A.alloc(512, BF16).rearrange("p (c n) -> p c n", c=4) for _ in range(2)] for _ in range(2)]
        NT = 9
        Tm = [A.alloc(512) for _ in range(NT)]
        SC = [[A.alloc(256, BF16) for _ in range(2)] for _ in range(4)]
        TTt = [[A.alloc(64, BF16) for _ in range(2)] for _ in range(4)]
        PXT = [[A.alloc(192, BF16) for _ in range(2)] for _ in range(4)]
        identb = A.alloc(64, BF16)
        cp_act(identb, ident, reads=["CT"], writes=["identb"])
        Zs_all = A.alloc(128, BF16)
        Zs = [Zs_all[:, ch_ * 64:(ch_ + 1) * 64] for ch_ in range(4)]
        Us_all = A.alloc(128, BF16)
        Us = [Us_all[:, ch_ * 64:(ch_ + 1) * 64] for ch_ in range(4)]
        Hb = [A.alloc(32, BF16) for _ in range(2)]
        Hst = [A.alloc(64) for _ in range(2)]
        gS = Tm[0]
        small = A.alloc(32 * 4)
        s_sum, s_nm, s_var, s_ss = (small[:, i * 32:(i + 1) * 32] for i in range(4))

        n_hp = int(stop_after[2:]) if str(stop_after).startswith("hp") else NHP
        nextW = []
        for hp in range(n_hp):
            cb = hp * 10
            col = lambda j: CPt[:, cb + j:cb + j + 1]
            hpc = slice(hp * 128, (hp + 1) * 128)
            P.dma("pool", lw_hp[:, 0, :], wupfb_d[:, hpc], writes=["lw0"])
            P.dma("pool", lw_hp[:, 1, :], aupfb_d[:, hpc], writes=["lw1"])
            P.dma("pool", lw_hp[:, 2, :], gup_d[0:128, hpc], writes=["lw2"])
            P.dma("pool", lw_hp[0:96, 3, :], gup_d[128:224, hpc], writes=["lw3"])
            P.dma("sp", lnxb, lnx_d[:, hpc].partition_broadcast(128), writes=["lnxb"])
            for j, (rw_, rk_) in enumerate(((rawr, "rawr"), (rawk, "rawk"), (rawv, "rawv"))):
                def evac_r(tb, ps, bk, rw_=rw_, rk_=rk_):
                    cp_act(rw_[:, 1 + tb * 512:1 + (tb + 1) * 512], ps, reads=[bk], writes=[rk_])
                proj_cols(R_OFF + j * RW + hp * 128, 128, evac_r, pre=(nextW.pop(0) if nextW else None))
                token_shift(rw_, rk_, 128, cb + j, YFflat, "YF")
            r_fm = rawr[:, 1:S + 1]
            k_fm = rawk[:, 1:S + 1]
            v_fm = rawv[:, 1:S + 1]
            if hp == 0:
                tap("r0", r_fm, ["rawr"], [128, S])
                tap("v0", v_fm, ["rawv"], [128, S])
            for cq in range(4):
                bi = pbank[0] % 2
                pbank[0] += 1
                bk = f"B{bi}"
                for j in range(4):
                    c = cq * 4 + j
                    tr(banks[bi][:, j * 128:(j + 1) * 128], v_fm[:, c * 128:(c + 1) * 128], ident, reads=["rawv", "CT"], writes=[bk])
                cp_act(VT[:, cq * 4:cq * 4 + 4, :], banks[bi][:, :].rearrange("p (c n) -> p c n", c=4), reads=[bk], writes=["VT"])

            cut("c_proj")
            P.op("pool", lambda e: e.memset(Hst[0], 0.0), writes=["H0"])
            P.op("pool", lambda e: e.memset(Hst[1], 0.0), writes=["H1"])

            def precompute(it, dr):
                pb = it % 2
                tb = it if dr == 0 else 3 - it
                bsl = slice(tb * 512, (tb + 1) * 512)
                dsl = slice(dr * 64, (dr + 1) * 64)
                sgm, cs, csx, e1, e2, e3, asg, kkr, sq = Tm
                t1 = csx
                k_sgm, k_cs, k_csx, k_e1, k_e2, k_e3, k_asg, k_kkr, k_sq = [f"T{i}" for i in range(9)]
                k_t1 = k_csx
                ARk, BKk, BKTk = f"AR{dr}_{pb}", f"BK{dr}", f"BKT{dr}_{pb}"
                v4 = lambda a: a.rearrange("p (c t) -> p c t", c=4)
                mm(banks[0][:, :], lw_hp[dsl, 0, :], twfb[dsl, bsl], reads=["lw0", "twfb"], writes=["B0"])
                mm(banks[1][:, :], lw_hp[dsl, 1, :], alfb[dsl, bsl], reads=["lw1", "alfb"], writes=["B1"])
                act(sgm, banks[0][:, :], AF.Sigmoid, bias=col(3 + dr), reads=["B0", "CP"], writes=[k_sgm])
                act(asg, banks[1][:, :], AF.Sigmoid, bias=col(5 + dr), reads=["B1", "CP"], writes=[k_asg])
                ts("dve", kkr, k_fm[:, bsl], col(7), ALU.mult, reads=["rawk", "CP"], writes=[k_kkr])
                yield
                tt("pool", sq, kkr, kkr, ALU.mult, reads=[k_kkr], writes=[k_sq])
                for c in range(4):
                    P.op("dve", lambda e, c=c: e.tensor_tensor_scan(cs[:, c * 128:(c + 1) * 128], ones128,
                                                                    sgm[:, c * 128:(c + 1) * 128], 0.0, ALU.mult, ALU.add),
                         reads=[k_sgm, "ones"], writes=[k_cs])
                    if c == 1:
                        yield
                yield
                mm(banks[0][:, :], blockones, sq, reads=["CT", k_sq], writes=["B0"])
                tt("pool", csx, cs, sgm, ALU.subtract, reads=[k_cs, k_sgm], writes=[k_csx])
                act(WCt[:, dr, tb * 4:tb * 4 + 4], cs[:, 127:512:128], AF.Exp, scale=-CDEC, reads=[k_cs], writes=["WC"])
                yield
                act(sq, banks[0][:, :], AF.Ln, bias=tiny_col, reads=["B0", "tiny"], writes=[k_sq])
                act(sq, sq, AF.Exp, scale=-0.5, reads=[k_sq], writes=[k_sq])
                yield
                if dr == 0:
                    act(e1, cs, AF.Exp, scale=-CDEC, reads=[k_cs], writes=[k_e1])
                    act(e2, csx, AF.Exp, scale=-CDEC, reads=[k_csx], writes=[k_e2])
                    act(e3, cs, AF.Exp, scale=CDEC, reads=[k_cs], writes=[k_e3])
                else:
                    act(e1, csx, AF.Exp, scale=CDEC, reads=[k_csx], writes=[k_e1])
                    act(e2, cs, AF.Exp, scale=CDEC, reads=[k_cs], writes=[k_e2])
                    act(e3, csx, AF.Exp, scale=-CDEC, reads=[k_csx], writes=[k_e3])
                tt("pool", kkr, kkr, sq, ALU.mult, reads=[k_kkr, k_sq], writes=[k_kkr])
                yield
                ts("dve", t1, asg, col(8), ALU.mult, OMt[:, cb + 8:cb + 9], ALU.add, reads=[k_asg, "CP", "OM", k_csx], writes=[k_t1])
                yield
                tt("pool", t1, t1, k_fm[:, bsl], ALU.mult, reads=[k_t1, "rawk"], writes=[k_t1])
                tt("dve", AR[dr][pb][:, :, 128:256], v4(r_fm[:, bsl]), v4(e1), ALU.mult, reads=["rawr", k_e1], writes=[ARk])
                yield
                tt("pool", asg, asg, kkr, ALU.mult, reads=[k_asg, k_kkr], writes=[k_asg])
                stt("dve", AR[dr][pb][:, :, 0:128], v4(kkr), -1.0, v4(e2), ALU.mult, ALU.mult, reads=[k_kkr, k_e2], writes=[ARk])
                yield
                tt("pool", BK[dr][:, :, 0:128], v4(asg), v4(e3), ALU.mult, reads=[k_asg, k_e3], writes=[BKk])
                stt("dve", e1, r_fm[:, bsl], col(9), t1, ALU.mult, ALU.mult, reads=["rawr", "CP", k_t1, k_e1, ARk], writes=[k_e1])
                yield
                tt("pool", BK[dr][:, :, 128:256], v4(t1), v4(e3), ALU.mult, reads=[k_t1, k_e3], writes=[BKk])
                for c in range(4):
                    mm(banks[1][:, c * 2:c * 2 + 2], e1[:, c * 128:(c + 1) * 128], blockind, reads=[k_e1, "CT"], writes=["B1"])
                cp_act(Sd[dr][:, tb * 4:tb * 4 + 4, :], banks[1][:, 0:8].rearrange("p (c h) -> p c h", c=4), reads=["B1"], writes=[f"Sd{dr}"])
                yield
                cp_act(BKb[dr][pb], BK[dr], reads=[BKk], writes=[f"BKb{dr}_{pb}"])
                for cpair in range(2):
                    bi = cpair
                    bk = f"B{bi}"
                    for j in range(2):
                        c = cpair * 2 + j
                        tr(banks[bi][:, j * 256:j * 256 + 128], BK[dr][:, c, 0:128], ident, reads=[BKk, "CT"], writes=[bk])
                        tr(banks[bi][:, j * 256 + 128:j * 256 + 256], BK[dr][:, c, 128:256], ident, reads=[BKk, "CT"], writes=[bk])
                    cp_act(BKT[dr][pb][:, cpair * 2:cpair * 2 + 2, :], banks[bi][:, :].rearrange("p (c n) -> p c n", c=2), reads=[bk], writes=[BKTk])
                    yield
                if hp == 0 and it == 0:
                    tap(f"AR{dr}", AR[dr][pb], [ARk], [128, 4, 256], BF16)
                    tap(f"BK{dr}", BK[dr], [BKk], [128, 4, 256])
                    tap(f"BKT{dr}", BKT[dr][pb], [BKTk], [128, 4, 256], BF16)

            def chains_of(it, cl):
                chains = []
                for dr in range(2):
                    tb = it if dr == 0 else 3 - it
                    lc = cl if dr == 0 else 3 - cl
                    chunk = tb * 4 + lc
                    for h in range(2):
                        chains.append((dr, h, lc, chunk, dr * 2 + h))
                return chains

            def S_stages(it, cl):
                chains = chains_of(it, cl)
                pb = it % 2
                par = (it * 4 + cl) % 2
                st = []

                def scores():
                    for (dr, h, lc, chunk, ch) in chains:
                        hs = slice(h * 64, (h + 1) * 64)
                        ARc = AR[dr][pb][hs, lc, :]
                        BKc = BKb[dr][pb][hs, lc, :]
                        ARk, BKk = f"AR{dr}_{pb}", f"BKb{dr}_{pb}"
                        Qb = banks[2 + ch]
                        qk = f"B{2 + ch}"
                        sck = f"SC{ch}_{par}"
                        mm(banks[6][:, 0:256], BKc[:, 0:128], ARc, reads=[BKk, ARk], writes=["B6"])
                        mm(banks[6][:, 256:512], BKc[:, 128:256], ARc, reads=[BKk, ARk], writes=["B6"])
                        mm(Qb[:, 384:512], ARc[:, 0:128], BKc[:, 0:128], reads=[BKk, ARk], writes=[qk])
                        m4 = CT[:, C_M4F:C_M4F + 512] if dr == 0 else CT[:, C_M4B:C_M4B + 512]
                        mt = CT[:, C_MTF:C_MTF + 128] if dr == 0 else CT[:, C_MTB:C_MTB + 128]
                        tt("dve", SC[ch][par], banks[6][:, :], m4, ALU.mult, reads=["B6", "CT"], writes=[sck])
                        px = PXT[ch][0]
                        pk = f"PX{ch}_0"
                        tt("dve", px[:, 256:384], Qb[:, 384:512], mt, ALU.mult, reads=[qk, "CT"], writes=[pk])
                st.append(scores)

                def level(lvl):
                    def f():
                        for (dr, h, lc, chunk, ch) in chains:
                            Qb = banks[2 + ch]
                            qk = f"B{2 + ch}"
                            src = PXT[ch][lvl % 2]
                            dst = PXT[ch][(lvl + 1) % 2]
                            sk = f"PX{ch}_{lvl % 2}"
                            dk = f"PX{ch}_{(lvl + 1) % 2}"
                            if lvl == 0:
                                p0 = SC[ch][par][:, 0:128]
                                sck = f"SC{ch}_{par}"
                                mm(Qb[:, 256:384], p0, src[:, 256:384], reads=[sk, sck], writes=[qk])
                                mm(Qb[:, 0:128], src[:, 256:384], p0, reads=[sk, sck], writes=[qk])
                                mm(Qb[:, 128:256], identb, p0, start=True, stop=False, reads=[sck, "identb"], writes=[qk])
                                mm(Qb[:, 128:256], identb, identb, start=False, stop=True, reads=["identb"], writes=[qk])
                                if ch % 2 == 0 or (ch == 3 and lvl % 2 == 1):
                                    cp_act(dst, Qb[:, 0:384], reads=[qk], writes=[dk])
                                else:
                                    cp_dve(dst, Qb[:, 0:384], reads=[qk], writes=[dk])
                                continue
                            if lvl < 6:
                                mm(Qb[:, 256:384], src[:, 0:128], src[:, 256:384], reads=[sk], writes=[qk])
                                mm(Qb[:, 0:256], src[:, 256:384], src[:, 0:256], start=True, stop=False, reads=[sk], writes=[qk])
                            else:
                                mm(Qb[:, 128:256], src[:, 256:384], src[:, 128:256], start=True, stop=False, reads=[sk], writes=[qk])
                            mm(Qb[:, 128:256], identb, src[:, 128:256], start=False, stop=True, reads=[sk, "identb"], writes=[qk])
                            if lvl < 6:
                                if ch % 2 == 0 or (ch == 3 and lvl % 2 == 1):
                                    cp_act(dst, Qb[:, 0:384], reads=[qk], writes=[dk])
                                else:
                                    cp_dve(dst, Qb[:, 0:384], reads=[qk], writes=[dk])
                            else:
                                if ch % 2 == 0:
                                    cp_act(TTt[ch][par], Qb[:, 128:256], reads=[qk], writes=[f"TT{ch}_{par}"])
                                else:
                                    cp_dve(TTt[ch][par], Qb[:, 128:256], reads=[qk], writes=[f"TT{ch}_{par}"])
                    return f
                for lvl in range(7):
                    st.append(level(lvl))
                return st

            def R_stages(it, cl):
                chains = chains_of(it, cl)
                pb = it % 2
                par = (it * 4 + cl) % 2
                st = []

                import os
                KM = "r0,r2"

                def r0():
                    if "r0" not in KM:
                        for (dr, h, lc, chunk, ch) in chains:
                            hs = slice(h * 64, (h + 1) * 64)
                            wci = chunk if dr == 1 else max(chunk - 1, 0)
                            ts("pool", Hb[dr][hs, :], Hst[dr][hs, :], WCt[hs, dr, wci:wci + 1], ALU.mult, reads=[f"H{dr}", "WC"], writes=[f"Hb{dr}_{h}"])
                        return
                    for dr in range(2):
                        chunk = [c_ for (d_, h_, l_, c_, ch_) in chains if d_ == dr][0]
                        wci = chunk if dr == 1 else max(chunk - 1, 0)
                        act(Hb[dr], Hst[dr], AF.Copy, scale=WCt[:, dr, wci:wci + 1], reads=[f"H{dr}", "WC"], writes=[f"Hb{dr}_0", f"Hb{dr}_1"])

                def r1():
                    for (dr, h, lc, chunk, ch) in chains:
                        hs = slice(h * 64, (h + 1) * 64)
                        vsl = VT[:, chunk, h * 64:(h + 1) * 64]
                        za = banks[7][:, ch * 128:ch * 128 + 64]
                        mm(za, AR[dr][pb][hs, lc, 0:128], Hb[dr][hs, :], start=True, stop=False, reads=[f"AR{dr}_{pb}", f"Hb{dr}_{h}"], writes=["B7"])
                        mm(za, SC[ch][par][:, 256:384], vsl, start=False, stop=True, reads=[f"SC{ch}_{par}", "VT"], writes=["B7"])

                def r2():
                    if "r2" not in KM:
                        for (dr, h, lc, chunk, ch) in chains:
                            cp_act(Zs[ch], banks[7][:, ch * 128:ch * 128 + 64], reads=["B7"], writes=[f"Zs{ch}"])
                        return
                    cp_act(Zs_all.rearrange("p (c n) -> p c n", c=4), banks[7][:, :].rearrange("p (c n) -> p c n", c=4)[:, :, 0:64],
                           reads=["B7"], writes=[f"Zs{c_}" for c_ in range(4)])

                def r3():
                    for (dr, h, lc, chunk, ch) in chains:
                        mm(banks[7][:, ch * 128 + 64:ch * 128 + 128], TTt[ch][par], Zs[ch], reads=[f"TT{ch}_{par}", f"Zs{ch}"], writes=["B7"])

                def r4():
                    if "r2" not in KM:
                        for (dr, h, lc, chunk, ch) in chains:
                            cp_act(Us[ch], banks[7][:, ch * 128 + 64:ch * 128 + 128], reads=["B7"], writes=[f"Us{ch}"])
                        return
                    cp_act(Us_all.rearrange("p (c n) -> p c n", c=4), banks[7][:, :].rearrange("p (c n) -> p c n", c=4)[:, :, 64:128],
                           reads=["B7"], writes=[f"Us{c_}" for c_ in range(4)])

                def r5():
                    for (dr, h, lc, chunk, ch) in chains:
                        hs = slice(h * 64, (h + 1) * 64)
                        vsl = VT[:, chunk, h * 64:(h + 1) * 64]
                        ya = banks[7][:, ch * 128:ch * 128 + 64]
                        hb = banks[7][:, ch * 128 + 64:ch * 128 + 128]
                        mm(ya, AR[dr][pb][hs, lc, 128:256], Hb[dr][hs, :], start=True, stop=False, reads=[f"AR{dr}_{pb}", f"Hb{dr}_{h}"], writes=["B7"])
                        mm(ya, SC[ch][par][:, 128:256], Us[ch], start=False, stop=False, reads=[f"SC{ch}_{par}", f"Us{ch}"], writes=["B7"])
                        mm(ya, SC[ch][par][:, 384:512], vsl, start=False, stop=True, reads=[f"SC{ch}_{par}", "VT"], writes=["B7"])
                        mm(hb, BKT[dr][pb][:, lc, 0:128], Us[ch], start=True, stop=False, reads=[f"BKT{dr}_{pb}", f"Us{ch}"], writes=["B7"])
                        mm(hb, BKT[dr][pb][:, lc, 128:256], vsl, start=False, stop=True, reads=[f"BKT{dr}_{pb}", "VT"], writes=["B7"])

                def r6():
                    if "r6" not in KM:
                        for (dr, h, lc, chunk, ch) in chains:
                            ya = banks[7][:, ch * 128:ch * 128 + 64]
                            yk = f"YF{chunk}_{h}"
                            if it < 2:
                                cp_act(YF[:, chunk, h * 64:(h + 1) * 64], ya, reads=["B7", "YF"], writes=[yk])
                            else:
                                tt("dve", YF[:, chunk, h * 64:(h + 1) * 64], YF[:, chunk, h * 64:(h + 1) * 64], ya, ALU.add, reads=["B7", yk], writes=[yk])
                    for dr in (range(2) if "r6" in KM else []):
                        chunk = [c_ for (d_, h_, l_, c_, ch_) in chains if d_ == dr][0]
                        src = banks[7][:, dr * 256:(dr + 1) * 256].rearrange("p (c n) -> p c n", c=2)[:, :, 0:64]
                        dst = YF[:, chunk, :].rearrange("p (c n) -> p c n", c=2)
                        yk2 = [f"YF{chunk}_0", f"YF{chunk}_1"]
                        if it < 2:
                            cp_act(dst, src, reads=["B7", "YF"], writes=yk2)
                        else:
                            tt("dve", dst, dst, src, ALU.add, reads=["B7"] + yk2, writes=yk2)
                    for (dr, h, lc, chunk, ch) in chains:
                        hs = slice(h * 64, (h + 1) * 64)
                        hb = banks[7][hs, ch * 128 + 64:ch * 128 + 128]
                        wci = chunk if dr == 1 else max(chunk - 1, 0)
                        stt("dve", Hst[dr][hs, :], Hst[dr][hs, :], WCt[hs, dr, wci:wci + 1], hb, ALU.mult, ALU.add,
                            reads=[f"H{dr}", "WC", "B7", f"Hb{dr}_{h}"], writes=[f"H{dr}"])
                return [r0, r1, r2, r3, r4, r5, r6]

            for g_ in (precompute(0, 0), precompute(0, 1)):
                for _ in g_:
                    pass
            for f in S_stages(0, 0):
                f()
            for it in range(4):
                gens = [precompute(it + 1, 0), precompute(it + 1, 1)] if it < 3 else []
                for cl in range(4):
                    nxt = (it, cl + 1) if cl < 3 else ((it + 1, 0) if it < 3 else None)
                    if cl == 3:
                        for g_ in gens:
                            for _ in g_:
                                pass
                        gens = []
                    if it == 2 and cl == 0 and hp + 1 < n_hp:
                        nextW.append(prefetch_W(R_OFF + 0 * RW + (hp + 1) * 128, 128))
                        nextW.append(prefetch_W(R_OFF + 1 * RW + (hp + 1) * 128, 128))
                    rs = R_stages(it, cl)
                    ss = S_stages(*nxt) if nxt is not None else []
                    for k in range(max(len(rs), len(ss))):
                        if k < len(ss):
                            ss[k]()
                        if k < len(rs):
                            rs[k]()
                        if gens:
                            if next(gens[0], "done") == "done":
                                gens.pop(0)
            cut("c_rec")
            yks = [f"YF{c}_{h}" for c in range(16) for h in range(2)]
            Y3 = YFflat.rearrange("p (g n) -> p g n", g=32)
            gtmp = rawv[:, 1:S + 1]
            G3 = gtmp.rearrange("p (g n) -> p g n", g=32)
            V3 = VT.rearrange("p c (h n) -> p (c h) n", h=2)
            bc = lambda a: a.unsqueeze(2).to_broadcast([128, 32, 64])
            if hp == 0:
                tap("yscan", YFflat, yks, [128, S])
            tt("pool", s_ss, Sd[0].rearrange("p c h -> p (c h)"), Sd[1].rearrange("p c h -> p (c h)"), ALU.add, reads=["Sd0", "Sd1"], writes=["s_ss"])
            P.op("dve", lambda e: e.tensor_reduce(out=s_sum, in_=Y3, axis=AX.X, op=ALU.add), reads=yks, writes=["s_sum"])
            ts("dve", s_nm, s_sum, -1.0 / 64, ALU.mult, reads=["s_sum"], writes=["s_nm"])
            tt("dve", Y3, Y3, bc(s_nm), ALU.add, reads=yks + ["s_nm"], writes=["YF"])
            act(G3, Y3, AF.Square, reads=["YF", "rawv", "VT"], writes=["rawv"])
            P.op("dve", lambda e: e.tensor_reduce(out=s_var, in_=G3, axis=AX.X, op=ALU.add), reads=["rawv"], writes=["s_var"])
            ts("dve", s_var, s_var, 1.0 / 64, ALU.mult, GN_EPS, ALU.add, reads=["s_var"], writes=["s_var"])
            act(s_var, s_var, AF.Sqrt, reads=["s_var"], writes=["s_var"])
            P.op("dve", lambda e: e.reciprocal(s_var, s_var), reads=["s_var"], writes=["s_var"])
            tt("pool", G3, V3, bc(s_ss), ALU.mult, reads=["VT", "s_ss", "rawv"], writes=["rawv"])
            tt("dve", Y3, Y3, bc(s_var), ALU.mult, reads=["YF", "s_var"], writes=["YF"])
            lg = lnxb[:, 0, :].unsqueeze(1).to_broadcast([128, 16, 128])
            lb = lnxb[:, 1, :].unsqueeze(1).to_broadcast([128, 16, 128])
            tt("dve", YF, YF, lg, ALU.mult, reads=["YF", "lnxb"], writes=["YF"])
            tt("pool", YF, YF, lb, ALU.add, reads=["YF", "lnxb"], writes=["YF"])
            tt("dve", YFflat, YFflat, gtmp, ALU.add, reads=["YF", "rawv"], writes=["YF"])
            if hp == 0:
                tap("ypre", YFflat, ["YF"], [128, S])
            for tb in range(4):
                bsl = slice(tb * 512, (tb + 1) * 512)
                bi = pbank[0] % 2
                pbank[0] += 1
                bk = f"B{bi}"
                mm(banks[bi][:, :], lw_hp[:, 2, :], sg0[:, bsl], start=True, stop=False, reads=["lw2", "sg0"], writes=[bk])
                mm(banks[bi][:, :], lw_hp[0:96, 3, :], sg1[0:96, bsl], start=False, stop=True, reads=["lw3", "sg1"], writes=[bk])
                cp_act(gS, banks[bi][:, :], reads=[bk], writes=["T0"])
                bi = pbank[0] % 2
                pbank[0] += 1
                bk = f"B{bi}"
                for j in range(4):
                    c = tb * 4 + j
                    tr(banks[bi][:, j * 128:(j + 1) * 128], YF[:, c, :], ident, reads=["YF", "CT"], writes=[bk])
                tt("dve", ych[:, bsl], banks[bi][:, :], gS, ALU.mult, reads=[bk, "T0"], writes=["rawk"])
            P.dma("sp", ycat_d[4 + hp], ych, reads=["rawk"], writes=[f"ycat{4 + hp}"])
            if hp == 0:
                tap("ych0", ych, ["rawk"], [128, S], BF16)

        if stop_after is not None and str(stop_after).startswith("hp"):
            P.emit(finals)
            return nc, tap_out

        P.fence()
        A.release(mixer_mark)
        hB = A.alloc(4 * D).rearrange("p (t d) -> p t d", t=4)
        hTs = A.alloc(16 * 512 // 2, BF16).rearrange("p (k t) -> p k t", k=16)
        ffT = A.alloc(44 * 512 // 2, BF16).rearrange("p (f t) -> p f t", f=44)
        ycS = ffT[:, 0:16, :]
        lnp = A.alloc(4 * D).rearrange("p (w d) -> p w d", w=4)
        P.dma("sp", lnp, ln12_d.partition_broadcast(128), writes=["lnp"])
        xres = [A.alloc(512) for _ in range(2)]
        wpc = [A.alloc(4 * 512 // 2, BF16).rearrange("p (j d) -> p j d", j=4) for _ in range(3)]
        gpc = [A.alloc(16 * 256 // 2, BF16).rearrange("p (k c) -> p k c", k=16) for _ in range(2)]
        upc = [A.alloc(16 * 256 // 2, BF16).rearrange("p (k c) -> p k c", k=16) for _ in range(2)]
        sgt = [A.alloc(512) for _ in range(2)]
        junk = A.alloc(D)
        st_ = A.alloc(8)
        wctr2 = [0]
        xctr = [0]
        gctr = [0]

        def layer_norm(tile2d, key, gi):
            P.op("dve", lambda e: e.tensor_reduce(out=st_[:, 0:1], in_=tile2d, axis=AX.X, op=ALU.add), reads=[key], writes=["st0"])
            act(junk, tile2d, AF.Square, reads=[key], writes=["junk"])
            P.op("dve", lambda e: e.tensor_reduce(out=st_[:, 2:3], in_=junk, axis=AX.X, op=ALU.add), reads=["junk"], writes=["st2"])
            ts("dve", st_[:, 1:2], st_[:, 0:1], 1.0 / D, ALU.mult, reads=["st0"], writes=["st1"])
            tt("dve", st_[:, 5:6], st_[:, 1:2], st_[:, 1:2], ALU.mult, reads=["st1"], writes=["st5"])
            stt("dve", st_[:, 3:4], st_[:, 2:3], 1.0 / D, st_[:, 5:6], ALU.mult, ALU.subtract, reads=["st2", "st5"], writes=["st3"])
            ts("dve", st_[:, 3:4], st_[:, 3:4], LN_EPS, ALU.add, reads=["st3"], writes=["st3"])
            act(st_[:, 4:5], st_[:, 3:4], AF.Sqrt, reads=["st3"], writes=["st4"])
            P.op("dve", lambda e: e.reciprocal(st_[:, 4:5], st_[:, 4:5]), reads=["st4"], writes=["st4"])
            stt("dve", st_[:, 6:7], st_[:, 1:2], -1.0, st_[:, 4:5], ALU.mult, ALU.mult, reads=["st1", "st4"], writes=["st6"])
            act(tile2d, tile2d, AF.Identity, bias=st_[:, 6:7], scale=st_[:, 4:5], reads=[key, "st4", "st6"], writes=[key])
            tt("pool", tile2d, tile2d, lnp[:, gi, :], ALU.mult, reads=[key, "lnp"], writes=[key])
            tt("dve", tile2d, tile2d, lnp[:, gi + 1, :], ALU.add, reads=[key, "lnp"], writes=[key])

        n_tb = int(stop_after[2:]) if str(stop_after).startswith("tb") else 4
        for tb in range(n_tb):
            t0 = tb * 512
            for cc in range(16):
                P.dma("sp", ycS[:, cc, :], ycat_d[cc, :, t0:t0 + 512], reads=[f"ycat{cc}"], writes=[f"ffT{cc}"])
            for db in range(4):
                for ccg in range(4):
                    wi = wctr2[0] % 3
                    wctr2[0] += 1
                    P.dma("pool", wpc[wi], wout_d[ccg * 512:(ccg + 1) * 512, db * 512:(db + 1) * 512].rearrange("(j p) d -> p j d", p=128),
                          writes=[f"wpc{wi}"])
                    for j in range(4):
                        cc = ccg * 4 + j
                        for t4 in range(4):
                            mm(banks[t4][:, :], ycS[:, cc, t4 * 128:(t4 + 1) * 128], wpc[wi][:, j, :], start=(cc == 0), stop=(cc == 15),
                               reads=[f"ffT{cc}", f"wpc{wi}"], writes=[f"B{t4}"])
                for t4 in range(4):
                    xi = xctr[0] % 2
                    xctr[0] += 1
                    P.dma("sp", xres[xi], x_d[t0 + t4 * 128:t0 + (t4 + 1) * 128, db * 512:(db + 1) * 512], writes=[f"xres{xi}"])
                    stt("dve", hB[:, t4, db * 512:(db + 1) * 512], xres[xi], ALPHA, banks[t4][:, :], ALU.mult, ALU.add,
                        reads=[f"xres{xi}", f"B{t4}"], writes=[f"hB{t4}"])
            for t4 in range(4):
                layer_norm(hB[:, t4, :], f"hB{t4}", 0)
            if tb == 0:
                tap("h0", hB[:, 0, :], ["hB0"], [128, D])
            for kc in range(16):
                bi = 4 + kc % 2
                bk = f"B{bi}"
                for t4 in range(4):
                    tr(banks[bi][:, t4 * 128:(t4 + 1) * 128], hB[:, t4, kc * 128:(kc + 1) * 128], ident, reads=[f"hB{t4}", "CT"], writes=[bk])
                if kc % 2 == 0:
                    cp_act(hTs[:, kc, :], banks[bi][:, :], reads=[bk], writes=["hTs"])
                else:
                    cp_dve(hTs[:, kc, :], banks[bi][:, :], reads=[bk], writes=["hTs"])
            for fp in range(22):
                gi = gctr[0] % 2
                gctr[0] += 1
                P.dma("pool", gpc[gi], wg_d[:, fp * 256:(fp + 1) * 256].rearrange("(k p) c -> p k c", p=128), writes=[f"gpc{gi}"])
                P.dma("pool", upc[gi], wu_d[:, fp * 256:(fp + 1) * 256].rearrange("(k p) c -> p k c", p=128), writes=[f"upc{gi}"])
                for j in range(2):
                    fc = fp * 2 + j
                    for kc in range(16):
                        mm(banks[6][:, :], gpc[gi][:, kc, j * 128:(j + 1) * 128], hTs[:, kc, :], start=(kc == 0), stop=(kc == 15),
                           reads=[f"gpc{gi}", "hTs"], writes=["B6"])
                    for kc in range(16):
                        mm(banks[7][:, :], upc[gi][:, kc, j * 128:(j + 1) * 128], hTs[:, kc, :], start=(kc == 0), stop=(kc == 15),
                           reads=[f"upc{gi}", "hTs"], writes=["B7"])
                    si = fc % 2
                    act(sgt[si], banks[6][:, :], AF.Silu, reads=["B6"], writes=[f"sgt{si}"])
                    tt("dve", ffT[:, fc, :], sgt[si], banks[7][:, :], ALU.mult, reads=[f"sgt{si}", "B7"], writes=[f"ffT{fc}"])
            for db in range(4):
                for fg in range(11):
                    wi = wctr2[0] % 3
                    wctr2[0] += 1
                    P.dma("pool", wpc[wi], wd_d[fg * 512:(fg + 1) * 512, db * 512:(db + 1) * 512].rearrange("(j p) d -> p j d", p=128),
                          writes=[f"wpc{wi}"])
                    for j in range(4):
                        fc = fg * 4 + j
                        for t4 in range(4):
                            mm(banks[t4][:, :], ffT[:, fc, t4 * 128:(t4 + 1) * 128], wpc[wi][:, j, :], start=(fc == 0), stop=(fc == 43),
                               reads=[f"ffT{fc}", f"wpc{wi}"], writes=[f"B{t4}"])
                for t4 in range(4):
                    stt("dve", hB[:, t4, db * 512:(db + 1) * 512], hB[:, t4, db * 512:(db + 1) * 512], ALPHA, banks[t4][:, :], ALU.mult, ALU.add,
                        reads=[f"hB{t4}", f"B{t4}"], writes=[f"hB{t4}"])
            for t4 in range(4):
                layer_norm(hB[:, t4, :], f"hB{t4}", 2)
                finals.append(P.dma("sp", out_d[t0 + t4 * 128:t0 + (t4 + 1) * 128, :], hB[:, t4, :], reads=[f"hB{t4}"]))
        P.emit(finals)
        build_program.peak = A.peak
    return nc, tap_out


def _host_inputs(inp):
    f = lambda a: np.ascontiguousarray(np.asarray(a, dtype=np.float32))
    cp = np.zeros((128, NCP), np.float32)
    mu = f(inp["mu_shift"])[0]
    o = lambda off: off - R_OFF
    for hp in range(NHP):
        sl = slice(hp * 128, (hp + 1) * 128)
        cb = hp * 10
        cp[:, cb + 0] = mu[o(R_OFF) + hp * 128: o(R_OFF) + (hp + 1) * 128]
        cp[:, cb + 1] = mu[o(K_OFF) + hp * 128: o(K_OFF) + (hp + 1) * 128]
        cp[:, cb + 2] = mu[o(V_OFF) + hp * 128: o(V_OFF) + (hp + 1) * 128]
        cp[:, cb + 3] = f(inp["w0_f"])[0][sl]
        cp[:, cb + 4] = f(inp["w0_b"])[0][sl]
        cp[:, cb + 5] = f(inp["a0_f"])[0][sl]
        cp[:, cb + 6] = f(inp["a0_b"])[0][sl]
        cp[:, cb + 7] = f(inp["k_k"])[0][sl]
        cp[:, cb + 8] = f(inp["k_a"])[0][sl]
        cp[:, cb + 9] = f(inp["r_k"])[0].reshape(-1)[sl]
    cp[:, 120] = mu[o(G_OFF):o(G_OFF) + 128]
    cp[:96, 121] = mu[o(G_OFF) + 128:o(G_OFF) + 224]
    cp[:, 122] = mu[o(WF_OFF):o(WF_OFF) + 128]
    cp[:, 123] = mu[o(AF_OFF):o(AF_OFF) + 128]
    cs, ss, dn = _host_dft()
    shared = {
        "w_in": f(inp["w_in"])[0], "w_out": f(inp["w_out"])[0], "w_gate": f(inp["w_ffn_gate"])[0],
        "w_up": f(inp["w_ffn_up"])[0], "w_down": f(inp["w_ffn_down"])[0],
        "wupfb": np.ascontiguousarray(np.concatenate([f(inp["w_up_f"])[0], f(inp["w_up_b"])[0]], 0)),
        "aupfb": np.ascontiguousarray(np.concatenate([f(inp["a_up_f"])[0], f(inp["a_up_b"])[0]], 0)),
        "g_up": f(inp["g_up"])[0], "cp": cp,
        "lnx": np.ascontiguousarray(np.stack([f(inp["lnx_g"])[0], f(inp["lnx_b"])[0]], 0)),
        "ln12": np.ascontiguousarray(np.stack([f(inp["ln1_g"])[0], f(inp["ln1_b"])[0], f(inp["ln2_g"])[0], f(inp["ln2_b"])[0]], 0)),
        "consts": _host_consts(), "dftC": cs, "dftS": ss, "dftN": dn,
    }
    return shared


def kernel(**inputs):
    x = np.asarray(inputs["x"], dtype=np.float32)
    shared = _host_inputs(inputs)
    nc, _ = build_program()
    in_maps = []
    for b in range(8):
        m = dict(shared)
        m["x"] = np.ascontiguousarray(x[b])
        m["xT"] = np.ascontiguousarray(x[b].T)
        in_maps.append(m)
    res = run_bass_kernel_spmd(nc, in_maps, core_ids=list(range(8)))
    return np.stack([np.asarray(r["out"], dtype=np.float32) for r in res.results], 0)
```

```python
import contextlib
import numpy as np
import ml_dtypes
import concourse.bass as bass
import concourse.mybir as mybir
from concourse.bass_utils import run_bass_kernel_spmd

F32 = mybir.dt.float32
BF16 = mybir.dt.bfloat16
AF = mybir.ActivationFunctionType
ALU = mybir.AluOpType
AX = mybir.AxisListType

D = 2048
S = 2048
FW = 512
RW = 1536
NH = 24
N = 64
R_OFF = 512
K_OFF = R_OFF + RW
V_OFF = K_OFF + RW
G_OFF = V_OFF + RW
WF_OFF = G_OFF + 224
AF_OFF = WF_OFF + 128
IN_COLS = 5600
DFF = 5632
ALPHA = 2.0 ** 0.25
LN_EPS = 1e-5
GN_EPS = 64e-5
CDEC = float(np.exp(-0.5))
NHP = 12
NCP = 124

ENGS = ("pe", "act", "dve", "pool", "sp")
EPOCH = 12000
NDSEM = 16


class Op:
    __slots__ = ("eng", "idx", "fn", "deps", "is_dma", "needs_inc", "semval", "dma_slot", "dma_val", "seq", "vc")

    def __init__(self, eng, idx, fn, is_dma):
        self.eng = eng
        self.idx = idx
        self.fn = fn
        self.deps = []
        self.is_dma = is_dma
        self.needs_inc = False
        self.semval = None
        self.dma_slot = None
        self.dma_val = None


class Prog:
    def __init__(self, nc):
        self.nc = nc
        self.ops = {e: [] for e in ENGS}
        self.last_w = {}
        self.readers = {}
        self.ndma = {e: 0 for e in ENGS}
        self.dma_ring = {e: [None] * NDSEM for e in ENGS}
        self.fence_ops = []
        self.nseq = 0

    def _add(self, eng, fn, reads, writes, is_dma):
        op = Op(eng, len(self.ops[eng]), fn, is_dma)
        op.seq = self.nseq
        self.nseq += 1
        deps = list(self.fence_ops)
        xr = [k for k in reads if len(k) == 2 and k[0] == "B"]
        if xr:
            reads = [k for k in reads if k not in xr]
            writes = list(writes) + xr
        for k in reads:
            w = self.last_w.get(k)
            if w is not None:
                deps.append(w)
        for k in writes:
            w = self.last_w.get(k)
            if w is not None:
                deps.append(w)
            deps.extend(self.readers.get(k, ()))
        if is_dma:
            slot = self.ndma[eng] % NDSEM
            prev = self.dma_ring[eng][slot]
            if prev is not None:
                deps.append(prev)
            op.dma_slot = slot
            op.dma_val = 16 * (self.ndma[eng] // NDSEM + 1)
            self.dma_ring[eng][slot] = op
            self.ndma[eng] += 1
        op.deps = deps
        for k in reads:
            self.readers.setdefault(k, []).append(op)
        for k in writes:
            self.last_w[k] = op
            self.readers[k] = []
        self.ops[eng].append(op)
        return op

    def op(self, eng, fn, reads=(), writes=()):
        return self._add(eng, fn, reads, writes, False)

    def dma(self, eng, out, in_, reads=(), writes=()):
        def fn(e):
            return e.dma_start(out=out, in_=in_)
        return self._add(eng, fn, reads, writes, True)

    def fence(self):
        f = []
        for e in ENGS:
            if self.ops[e]:
                last_c = None
                for o in reversed(self.ops[e]):
                    if not o.is_dma:
                        last_c = o
                        break
                if last_c is not None:
                    f.append(last_c)
            for o in self.dma_ring[e]:
                if o is not None:
                    f.append(o)
        self.fence_ops = f
        self.last_w = {}
        self.readers = {}

    def emit(self, final_wait_ops=()):
        nc = self.nc
        all_ops = sorted((op for e in ENGS for op in self.ops[e]), key=lambda o: o.seq)
        know = {e: {} for e in ENGS}
        for op in all_ops:
            E = op.eng
            kn = know[E]
            nd = []
            seen = set()
            for d in op.deps:
                if d is op or id(d) in seen:
                    continue
                seen.add(id(d))
                if d.is_dma:
                    nd.append(d)
                    for k_, v_ in d.vc.items():
                        if kn.get(k_, -1) < v_:
                            kn[k_] = v_
                    continue
                if d.eng == E:
                    if not op.is_dma:
                        if E in ("pe", "sp"):
                            continue
                        if d.idx < op.idx - 2:
                            continue
                    d.needs_inc = True
                    nd.append(d)
                    continue
                if kn.get(d.eng, -1) >= d.idx:
                    continue
                d.needs_inc = True
                nd.append(d)
                for k_, v_ in d.vc.items():
                    if kn.get(k_, -1) < v_:
                        kn[k_] = v_
            op.deps = nd
            op.vc = dict(kn)
            if not op.is_dma:
                op.vc[E] = op.idx
        for o in final_wait_ops:
            if not o.is_dma:
                o.needs_inc = True
        nep = {}
        for e in ENGS:
            c = 0
            for op in self.ops[e]:
                if not op.is_dma and op.needs_inc:
                    op.semval = c
                    c += 1
            nep[e] = max(1, -(-c // EPOCH))
        with contextlib.ExitStack() as st:
            csem = {e: [st.enter_context(nc.semaphore(f"c_{e}_{i}")) for i in range(nep[e])] for e in ENGS}
            dsem = {e: [st.enter_context(nc.semaphore(f"d_{e}_{i}")) for i in range(NDSEM)]
                    for e in ENGS if self.ndma[e] > 0}
            block = st.enter_context(nc.Block())

            def run(e, eng):
                waited = {}
                for op in self.ops[e]:
                    for d in op.deps:
                        if d.is_dma:
                            key = ("d", d.eng, d.dma_slot)
                            val = d.dma_val
                            sem = dsem[d.eng][d.dma_slot]
                        else:
                            ep, v = divmod(d.semval, EPOCH)
                            key = ("c", d.eng, ep)
                            val = v + 1
                            sem = csem[d.eng][ep]
                        if waited.get(key, 0) >= val:
                            continue
                        waited[key] = val
                        eng.wait_ge(sem, val)
                    inst = op.fn(eng)
                    if op.is_dma:
                        inst.then_inc(dsem[e][op.dma_slot], 16)
                    elif op.needs_inc:
                        ep, v = divmod(op.semval, EPOCH)
                        inst.then_inc(csem[e][ep], 1)
                if e == "sp":
                    for o in final_wait_ops:
                        if o.is_dma:
                            eng.wait_ge(dsem[o.eng][o.dma_slot], o.dma_val)
                        else:
                            ep, v = divmod(o.semval, EPOCH)
                            eng.wait_ge(csem[o.eng][ep], v + 1)

            @block.tensor
            def _(eng):
                run("pe", eng)

            @block.scalar
            def _(eng):
                run("act", eng)

            @block.vector
            def _(eng):
                run("dve", eng)

            @block.gpsimd
            def _(eng):
                run("pool", eng)

            @block.sync
            def _(eng):
                run("sp", eng)


class Arena:
    def __init__(self, ap, ncols):
        self.ap = ap
        self.n = ncols
        self.top = 0
        self.peak = 0

    def alloc(self, cols, dtype=F32):
        a = self.ap[:, self.top:self.top + cols]
        self.top += cols
        self.peak = max(self.peak, self.top)
        assert self.top <= self.n, f"SBUF arena overflow {self.top} > {self.n}"
        if dtype == BF16:
            a = a.bitcast(BF16)
        return a

    def mark(self):
        return self.top

    def release(self, m):
        self.top = m


C_M4F = 0
C_M4B = 512
C_MTF = 1024
C_MTB = 1152
C_ID = 1280
C_BO = 1408
C_BI = 1536
NCONST = 1538


def _host_consts():
    idx = np.arange(128)
    sf = (idx[:, None] < idx[None, :]).astype(np.float32)
    inf_ = (idx[:, None] <= idx[None, :]).astype(np.float32)
    sb = (idx[:, None] > idx[None, :]).astype(np.float32)
    inb = (idx[:, None] >= idx[None, :]).astype(np.float32)
    c = np.zeros((128, NCONST), np.float32)
    c[:, C_M4F:C_M4F + 512] = np.concatenate([sf, inf_, sf, inf_], 1)
    c[:, C_M4B:C_M4B + 512] = np.concatenate([sb, inb, sb, inb], 1)
    c[:, C_MTF:C_MTF + 128] = sf.T
    c[:, C_MTB:C_MTB + 128] = sb.T
    c[:, C_ID:C_ID + 128] = np.eye(128, dtype=np.float32)
    bo = np.zeros((128, 128), np.float32)
    bo[:64, :64] = 1
    bo[64:, 64:] = 1
    c[:, C_BO:C_BO + 128] = bo
    c[:64, C_BI] = 1
    c[64:, C_BI + 1] = 1
    return c


def _host_dft():
    t = np.arange(S, dtype=np.int64)
    m = (t[:, None] * t[None, :]) % S
    ang = 2.0 * np.pi * m.astype(np.float64) / S
    cs = np.cos(ang).astype(ml_dtypes.bfloat16)
    ss = np.sin(ang).astype(ml_dtypes.bfloat16)
    n = np.arange(128, dtype=np.int64)
    mn = (n[:, None] * n[None, :]) % 128
    an = 2.0 * np.pi * mn.astype(np.float64) / 128
    norm = 1.0 / np.sqrt(S * 128.0)
    dn = np.concatenate([np.cos(an) * norm, -np.sin(an) * norm], 1).astype(ml_dtypes.bfloat16)
    return cs, ss, dn


class _Stop(Exception):
    pass


def build_program(taps=None, stop_after=None):
    try:
        return _build_program(taps, stop_after)
    except _Stop as e:
        return e.args[0]


def _build_program(taps=None, stop_after=None):
    taps = taps or {}
    nc = bass.Bass("TRN2", target_bir_lowering=False)
    din = lambda n, s, d=F32: nc.dram_tensor(n, list(s), d, kind="ExternalInput").ap()
    xT_d = din("xT", [D, S])
    x_d = din("x", [S, D])
    win_d = din("w_in", [D, IN_COLS])
    wout_d = din("w_out", [D, D])
    wg_d = din("w_gate", [D, DFF])
    wu_d = din("w_up", [D, DFF])
    wd_d = din("w_down", [DFF, D])
    wupfb_d = din("wupfb", [128, RW])
    aupfb_d = din("aupfb", [128, RW])
    gup_d = din("g_up", [224, RW])
    cp_d = din("cp", [128, NCP])
    lnx_d = din("lnx", [2, RW])
    ln12_d = din("ln12", [4, D])
    consts_d = din("consts", [128, NCONST])
    dftc_d = din("dftC", [S, S], BF16)
    dfts_d = din("dftS", [S, S], BF16)
    dftn_d = din("dftN", [128, 256], BF16)
    out_d = nc.dram_tensor("out", [S, D], F32, kind="ExternalOutput").ap()
    ycat_d = nc.dram_tensor("ycat_scr", [16, 128, S], BF16).ap()
    tap_out = {}

    with contextlib.ExitStack() as top:
        ARENA_COLS = 52600
        arena_t = top.enter_context(nc.sbuf_tensor("arena", [128, ARENA_COLS], F32))
        A = Arena(arena_t, ARENA_COLS)
        banks = [top.enter_context(nc.psum_tensor(f"bank{i}", [128, 512], F32)) for i in range(8)]
        P = Prog(nc)
        finals = []

        def tap(name, ap, keys, shape, dtype=F32):
            if name not in taps:
                return
            t = nc.dram_tensor("tap_" + name, list(shape), dtype, kind="ExternalOutput").ap()
            tap_out[name] = t
            sel = taps[name]
            finals.append(P.dma("sp", sel(t) if callable(sel) else t, ap, reads=keys))

        def cut(name):
            if stop_after == name:
                for e_ in ("pe", "act", "dve", "pool"):
                    for o_ in reversed(P.ops[e_]):
                        if not o_.is_dma:
                            finals.append(o_)
                            break
                P.emit(finals)
                raise _Stop((nc, tap_out))

        def mm(out, lhsT, rhs, start=True, stop=True, reads=(), writes=()):
            return P.op("pe", lambda e: e.matmul(out, lhsT, rhs, start=start, stop=stop), reads, writes)

        def tr(out, in_, ident, reads=(), writes=()):
            return P.op("pe", lambda e: e.transpose(out, in_, ident), reads, writes)

        def act(out, in_, func, bias=None, scale=1.0, reads=(), writes=(), accum_out=None):
            kw = {}
            if bias is not None:
                kw["bias"] = bias
            if accum_out is not None:
                kw["accum_out"] = accum_out
            return P.op("act", lambda e: e.activation(out=out, in_=in_, func=func, scale=scale, **kw), reads, writes)

        def tt(eng, out, in0, in1, op, reads=(), writes=()):
            return P.op(eng, lambda e: e.tensor_tensor(out=out, in0=in0, in1=in1, op=op), reads, writes)

        def ts(eng, out, in0, s1, op0, s2=None, op1=None, reads=(), writes=()):
            if op1 is None:
                return P.op(eng, lambda e: e.tensor_scalar(out=out, in0=in0, scalar1=s1, scalar2=None, op0=op0), reads, writes)
            return P.op(eng, lambda e: e.tensor_scalar(out=out, in0=in0, scalar1=s1, scalar2=s2, op0=op0, op1=op1), reads, writes)

        def stt(eng, out, in0, scalar, in1, op0, op1, reads=(), writes=()):
            return P.op(eng, lambda e: e.scalar_tensor_tensor(out=out, in0=in0, scalar=scalar, in1=in1, op0=op0, op1=op1), reads, writes)

        def cp_act(out, in_, reads=(), writes=()):
            return P.op("act", lambda e: e.copy(out, in_), reads, writes)

        def cp_dve(out, in_, reads=(), writes=()):
            return P.op("dve", lambda e: e.tensor_copy(out, in_), reads, writes)

        CT = A.alloc(NCONST)
        CPt = A.alloc(NCP)
        OMt = A.alloc(NCP)
        HMt = A.alloc(NCP)
        ident = CT[:, C_ID:C_ID + 128]
        blockones = CT[:, C_BO:C_BO + 128]
        blockind = CT[:, C_BI:C_BI + 2]
        P.dma("sp", CT, consts_d, writes=["CT"])
        P.dma("sp", CPt, cp_d, writes=["CP"])
        ts("dve", OMt, CPt, -1.0, ALU.mult, 1.0, ALU.add, reads=["CP"], writes=["OM"])
        ts("dve", HMt, CPt, 0.5, ALU.mult, reads=["CP"], writes=["HM"])
        ones128 = A.alloc(128)
        P.op("pool", lambda e: e.memset(ones128, 1.0), writes=["ones"])
        tiny_col = A.alloc(1)
        P.op("pool", lambda e: e.memset(tiny_col, 1e-24), writes=["tiny"])

        mixer_mark = A.mark()
        xTs = A.alloc(16 * S // 2, BF16).rearrange("p (k t) -> p k t", k=16)
        for kc in range(16):
            P.dma("pool", xTs[:, kc, :], xT_d[kc * 128:(kc + 1) * 128, :], writes=[f"xT{kc}"])
        XK = [f"xT{kc}" for kc in range(16)]

        NW = 2
        Wb = [A.alloc(16 * 128 // 2, BF16).rearrange("p (k c) -> p k c", k=16) for _ in range(NW)]
        wctr = [0]
        pbank = [0]

        def prefetch_W(c0, ncol):
            wi = wctr[0] % NW
            wctr[0] += 1
            W = Wb[wi]
            wk = f"W{wi}"
            P.dma("pool", W[:, :, 0:ncol], win_d[:, c0:c0 + ncol].rearrange("(k p) c -> p k c", p=128), writes=[wk])
            return (W, wk)

        def proj_cols(c0, ncol, evac, pre=None):
            W, wk = pre if pre is not None else prefetch_W(c0, ncol)
            for tb in range(4):
                bi = pbank[0] % 2
                pbank[0] += 1
                bk = f"B{bi}"
                ps = banks[bi][0:ncol, :]
                for kc in range(16):
                    mm(ps, W[:, kc, 0:ncol], xTs[:, kc, tb * 512:(tb + 1) * 512], start=(kc == 0), stop=(kc == 15),
                       reads=[wk, f"xT{kc}"], writes=[bk])
                evac(tb, ps, bk)

        def token_shift(raw, rk, npart, col, tmp, tk):
            tt("dve", tmp[0:npart, :], raw[0:npart, 0:S], raw[0:npart, 2:S + 2], ALU.add, reads=[rk], writes=[tk])
            act(raw[0:npart, 1:S + 1], raw[0:npart, 1:S + 1], AF.Identity, scale=OMt[0:npart, col:col + 1], reads=[rk, "OM"], writes=[rk])
            stt("dve", raw[0:npart, 1:S + 1], tmp[0:npart, :], HMt[0:npart, col:col + 1], raw[0:npart, 1:S + 1],
                ALU.mult, ALU.add, reads=[rk, tk, "HM"], writes=[rk])

        sg0 = A.alloc(S // 2, BF16)
        sg1 = A.alloc(S // 2, BF16)
        twfb = A.alloc(S // 2, BF16)
        alfb = A.alloc(S // 2, BF16)

        a2_mark = A.mark()
        uT = A.alloc(S // 2, BF16)
        Acs = [A.alloc(16 * 256 // 2, BF16).rearrange("p (c n) -> p c n", c=16) for _ in range(4)]
        dftN = A.alloc(128, BF16)
        P.dma("sp", dftN, dftn_d, writes=["dftN"])
        yfo = [A.alloc(S // 2, BF16) for _ in range(4)]
        for g in range(4):
            def evac_u(tb, ps, bk):
                cp_act(uT[:, tb * 512:(tb + 1) * 512], ps, reads=[bk], writes=["uT"])
            proj_cols(g * 128, 128, evac_u)
            for tcp in range(8):
                bi = pbank[0] % 2
                pbank[0] += 1
                bk = f"B{bi}"
                for j in range(2):
                    tc = tcp * 2 + j
                    mm(banks[bi][:, j * 256:(j + 1) * 256], uT[:, tc * 128:(tc + 1) * 128], dftN,
                       reads=["uT", "dftN"], writes=[bk])
                cp_dve(Acs[g][:, tcp * 2:tcp * 2 + 2, :], banks[bi][:, :].rearrange("p (c n) -> p c n", c=2),
                       reads=[bk], writes=[f"Acs{g}"])
        tap("acs0", Acs[0], ["Acs0"], [128, 16, 256], BF16)
        dpc = [A.alloc(4 * 512 // 2, BF16).rearrange("p (j t) -> p j t", j=4) for _ in range(2)]
        dps = [A.alloc(4 * 512 // 2, BF16).rearrange("p (j t) -> p j t", j=4) for _ in range(2)]
        pc = 0
        for tpb in range(4):
            for q in range(4):
                bsel = pc % 2
                pc += 1
                P.dma("sp", dpc[bsel], dftc_d[q * 512:(q + 1) * 512, tpb * 512:(tpb + 1) * 512].rearrange("(j p) t -> p j t", p=128),
                      writes=[f"dpc{bsel}"])
                P.dma("sp", dps[bsel], dfts_d[q * 512:(q + 1) * 512, tpb * 512:(tpb + 1) * 512].rearrange("(j p) t -> p j t", p=128),
                      writes=[f"dps{bsel}"])
                for g in range(4):
                    for j in range(4):
                        tc = q * 4 + j
                        mm(banks[2 + g][:, :], Acs[g][:, tc, 0:128], dpc[bsel][:, j, :], start=(tc == 0), stop=False,
                           reads=[f"Acs{g}", f"dpc{bsel}"], writes=[f"B{2 + g}"])
                        mm(banks[2 + g][:, :], Acs[g][:, tc, 128:256], dps[bsel][:, j, :], start=False, stop=(tc == 15),
                           reads=[f"Acs{g}", f"dps{bsel}"], writes=[f"B{2 + g}"])
            for g in range(4):
                if g % 2 == 0:
                    cp_act(yfo[g][:, tpb * 512:(tpb + 1) * 512], banks[2 + g][:, :], reads=[f"B{2 + g}"], writes=[f"yfo{g}"])
                else:
                    cp_dve(yfo[g][:, tpb * 512:(tpb + 1) * 512], banks[2 + g][:, :], reads=[f"B{2 + g}"], writes=[f"yfo{g}"])
        for g in range(4):
            P.dma("sp", ycat_d[g], yfo[g], reads=[f"yfo{g}"], writes=[f"ycat{g}"])
        tap("yfo0", yfo[0], ["yfo0"], [128, S], BF16)

        lraw = A.alloc(S + 2)
        ltmp = A.alloc(S)
        P.op("pool", lambda e: e.memset(lraw[:, 0:1], 0.0), writes=["lraw"])
        P.op("pool", lambda e: e.memset(lraw[:, S + 1:S + 2], 0.0), writes=["lraw"])
        lora_specs = [(G_OFF, 128, sg0, "sg0", AF.Sigmoid, 120), (G_OFF + 128, 96, sg1, "sg1", AF.Sigmoid, 121),
                      (WF_OFF, 128, twfb, "twfb", AF.Tanh, 122), (AF_OFF, 128, alfb, "alfb", AF.Identity, 123)]
        for (c0, ncol, dst, dk, fn, col) in lora_specs:
            def evac_l(tb, ps, bk, ncol=ncol):
                cp_act(lraw[0:ncol, 1 + tb * 512:1 + (tb + 1) * 512], ps, reads=[bk], writes=["lraw"])
            proj_cols(c0, ncol, evac_l)
            token_shift(lraw, "lraw", ncol, col, ltmp, "ltmp")
            act(dst[0:ncol, :], lraw[0:ncol, 1:S + 1], fn, reads=["lraw"], writes=[dk])
        tap("twfb", twfb, ["twfb"], [128, S], BF16)
        tap("sg1", sg1, ["sg1"], [128, S], BF16)

        if stop_after == "A1":
            P.emit(finals)
            return nc, tap_out

        P.fence()
        A.release(a2_mark)
        rawr = A.alloc(S + 2)
        rawk = A.alloc(S + 2)
        rawv = A.alloc(S + 2)
        for rw_, k_ in ((rawr, "rawr"), (rawk, "rawk"), (rawv, "rawv")):
            P.op("pool", lambda e, rw_=rw_: e.memset(rw_[:, 0:1], 0.0), writes=[k_])
            P.op("pool", lambda e, rw_=rw_: e.memset(rw_[:, S + 1:S + 2], 0.0), writes=[k_])
        VT = A.alloc(S // 2, BF16).rearrange("p (c n) -> p c n", c=16)
        YF = A.alloc(S).rearrange("p (c n) -> p c n", c=16)
        YFflat = YF.rearrange("p c n -> p (c n)")
        Sd = [A.alloc(32).rearrange("p (c h) -> p c h", c=16) for _ in range(2)]
        WCt = A.alloc(32).rearrange("p (d c) -> p d c", d=2)
        lw_hp = A.alloc(4 * 64, BF16).rearrange("p (w c) -> p w c", w=4)
        lnxb = A.alloc(256).rearrange("p (w c) -> p w c", w=2)
        ych = rawk[:, 2:2 + S // 2].bitcast(BF16)
        AR = [[A.alloc(512, BF16).rearrange("p (c n) -> p c n", c=4) for _ in range(2)] for _ in range(2)]
        BKb = [[A.alloc(512, BF16).rearrange("p (c n) -> p c n", c=4) for _ in range(2)] for _ in range(2)]
        BK = [A.alloc(1024).rearrange("p (c n) -> p c n", c=4) for _ in range(2)]
        BKT = [[A.alloc(512, BF16).rearrange("p (c n) -> p c n", c=4) for _ in range(2)] for _ in range(2)]
        NT = 9
        Tm = [A.alloc(512) for _ in range(NT)]
        SC = [[A.alloc(256, BF16) for _ in range(2)] for _ in range(4)]
        TTt = [[A.alloc(64, BF16) for _ in range(2)] for _ in range(4)]
        PXT = [[A.alloc(192, BF16) for _ in range(2)] for _ in range(4)]
        identb = A.alloc(64, BF16)
        cp_act(identb, ident, reads=["CT"], writes=["identb"])
        Zs_all = A.alloc(128, BF16)
        Zs = [Zs_all[:, ch_ * 64:(ch_ + 1) * 64] for ch_ in range(4)]
        Us_all = A.alloc(128, BF16)
        Us = [Us_all[:, ch_ * 64:(ch_ + 1) * 64] for ch_ in range(4)]
        Hb = [A.alloc(32, BF16) for _ in range(2)]
        Hst = [A.alloc(64) for _ in range(2)]
        gS = Tm[0]
        small = A.alloc(32 * 4)
        s_sum, s_nm, s_var, s_ss = (small[:, i * 32:(i + 1) * 32] for i in range(4))

        n_hp = int(stop_after[2:]) if str(stop_after).startswith("hp") else NHP
        nextW = []
        for hp in range(n_hp):
            cb = hp * 10
            col = lambda j: CPt[:, cb + j:cb + j + 1]
            hpc = slice(hp * 128, (hp + 1) * 128)
            P.dma("pool", lw_hp[:, 0, :], wupfb_d[:, hpc], writes=["lw0"])
            P.dma("pool", lw_hp[:, 1, :], aupfb_d[:, hpc], writes=["lw1"])
            P.dma("pool", lw_hp[:, 2, :], gup_d[0:128, hpc], writes=["lw2"])
            P.dma("pool", lw_hp[0:96, 3, :], gup_d[128:224, hpc], writes=["lw3"])
            P.dma("sp", lnxb, lnx_d[:, hpc].partition_broadcast(128), writes=["lnxb"])
            for j, (rw_, rk_) in enumerate(((rawr, "rawr"), (rawk, "rawk"), (rawv, "rawv"))):
                def evac_r(tb, ps, bk, rw_=rw_, rk_=rk_):
                    cp_act(rw_[:, 1 + tb * 512:1 + (tb + 1) * 512], ps, reads=[bk], writes=[rk_])
                proj_cols(R_OFF + j * RW + hp * 128, 128, evac_r, pre=(nextW.pop(0) if nextW else None))
                token_shift(rw_, rk_, 128, cb + j, YFflat, "YF")
            r_fm = rawr[:, 1:S + 1]
            k_fm = rawk[:, 1:S + 1]
            v_fm = rawv[:, 1:S + 1]
            if hp == 0:
                tap("r0", r_fm, ["rawr"], [128, S])
                tap("v0", v_fm, ["rawv"], [128, S])
            for cq in range(4):
                bi = pbank[0] % 2
                pbank[0] += 1
                bk = f"B{bi}"
                for j in range(4):
                    c = cq * 4 + j
                    tr(banks[bi][:, j * 128:(j + 1) * 128], v_fm[:, c * 128:(c + 1) * 128], ident, reads=["rawv", "CT"], writes=[bk])
                cp_act(VT[:, cq * 4:cq * 4 + 4, :], banks[bi][:, :].rearrange("p (c n) -> p c n", c=4), reads=[bk], writes=["VT"])

            cut("c_proj")
            P.op("pool", lambda e: e.memset(Hst[0], 0.0), writes=["H0"])
            P.op("pool", lambda e: e.memset(Hst[1], 0.0), writes=["H1"])

            def precompute(it, dr):
                pb = it % 2
                tb = it if dr == 0 else 3 - it
                bsl = slice(tb * 512, (tb + 1) * 512)
                dsl = slice(dr * 64, (dr + 1) * 64)
                sgm, cs, csx, e1, e2, e3, asg, kkr, sq = Tm
                t1 = csx
                k_sgm, k_cs, k_csx, k_e1, k_e2, k_e3, k_asg, k_kkr, k_sq = [f"T{i}" for i in range(9)]
                k_t1 = k_csx
                ARk, BKk, BKTk = f"AR{dr}_{pb}", f"BK{dr}", f"BKT{dr}_{pb}"
                v4 = lambda a: a.rearrange("p (c t) -> p c t", c=4)
                mm(banks[0][:, :], lw_hp[dsl, 0, :], twfb[dsl, bsl], reads=["lw0", "twfb"], writes=["B0"])
                mm(banks[1][:, :], lw_hp[dsl, 1, :], alfb[dsl, bsl], reads=["lw1", "alfb"], writes=["B1"])
                act(sgm, banks[0][:, :], AF.Sigmoid, bias=col(3 + dr), reads=["B0", "CP"], writes=[k_sgm])
                act(asg, banks[1][:, :], AF.Sigmoid, bias=col(5 + dr), reads=["B1", "CP"], writes=[k_asg])
                ts("dve", kkr, k_fm[:, bsl], col(7), ALU.mult, reads=["rawk", "CP"], writes=[k_kkr])
                yield
                tt("pool", sq, kkr, kkr, ALU.mult, reads=[k_kkr], writes=[k_sq])
                for c in range(4):
                    P.op("dve", lambda e, c=c: e.tensor_tensor_scan(cs[:, c * 128:(c + 1) * 128], ones128,
                                                                    sgm[:, c * 128:(c + 1) * 128], 0.0, ALU.mult, ALU.add),
                         reads=[k_sgm, "ones"], writes=[k_cs])
                    if c == 1:
                        yield
                yield
                mm(banks[0][:, :], blockones, sq, reads=["CT", k_sq], writes=["B0"])
                tt("pool", csx, cs, sgm, ALU.subtract, reads=[k_cs, k_sgm], writes=[k_csx])
                act(WCt[:, dr, tb * 4:tb * 4 + 4], cs[:, 127:512:128], AF.Exp, scale=-CDEC, reads=[k_cs], writes=["WC"])
                yield
                act(sq, banks[0][:, :], AF.Ln, bias=tiny_col, reads=["B0", "tiny"], writes=[k_sq])
                act(sq, sq, AF.Exp, scale=-0.5, reads=[k_sq], writes=[k_sq])
                yield
                if dr == 0:
                    act(e1, cs, AF.Exp, scale=-CDEC, reads=[k_cs], writes=[k_e1])
                    act(e2, csx, AF.Exp, scale=-CDEC, reads=[k_csx], writes=[k_e2])
                    act(e3, cs, AF.Exp, scale=CDEC, reads=[k_cs], writes=[k_e3])
                else:
                    act(e1, csx, AF.Exp, scale=CDEC, reads=[k_csx], writes=[k_e1])
                    act(e2, cs, AF.Exp, scale=CDEC, reads=[k_cs], writes=[k_e2])
                    act(e3, csx, AF.Exp, scale=-CDEC, reads=[k_csx], writes=[k_e3])
                tt("pool", kkr, kkr, sq, ALU.mult, reads=[k_kkr, k_sq], writes=[k_kkr])
                yield
                ts("dve", t1, asg, col(8), ALU.mult, OMt[:, cb + 8:cb + 9], ALU.add, reads=[k_asg, "CP", "OM", k_csx], writes=[k_t1])
                yield
                tt("pool", t1, t1, k_fm[:, bsl], ALU.mult, reads=[k_t1, "rawk"], writes=[k_t1])
                tt("dve", AR[dr][pb][:, :, 128:256], v4(r_fm[:, bsl]), v4(e1), ALU.mult, reads=["rawr", k_e1], writes=[ARk])
                yield
                tt("pool", asg, asg, kkr, ALU.mult, reads=[k_asg, k_kkr], writes=[k_asg])
                stt("dve", AR[dr][pb][:, :, 0:128], v4(kkr), -1.0, v4(e2), ALU.mult, ALU.mult, reads=[k_kkr, k_e2], writes=[ARk])
                yield
                tt("pool", BK[dr][:, :, 0:128], v4(asg), v4(e3), ALU.mult, reads=[k_asg, k_e3], writes=[BKk])
                stt("dve", e1, r_fm[:, bsl], col(9), t1, ALU.mult, ALU.mult, reads=["rawr", "CP", k_t1, k_e1, ARk], writes=[k_e1])
                yield
                tt("pool", BK[dr][:, :, 128:256], v4(t1), v4(e3), ALU.mult, reads=[k_t1, k_e3], writes=[BKk])
                for c in range(4):
                    mm(banks[1][:, c * 2:c * 2 + 2], e1[:, c * 128:(c + 1) * 128], blockind, reads=[k_e1, "CT"], writes=["B1"])
                cp_act(Sd[dr][:, tb * 4:tb * 4 + 4, :], banks[1][:, 0:8].rearrange("p (c h) -> p c h", c=4), reads=["B1"], writes=[f"Sd{dr}"])
                yield
                cp_act(BKb[dr][pb], BK[dr], reads=[BKk], writes=[f"BKb{dr}_{pb}"])
                for cpair in range(2):
                    bi = cpair
                    bk = f"B{bi}"
                    for j in range(2):
                        c = cpair * 2 + j
                        tr(banks[bi][:, j * 256:j * 256 + 128], BK[dr][:, c, 0:128], ident, reads=[BKk, "CT"], writes=[bk])
                        tr(banks[bi][:, j * 256 + 128:j * 256 + 256], BK[dr][:, c, 128:256], ident, reads=[BKk, "CT"], writes=[bk])
                    cp_act(BKT[dr][pb][:, cpair * 2:cpair * 2 + 2, :], banks[bi][:, :].rearrange("p (c n) -> p c n", c=2), reads=[bk], writes=[BKTk])
                    yield
                if hp == 0 and it == 0:
                    tap(f"AR{dr}", AR[dr][pb], [ARk], [128, 4, 256], BF16)
                    tap(f"BK{dr}", BK[dr], [BKk], [128, 4, 256])
                    tap(f"BKT{dr}", BKT[dr][pb], [BKTk], [128, 4, 256], BF16)

            def chains_of(it, cl):
                chains = []
                for dr in range(2):
                    tb = it if dr == 0 else 3 - it
                    lc = cl if dr == 0 else 3 - cl
                    chunk = tb * 4 + lc
                    for h in range(2):
                        chains.append((dr, h, lc, chunk, dr * 2 + h))
                return chains

            def S_stages(it, cl):
                chains = chains_of(it, cl)
                pb = it % 2
                par = (it * 4 + cl) % 2
                st = []

                def scores():
                    for (dr, h, lc, chunk, ch) in chains:
                        hs = slice(h * 64, (h + 1) * 64)
                        ARc = AR[dr][pb][hs, lc, :]
                        BKc = BKb[dr][pb][hs, lc, :]
                        ARk, BKk = f"AR{dr}_{pb}", f"BKb{dr}_{pb}"
                        Qb = banks[2 + ch]
                        qk = f"B{2 + ch}"
                        sck = f"SC{ch}_{par}"
                        mm(banks[6][:, 0:256], BKc[:, 0:128], ARc, reads=[BKk, ARk], writes=["B6"])
                        mm(banks[6][:, 256:512], BKc[:, 128:256], ARc, reads=[BKk, ARk], writes=["B6"])
                        mm(Qb[:, 384:512], ARc[:, 0:128], BKc[:, 0:128], reads=[BKk, ARk], writes=[qk])
                        m4 = CT[:, C_M4F:C_M4F + 512] if dr == 0 else CT[:, C_M4B:C_M4B + 512]
                        mt = CT[:, C_MTF:C_MTF + 128] if dr == 0 else CT[:, C_MTB:C_MTB + 128]
                        tt("dve", SC[ch][par], banks[6][:, :], m4, ALU.mult, reads=["B6", "CT"], writes=[sck])
                        px = PXT[ch][0]
                        pk = f"PX{ch}_0"
                        tt("dve", px[:, 256:384], Qb[:, 384:512], mt, ALU.mult, reads=[qk, "CT"], writes=[pk])
                st.append(scores)

                def level(lvl):
                    def f():
                        for (dr, h, lc, chunk, ch) in chains:
                            Qb = banks[2 + ch]
                            qk = f"B{2 + ch}"
                            src = PXT[ch][lvl % 2]
                            dst = PXT[ch][(lvl + 1) % 2]
                            sk = f"PX{ch}_{lvl % 2}"
                            dk = f"PX{ch}_{(lvl + 1) % 2}"
                            if lvl == 0:
                                p0 = SC[ch][par][:, 0:128]
                                sck = f"SC{ch}_{par}"
                                mm(Qb[:, 256:384], p0, src[:, 256:384], reads=[sk, sck], writes=[qk])
                                mm(Qb[:, 0:128], src[:, 256:384], p0, reads=[sk, sck], writes=[qk])
                                mm(Qb[:, 128:256], identb, p0, start=True, stop=False, reads=[sck, "identb"], writes=[qk])
                                mm(Qb[:, 128:256], identb, identb, start=False, stop=True, reads=["identb"], writes=[qk])
                                if ch % 2 == 0:
                                    cp_act(dst, Qb[:, 0:384], reads=[qk], writes=[dk])
                                else:
                                    cp_dve(dst, Qb[:, 0:384], reads=[qk], writes=[dk])
                                continue
                            if lvl < 6:
                                mm(Qb[:, 256:384], src[:, 0:128], src[:, 256:384], reads=[sk], writes=[qk])
                                mm(Qb[:, 0:256], src[:, 256:384], src[:, 0:256], start=True, stop=False, reads=[sk], writes=[qk])
                            else:
                                mm(Qb[:, 128:256], src[:, 256:384], src[:, 128:256], start=True, stop=False, reads=[sk], writes=[qk])
                            mm(Qb[:, 128:256], identb, src[:, 128:256], start=False, stop=True, reads=[sk, "identb"], writes=[qk])
                            if lvl < 6:
                                if ch % 2 == 0:
                                    cp_act(dst, Qb[:, 0:384], reads=[qk], writes=[dk])
                                else:
                                    cp_dve(dst, Qb[:, 0:384], reads=[qk], writes=[dk])
                            else:
                                if ch % 2 == 0:
                                    cp_act(TTt[ch][par], Qb[:, 128:256], reads=[qk], writes=[f"TT{ch}_{par}"])
                                else:
                                    cp_dve(TTt[ch][par], Qb[:, 128:256], reads=[qk], writes=[f"TT{ch}_{par}"])
                    return f
                for lvl in range(7):
                    st.append(level(lvl))
                return st

            def R_stages(it, cl):
                chains = chains_of(it, cl)
                pb = it % 2
                par = (it * 4 + cl) % 2
                st = []

                import os
                KM = "r0,r2"

                def r0():
                    if "r0" not in KM:
                        for (dr, h, lc, chunk, ch) in chains:
                            hs = slice(h * 64, (h + 1) * 64)
                            wci = chunk if dr == 1 else max(chunk - 1, 0)
                            ts("pool", Hb[dr][hs, :], Hst[dr][hs, :], WCt[hs, dr, wci:wci + 1], ALU.mult, reads=[f"H{dr}", "WC"], writes=[f"Hb{dr}_{h}"])
                        return
                    for dr in range(2):
                        chunk = [c_ for (d_, h_, l_, c_, ch_) in chains if d_ == dr][0]
                        wci = chunk if dr == 1 else max(chunk - 1, 0)
                        act(Hb[dr], Hst[dr], AF.Copy, scale=WCt[:, dr, wci:wci + 1], reads=[f"H{dr}", "WC"], writes=[f"Hb{dr}_0", f"Hb{dr}_1"])

                def r1():
                    for (dr, h, lc, chunk, ch) in chains:
                        hs = slice(h * 64, (h + 1) * 64)
                        vsl = VT[:, chunk, h * 64:(h + 1) * 64]
                        za = banks[7][:, ch * 128:ch * 128 + 64]
                        mm(za, AR[dr][pb][hs, lc, 0:128], Hb[dr][hs, :], start=True, stop=False, reads=[f"AR{dr}_{pb}", f"Hb{dr}_{h}"], writes=["B7"])
                        mm(za, SC[ch][par][:, 256:384], vsl, start=False, stop=True, reads=[f"SC{ch}_{par}", "VT"], writes=["B7"])

                def r2():
                    if "r2" not in KM:
                        for (dr, h, lc, chunk, ch) in chains:
                            cp_act(Zs[ch], banks[7][:, ch * 128:ch * 128 + 64], reads=["B7"], writes=[f"Zs{ch}"])
                        return
                    cp_act(Zs_all.rearrange("p (c n) -> p c n", c=4), banks[7][:, :].rearrange("p (c n) -> p c n", c=4)[:, :, 0:64],
                           reads=["B7"], writes=[f"Zs{c_}" for c_ in range(4)])

                def r3():
                    for (dr, h, lc, chunk, ch) in chains:
                        mm(banks[7][:, ch * 128 + 64:ch * 128 + 128], TTt[ch][par], Zs[ch], reads=[f"TT{ch}_{par}", f"Zs{ch}"], writes=["B7"])

                def r4():
                    if "r2" not in KM:
                        for (dr, h, lc, chunk, ch) in chains:
                            cp_act(Us[ch], banks[7][:, ch * 128 + 64:ch * 128 + 128], reads=["B7"], writes=[f"Us{ch}"])
                        return
                    cp_act(Us_all.rearrange("p (c n) -> p c n", c=4), banks[7][:, :].rearrange("p (c n) -> p c n", c=4)[:, :, 64:128],
                           reads=["B7"], writes=[f"Us{c_}" for c_ in range(4)])

                def r5():
                    for (dr, h, lc, chunk, ch) in chains:
                        hs = slice(h * 64, (h + 1) * 64)
                        vsl = VT[:, chunk, h * 64:(h + 1) * 64]
                        ya = banks[7][:, ch * 128:ch * 128 + 64]
                        hb = banks[7][:, ch * 128 + 64:ch * 128 + 128]
                        mm(ya, AR[dr][pb][hs, lc, 128:256], Hb[dr][hs, :], start=True, stop=False, reads=[f"AR{dr}_{pb}", f"Hb{dr}_{h}"], writes=["B7"])
                        mm(ya, SC[ch][par][:, 128:256], Us[ch], start=False, stop=False, reads=[f"SC{ch}_{par}", f"Us{ch}"], writes=["B7"])
                        mm(ya, SC[ch][par][:, 384:512], vsl, start=False, stop=True, reads=[f"SC{ch}_{par}", "VT"], writes=["B7"])
                        mm(hb, BKT[dr][pb][:, lc, 0:128], Us[ch], start=True, stop=False, reads=[f"BKT{dr}_{pb}", f"Us{ch}"], writes=["B7"])
                        mm(hb, BKT[dr][pb][:, lc, 128:256], vsl, start=False, stop=True, reads=[f"BKT{dr}_{pb}", "VT"], writes=["B7"])

                def r6():
                    if "r6" not in KM:
                        for (dr, h, lc, chunk, ch) in chains:
                            ya = banks[7][:, ch * 128:ch * 128 + 64]
                            yk = f"YF{chunk}_{h}"
                            if it < 2:
                                cp_act(YF[:, chunk, h * 64:(h + 1) * 64], ya, reads=["B7", "YF"], writes=[yk])
                            else:
                                tt("dve", YF[:, chunk, h * 64:(h + 1) * 64], YF[:, chunk, h * 64:(h + 1) * 64], ya, ALU.add, reads=["B7", yk], writes=[yk])
                    for dr in (range(2) if "r6" in KM else []):
                        chunk = [c_ for (d_, h_, l_, c_, ch_) in chains if d_ == dr][0]
                        src = banks[7][:, dr * 256:(dr + 1) * 256].rearrange("p (c n) -> p c n", c=2)[:, :, 0:64]
                        dst = YF[:, chunk, :].rearrange("p (c n) -> p c n", c=2)
                        yk2 = [f"YF{chunk}_0", f"YF{chunk}_1"]
                        if it < 2:
                            cp_act(dst, src, reads=["B7", "YF"], writes=yk2)
                        else:
                            tt("dve", dst, dst, src, ALU.add, reads=["B7"] + yk2, writes=yk2)
                    for (dr, h, lc, chunk, ch) in chains:
                        hs = slice(h * 64, (h + 1) * 64)
                        hb = banks[7][hs, ch * 128 + 64:ch * 128 + 128]
                        wci = chunk if dr == 1 else max(chunk - 1, 0)
                        stt("dve", Hst[dr][hs, :], Hst[dr][hs, :], WCt[hs, dr, wci:wci + 1], hb, ALU.mult, ALU.add,
                            reads=[f"H{dr}", "WC", "B7", f"Hb{dr}_{h}"], writes=[f"H{dr}"])
                return [r0, r1, r2, r3, r4, r5, r6]

            for g_ in (precompute(0, 0), precompute(0, 1)):
                for _ in g_:
                    pass
            for f in S_stages(0, 0):
                f()
            for it in range(4):
                gens = [precompute(it + 1, 0), precompute(it + 1, 1)] if it < 3 else []
                for cl in range(4):
                    nxt = (it, cl + 1) if cl < 3 else ((it + 1, 0) if it < 3 else None)
                    if cl == 3:
                        for g_ in gens:
                            for _ in g_:
                                pass
                        gens = []
                    if it == 2 and cl == 0 and hp + 1 < n_hp:
                        nextW.append(prefetch_W(R_OFF + 0 * RW + (hp + 1) * 128, 128))
                        nextW.append(prefetch_W(R_OFF + 1 * RW + (hp + 1) * 128, 128))
                    rs = R_stages(it, cl)
                    ss = S_stages(*nxt) if nxt is not None else []
                    for k in range(max(len(rs), len(ss))):
                        if k < len(ss):
                            ss[k]()
                        if k < len(rs):
                            rs[k]()
                        if gens:
                            if next(gens[0], "done") == "done":
                                gens.pop(0)
            cut("c_rec")
            yks = [f"YF{c}_{h}" for c in range(16) for h in range(2)]
            Y3 = YFflat.rearrange("p (g n) -> p g n", g=32)
            gtmp = rawv[:, 1:S + 1]
            G3 = gtmp.rearrange("p (g n) -> p g n", g=32)
            V3 = VT.rearrange("p c (h n) -> p (c h) n", h=2)
            bc = lambda a: a.unsqueeze(2).to_broadcast([128, 32, 64])
            if hp == 0:
                tap("yscan", YFflat, yks, [128, S])
            tt("pool", s_ss, Sd[0].rearrange("p c h -> p (c h)"), Sd[1].rearrange("p c h -> p (c h)"), ALU.add, reads=["Sd0", "Sd1"], writes=["s_ss"])
            P.op("dve", lambda e: e.tensor_reduce(out=s_sum, in_=Y3, axis=AX.X, op=ALU.add), reads=yks, writes=["s_sum"])
            ts("dve", s_nm, s_sum, -1.0 / 64, ALU.mult, reads=["s_sum"], writes=["s_nm"])
            tt("dve", Y3, Y3, bc(s_nm), ALU.add, reads=yks + ["s_nm"], writes=["YF"])
            act(G3, Y3, AF.Square, reads=["YF", "rawv", "VT"], writes=["rawv"])
            P.op("dve", lambda e: e.tensor_reduce(out=s_var, in_=G3, axis=AX.X, op=ALU.add), reads=["rawv"], writes=["s_var"])
            ts("dve", s_var, s_var, 1.0 / 64, ALU.mult, GN_EPS, ALU.add, reads=["s_var"], writes=["s_var"])
            act(s_var, s_var, AF.Sqrt, reads=["s_var"], writes=["s_var"])
            P.op("dve", lambda e: e.reciprocal(s_var, s_var), reads=["s_var"], writes=["s_var"])
            tt("pool", G3, V3, bc(s_ss), ALU.mult, reads=["VT", "s_ss", "rawv"], writes=["rawv"])
            tt("dve", Y3, Y3, bc(s_var), ALU.mult, reads=["YF", "s_var"], writes=["YF"])
            lg = lnxb[:, 0, :].unsqueeze(1).to_broadcast([128, 16, 128])
            lb = lnxb[:, 1, :].unsqueeze(1).to_broadcast([128, 16, 128])
            tt("dve", YF, YF, lg, ALU.mult, reads=["YF", "lnxb"], writes=["YF"])
            tt("pool", YF, YF, lb, ALU.add, reads=["YF", "lnxb"], writes=["YF"])
            tt("dve", YFflat, YFflat, gtmp, ALU.add, reads=["YF", "rawv"], writes=["YF"])
            if hp == 0:
                tap("ypre", YFflat, ["YF"], [128, S])
            for tb in range(4):
                bsl = slice(tb * 512, (tb + 1) * 512)
                bi = pbank[0] % 2
                pbank[0] += 1
                bk = f"B{bi}"
                mm(banks[bi][:, :], lw_hp[:, 2, :], sg0[:, bsl], start=True, stop=False, reads=["lw2", "sg0"], writes=[bk])
                mm(banks[bi][:, :], lw_hp[0:96, 3, :], sg1[0:96, bsl], start=False, stop=True, reads=["lw3", "sg1"], writes=[bk])
                cp_act(gS, banks[bi][:, :], reads=[bk], writes=["T0"])
                bi = pbank[0] % 2
                pbank[0] += 1
                bk = f"B{bi}"
                for j in range(4):
                    c = tb * 4 + j
                    tr(banks[bi][:, j * 128:(j + 1) * 128], YF[:, c, :], ident, reads=["YF", "CT"], writes=[bk])
                tt("dve", ych[:, bsl], banks[bi][:, :], gS, ALU.mult, reads=[bk, "T0"], writes=["rawk"])
            P.dma("sp", ycat_d[4 + hp], ych, reads=["rawk"], writes=[f"ycat{4 + hp}"])
            if hp == 0:
                tap("ych0", ych, ["rawk"], [128, S], BF16)

        if stop_after is not None and str(stop_after).startswith("hp"):
            P.emit(finals)
            return nc, tap_out

        P.fence()
        A.release(mixer_mark)
        hB = A.alloc(4 * D).rearrange("p (t d) -> p t d", t=4)
        hTs = A.alloc(16 * 512 // 2, BF16).rearrange("p (k t) -> p k t", k=16)
        ffT = A.alloc(44 * 512 // 2, BF16).rearrange("p (f t) -> p f t", f=44)
        ycS = ffT[:, 0:16, :]
        lnp = A.alloc(4 * D).rearrange("p (w d) -> p w d", w=4)
        P.dma("sp", lnp, ln12_d.partition_broadcast(128), writes=["lnp"])
        xres = [A.alloc(512) for _ in range(2)]
        wpc = [A.alloc(4 * 512 // 2, BF16).rearrange("p (j d) -> p j d", j=4) for _ in range(3)]
        gpc = [A.alloc(16 * 256 // 2, BF16).rearrange("p (k c) -> p k c", k=16) for _ in range(2)]
        upc = [A.alloc(16 * 256 // 2, BF16).rearrange("p (k c) -> p k c", k=16) for _ in range(2)]
        sgt = [A.alloc(512) for _ in range(2)]
        junk = A.alloc(D)
        st_ = A.alloc(8)
        wctr2 = [0]
        xctr = [0]
        gctr = [0]

        def layer_norm(tile2d, key, gi):
            P.op("dve", lambda e: e.tensor_reduce(out=st_[:, 0:1], in_=tile2d, axis=AX.X, op=ALU.add), reads=[key], writes=["st0"])
            act(junk, tile2d, AF.Square, reads=[key], writes=["junk"])
            P.op("dve", lambda e: e.tensor_reduce(out=st_[:, 2:3], in_=junk, axis=AX.X, op=ALU.add), reads=["junk"], writes=["st2"])
            ts("dve", st_[:, 1:2], st_[:, 0:1], 1.0 / D, ALU.mult, reads=["st0"], writes=["st1"])
            tt("dve", st_[:, 5:6], st_[:, 1:2], st_[:, 1:2], ALU.mult, reads=["st1"], writes=["st5"])
            stt("dve", st_[:, 3:4], st_[:, 2:3], 1.0 / D, st_[:, 5:6], ALU.mult, ALU.subtract, reads=["st2", "st5"], writes=["st3"])
            ts("dve", st_[:, 3:4], st_[:, 3:4], LN_EPS, ALU.add, reads=["st3"], writes=["st3"])
            act(st_[:, 4:5], st_[:, 3:4], AF.Sqrt, reads=["st3"], writes=["st4"])
            P.op("dve", lambda e: e.reciprocal(st_[:, 4:5], st_[:, 4:5]), reads=["st4"], writes=["st4"])
            stt("dve", st_[:, 6:7], st_[:, 1:2], -1.0, st_[:, 4:5], ALU.mult, ALU.mult, reads=["st1", "st4"], writes=["st6"])
            act(tile2d, tile2d, AF.Identity, bias=st_[:, 6:7], scale=st_[:, 4:5], reads=[key, "st4", "st6"], writes=[key])
            tt("pool", tile2d, tile2d, lnp[:, gi, :], ALU.mult, reads=[key, "lnp"], writes=[key])
            tt("dve", tile2d, tile2d, lnp[:, gi + 1, :], ALU.add, reads=[key, "lnp"], writes=[key])

        n_tb = int(stop_after[2:]) if str(stop_after).startswith("tb") else 4
        for tb in range(n_tb):
            t0 = tb * 512
            for cc in range(16):
                P.dma("sp", ycS[:, cc, :], ycat_d[cc, :, t0:t0 + 512], reads=[f"ycat{cc}"], writes=[f"ffT{cc}"])
            for db in range(4):
                for ccg in range(4):
                    wi = wctr2[0] % 3
                    wctr2[0] += 1
                    P.dma("pool", wpc[wi], wout_d[ccg * 512:(ccg + 1) * 512, db * 512:(db + 1) * 512].rearrange("(j p) d -> p j d", p=128),
                          writes=[f"wpc{wi}"])
                    for j in range(4):
                        cc = ccg * 4 + j
                        for t4 in range(4):
                            mm(banks[t4][:, :], ycS[:, cc, t4 * 128:(t4 + 1) * 128], wpc[wi][:, j, :], start=(cc == 0), stop=(cc == 15),
                               reads=[f"ffT{cc}", f"wpc{wi}"], writes=[f"B{t4}"])
                for t4 in range(4):
                    xi = xctr[0] % 2
                    xctr[0] += 1
                    P.dma("sp", xres[xi], x_d[t0 + t4 * 128:t0 + (t4 + 1) * 128, db * 512:(db + 1) * 512], writes=[f"xres{xi}"])
                    stt("dve", hB[:, t4, db * 512:(db + 1) * 512], xres[xi], ALPHA, banks[t4][:, :], ALU.mult, ALU.add,
                        reads=[f"xres{xi}", f"B{t4}"], writes=[f"hB{t4}"])
            for t4 in range(4):
                layer_norm(hB[:, t4, :], f"hB{t4}", 0)
            if tb == 0:
                tap("h0", hB[:, 0, :], ["hB0"], [128, D])
            for kc in range(16):
                bi = 4 + kc % 2
                bk = f"B{bi}"
                for t4 in range(4):
                    tr(banks[bi][:, t4 * 128:(t4 + 1) * 128], hB[:, t4, kc * 128:(kc + 1) * 128], ident, reads=[f"hB{t4}", "CT"], writes=[bk])
                if kc % 2 == 0:
                    cp_act(hTs[:, kc, :], banks[bi][:, :], reads=[bk], writes=["hTs"])
                else:
                    cp_dve(hTs[:, kc, :], banks[bi][:, :], reads=[bk], writes=["hTs"])
            for fp in range(22):
                gi = gctr[0] % 2
                gctr[0] += 1
                P.dma("pool", gpc[gi], wg_d[:, fp * 256:(fp + 1) * 256].rearrange("(k p) c -> p k c", p=128), writes=[f"gpc{gi}"])
                P.dma("pool", upc[gi], wu_d[:, fp * 256:(fp + 1) * 256].rearrange("(k p) c -> p k c", p=128), writes=[f"upc{gi}"])
                for j in range(2):
                    fc = fp * 2 + j
                    for kc in range(16):
                        mm(banks[6][:, :], gpc[gi][:, kc, j * 128:(j + 1) * 128], hTs[:, kc, :], start=(kc == 0), stop=(kc == 15),
                           reads=[f"gpc{gi}", "hTs"], writes=["B6"])
                    for kc in range(16):
                        mm(banks[7][:, :], upc[gi][:, kc, j * 128:(j + 1) * 128], hTs[:, kc, :], start=(kc == 0), stop=(kc == 15),
                           reads=[f"upc{gi}", "hTs"], writes=["B7"])
                    si = fc % 2
                    act(sgt[si], banks[6][:, :], AF.Silu, reads=["B6"], writes=[f"sgt{si}"])
                    tt("dve", ffT[:, fc, :], sgt[si], banks[7][:, :], ALU.mult, reads=[f"sgt{si}", "B7"], writes=[f"ffT{fc}"])
            for db in range(4):
                for fg in range(11):
                    wi = wctr2[0] % 3
                    wctr2[0] += 1
                    P.dma("pool", wpc[wi], wd_d[fg * 512:(fg + 1) * 512, db * 512:(db + 1) * 512].rearrange("(j p) d -> p j d", p=128),
                          writes=[f"wpc{wi}"])
                    for j in range(4):
                        fc = fg * 4 + j
                        for t4 in range(4):
                            mm(banks[t4][:, :], ffT[:, fc, t4 * 128:(t4 + 1) * 128], wpc[wi][:, j, :], start=(fc == 0), stop=(fc == 43),
                               reads=[f"ffT{fc}", f"wpc{wi}"], writes=[f"B{t4}"])
                for t4 in range(4):
                    stt("dve", hB[:, t4, db * 512:(db + 1) * 512], hB[:, t4, db * 512:(db + 1) * 512], ALPHA, banks[t4][:, :], ALU.mult, ALU.add,
                        reads=[f"hB{t4}", f"B{t4}"], writes=[f"hB{t4}"])
            for t4 in range(4):
                layer_norm(hB[:, t4, :], f"hB{t4}", 2)
                finals.append(P.dma("sp", out_d[t0 + t4 * 128:t0 + (t4 + 1) * 128, :], hB[:, t4, :], reads=[f"hB{t4}"]))
        P.emit(finals)
        build_program.peak = A.peak
    return nc, tap_out


def _host_inputs(inp):
    f = lambda a: np.ascontiguousarray(np.asarray(a, dtype=np.float32))
    cp = np.zeros((128, NCP), np.float32)
    mu = f(inp["mu_shift"])[0]
    o = lambda off: off - R_OFF
    for hp in range(NHP):
        sl = slice(hp * 128, (hp + 1) * 128)
        cb = hp * 10
        cp[:, cb + 0] = mu[o(R_OFF) + hp * 128: o(R_OFF) + (hp + 1) * 128]
        cp[:, cb + 1] = mu[o(K_OFF) + hp * 128: o(K_OFF) + (hp + 1) * 128]
        cp[:, cb + 2] = mu[o(V_OFF) + hp * 128: o(V_OFF) + (hp + 1) * 128]
        cp[:, cb + 3] = f(inp["w0_f"])[0][sl]
        cp[:, cb + 4] = f(inp["w0_b"])[0][sl]
        cp[:, cb + 5] = f(inp["a0_f"])[0][sl]
        cp[:, cb + 6] = f(inp["a0_b"])[0][sl]
        cp[:, cb + 7] = f(inp["k_k"])[0][sl]
        cp[:, cb + 8] = f(inp["k_a"])[0][sl]
        cp[:, cb + 9] = f(inp["r_k"])[0].reshape(-1)[sl]
    cp[:, 120] = mu[o(G_OFF):o(G_OFF) + 128]
    cp[:96, 121] = mu[o(G_OFF) + 128:o(G_OFF) + 224]
    cp[:, 122] = mu[o(WF_OFF):o(WF_OFF) + 128]
    cp[:, 123] = mu[o(AF_OFF):o(AF_OFF) + 128]
    cs, ss, dn = _host_dft()
    shared = {
        "w_in": f(inp["w_in"])[0], "w_out": f(inp["w_out"])[0], "w_gate": f(inp["w_ffn_gate"])[0],
        "w_up": f(inp["w_ffn_up"])[0], "w_down": f(inp["w_ffn_down"])[0],
        "wupfb": np.ascontiguousarray(np.concatenate([f(inp["w_up_f"])[0], f(inp["w_up_b"])[0]], 0)),
        "aupfb": np.ascontiguousarray(np.concatenate([f(inp["a_up_f"])[0], f(inp["a_up_b"])[0]], 0)),
        "g_up": f(inp["g_up"])[0], "cp": cp,
        "lnx": np.ascontiguousarray(np.stack([f(inp["lnx_g"])[0], f(inp["lnx_b"])[0]], 0)),
        "ln12": np.ascontiguousarray(np.stack([f(inp["ln1_g"])[0], f(inp["ln1_b"])[0], f(inp["ln2_g"])[0], f(inp["ln2_b"])[0]], 0)),
        "consts": _host_consts(), "dftC": cs, "dftS": ss, "dftN": dn,
    }
    return shared


def kernel(**inputs):
    x = np.asarray(inputs["x"], dtype=np.float32)
    shared = _host_inputs(inputs)
    nc, _ = build_program()
    in_maps = []
    for b in range(8):
        m = dict(shared)
        m["x"] = np.ascontiguousarray(x[b])
        m["xT"] = np.ascontiguousarray(x[b].T)
        in_maps.append(m)
    res = run_bass_kernel_spmd(nc, in_maps, core_ids=list(range(8)))
    return np.stack([np.asarray(r["out"], dtype=np.float32) for r in res.results], 0)
```

```python
import contextlib
import numpy as np
import ml_dtypes
import concourse.bass as bass
import concourse.mybir as mybir
from concourse.bass_utils import run_bass_kernel_spmd

F32 = mybir.dt.float32
BF16 = mybir.dt.bfloat16
AF = mybir.ActivationFunctionType
ALU = mybir.AluOpType
AX = mybir.AxisListType

D = 2048
S = 2048
FW = 512
RW = 1536
NH = 24
N = 64
R_OFF = 512
K_OFF = R_OFF + RW
V_OFF = K_OFF + RW
G_OFF = V_OFF + RW
WF_OFF = G_OFF + 224
AF_OFF = WF_OFF + 128
IN_COLS = 5600
DFF = 5632
ALPHA = 2.0 ** 0.25
LN_EPS = 1e-5
GN_EPS = 64e-5
CDEC = float(np.exp(-0.5))
NHP = 12
NCP = 124

ENGS = ("pe", "act", "dve", "pool", "sp")
EPOCH = 12000
NDSEM = 16


class Op:
    __slots__ = ("eng", "idx", "fn", "deps", "is_dma", "needs_inc", "semval", "dma_slot", "dma_val")

    def __init__(self, eng, idx, fn, is_dma):
        self.eng = eng
        self.idx = idx
        self.fn = fn
        self.deps = []
        self.is_dma = is_dma
        self.needs_inc = False
        self.semval = None
        self.dma_slot = None
        self.dma_val = None


class Prog:
    def __init__(self, nc):
        self.nc = nc
        self.ops = {e: [] for e in ENGS}
        self.last_w = {}
        self.readers = {}
        self.ndma = {e: 0 for e in ENGS}
        self.dma_ring = {e: [None] * NDSEM for e in ENGS}
        self.fence_ops = []

    def _add(self, eng, fn, reads, writes, is_dma):
        op = Op(eng, len(self.ops[eng]), fn, is_dma)
        deps = list(self.fence_ops)
        xr = [k for k in reads if len(k) == 2 and k[0] == "B"]
        if xr:
            reads = [k for k in reads if k not in xr]
            writes = list(writes) + xr
        for k in reads:
            w = self.last_w.get(k)
            if w is not None:
                deps.append(w)
        for k in writes:
            w = self.last_w.get(k)
            if w is not None:
                deps.append(w)
            deps.extend(self.readers.get(k, ()))
        if is_dma:
            slot = self.ndma[eng] % NDSEM
            prev = self.dma_ring[eng][slot]
            if prev is not None:
                deps.append(prev)
            op.dma_slot = slot
            op.dma_val = 16 * (self.ndma[eng] // NDSEM + 1)
            self.dma_ring[eng][slot] = op
            self.ndma[eng] += 1
        op.deps = deps
        for k in reads:
            self.readers.setdefault(k, []).append(op)
        for k in writes:
            self.last_w[k] = op
            self.readers[k] = []
        self.ops[eng].append(op)
        return op

    def op(self, eng, fn, reads=(), writes=()):
        return self._add(eng, fn, reads, writes, False)

    def dma(self, eng, out, in_, reads=(), writes=()):
        def fn(e):
            return e.dma_start(out=out, in_=in_)
        return self._add(eng, fn, reads, writes, True)

    def fence(self):
        f = []
        for e in ENGS:
            if self.ops[e]:
                last_c = None
                for o in reversed(self.ops[e]):
                    if not o.is_dma:
                        last_c = o
                        break
                if last_c is not None:
                    f.append(last_c)
            for o in self.dma_ring[e]:
                if o is not None:
                    f.append(o)
        self.fence_ops = f
        self.last_w = {}
        self.readers = {}

    def emit(self, final_wait_ops=()):
        nc = self.nc
        for e in ENGS:
            for op in self.ops[e]:
                nd = []
                seen = set()
                for d in op.deps:
                    if d is op or id(d) in seen:
                        continue
                    seen.add(id(d))
                    if d.is_dma:
                        nd.append(d)
                        continue
                    if d.eng == op.eng and not op.is_dma:
                        if d.eng in ("pe", "sp"):
                            continue
                        if d.idx < op.idx - 2:
                            continue
                    d.needs_inc = True
                    nd.append(d)
                op.deps = nd
        for o in final_wait_ops:
            if not o.is_dma:
                o.needs_inc = True
        nep = {}
        for e in ENGS:
            c = 0
            for op in self.ops[e]:
                if not op.is_dma and op.needs_inc:
                    op.semval = c
                    c += 1
            nep[e] = max(1, -(-c // EPOCH))
        with contextlib.ExitStack() as st:
            csem = {e: [st.enter_context(nc.semaphore(f"c_{e}_{i}")) for i in range(nep[e])] for e in ENGS}
            dsem = {e: [st.enter_context(nc.semaphore(f"d_{e}_{i}")) for i in range(NDSEM)]
                    for e in ENGS if self.ndma[e] > 0}
            block = st.enter_context(nc.Block())

            def run(e, eng):
                waited = {}
                for op in self.ops[e]:
                    for d in op.deps:
                        if d.is_dma:
                            key = ("d", d.eng, d.dma_slot)
                            val = d.dma_val
                            sem = dsem[d.eng][d.dma_slot]
                        else:
                            ep, v = divmod(d.semval, EPOCH)
                            key = ("c", d.eng, ep)
                            val = v + 1
                            sem = csem[d.eng][ep]
                        if waited.get(key, 0) >= val:
                            continue
                        waited[key] = val
                        eng.wait_ge(sem, val)
                    inst = op.fn(eng)
                    if op.is_dma:
                        inst.then_inc(dsem[e][op.dma_slot], 16)
                    elif op.needs_inc:
                        ep, v = divmod(op.semval, EPOCH)
                        inst.then_inc(csem[e][ep], 1)
                if e == "sp":
                    for o in final_wait_ops:
                        if o.is_dma:
                            eng.wait_ge(dsem[o.eng][o.dma_slot], o.dma_val)
                        else:
                            ep, v = divmod(o.semval, EPOCH)
                            eng.wait_ge(csem[o.eng][ep], v + 1)

            @block.tensor
            def _(eng):
                run("pe", eng)

            @block.scalar
            def _(eng):
                run("act", eng)

            @block.vector
            def _(eng):
                run("dve", eng)

            @block.gpsimd
            def _(eng):
                run("pool", eng)

            @block.sync
            def _(eng):
                run("sp", eng)


class Arena:
    def __init__(self, ap, ncols):
        self.ap = ap
        self.n = ncols
        self.top = 0
        self.peak = 0

    def alloc(self, cols, dtype=F32):
        a = self.ap[:, self.top:self.top + cols]
        self.top += cols
        self.peak = max(self.peak, self.top)
        assert self.top <= self.n, f"SBUF arena overflow {self.top} > {self.n}"
        if dtype == BF16:
            a = a.bitcast(BF16)
        return a

    def mark(self):
        return self.top

    def release(self, m):
        self.top = m


C_M4F = 0
C_M4B = 512
C_MTF = 1024
C_MTB = 1152
C_ID = 1280
C_BO = 1408
C_BI = 1536
NCONST = 1538


def _host_consts():
    idx = np.arange(128)
    sf = (idx[:, None] < idx[None, :]).astype(np.float32)
    inf_ = (idx[:, None] <= idx[None, :]).astype(np.float32)
    sb = (idx[:, None] > idx[None, :]).astype(np.float32)
    inb = (idx[:, None] >= idx[None, :]).astype(np.float32)
    c = np.zeros((128, NCONST), np.float32)
    c[:, C_M4F:C_M4F + 512] = np.concatenate([sf, inf_, sf, inf_], 1)
    c[:, C_M4B:C_M4B + 512] = np.concatenate([sb, inb, sb, inb], 1)
    c[:, C_MTF:C_MTF + 128] = sf.T
    c[:, C_MTB:C_MTB + 128] = sb.T
    c[:, C_ID:C_ID + 128] = np.eye(128, dtype=np.float32)
    bo = np.zeros((128, 128), np.float32)
    bo[:64, :64] = 1
    bo[64:, 64:] = 1
    c[:, C_BO:C_BO + 128] = bo
    c[:64, C_BI] = 1
    c[64:, C_BI + 1] = 1
    return c


def _host_dft():
    t = np.arange(S, dtype=np.int64)
    m = (t[:, None] * t[None, :]) % S
    ang = 2.0 * np.pi * m.astype(np.float64) / S
    cs = np.cos(ang).astype(ml_dtypes.bfloat16)
    ss = np.sin(ang).astype(ml_dtypes.bfloat16)
    n = np.arange(128, dtype=np.int64)
    mn = (n[:, None] * n[None, :]) % 128
    an = 2.0 * np.pi * mn.astype(np.float64) / 128
    norm = 1.0 / np.sqrt(S * 128.0)
    dn = np.concatenate([np.cos(an) * norm, -np.sin(an) * norm], 1).astype(ml_dtypes.bfloat16)
    return cs, ss, dn


class _Stop(Exception):
    pass


def build_program(taps=None, stop_after=None):
    try:
        return _build_program(taps, stop_after)
    except _Stop as e:
        return e.args[0]


def _build_program(taps=None, stop_after=None):
    taps = taps or {}
    nc = bass.Bass("TRN2", target_bir_lowering=False)
    din = lambda n, s, d=F32: nc.dram_tensor(n, list(s), d, kind="ExternalInput").ap()
    xT_d = din("xT", [D, S])
    x_d = din("x", [S, D])
    win_d = din("w_in", [D, IN_COLS])
    wout_d = din("w_out", [D, D])
    wg_d = din("w_gate", [D, DFF])
    wu_d = din("w_up", [D, DFF])
    wd_d = din("w_down", [DFF, D])
    wupfb_d = din("wupfb", [128, RW])
    aupfb_d = din("aupfb", [128, RW])
    gup_d = din("g_up", [224, RW])
    cp_d = din("cp", [128, NCP])
    lnx_d = din("lnx", [2, RW])
    ln12_d = din("ln12", [4, D])
    consts_d = din("consts", [128, NCONST])
    dftc_d = din("dftC", [S, S], BF16)
    dfts_d = din("dftS", [S, S], BF16)
    dftn_d = din("dftN", [128, 256], BF16)
    out_d = nc.dram_tensor("out", [S, D], F32, kind="ExternalOutput").ap()
    ycat_d = nc.dram_tensor("ycat_scr", [16, 128, S], BF16).ap()
    tap_out = {}

    with contextlib.ExitStack() as top:
        ARENA_COLS = 52600
        arena_t = top.enter_context(nc.sbuf_tensor("arena", [128, ARENA_COLS], F32))
        A = Arena(arena_t, ARENA_COLS)
        banks = [top.enter_context(nc.psum_tensor(f"bank{i}", [128, 512], F32)) for i in range(8)]
        P = Prog(nc)
        finals = []

        def tap(name, ap, keys, shape, dtype=F32):
            if name not in taps:
                return
            t = nc.dram_tensor("tap_" + name, list(shape), dtype, kind="ExternalOutput").ap()
            tap_out[name] = t
            sel = taps[name]
            finals.append(P.dma("sp", sel(t) if callable(sel) else t, ap, reads=keys))

        def cut(name):
            if stop_after == name:
                for e_ in ("pe", "act", "dve", "pool"):
                    for o_ in reversed(P.ops[e_]):
                        if not o_.is_dma:
                            finals.append(o_)
                            break
                P.emit(finals)
                raise _Stop((nc, tap_out))

        def mm(out, lhsT, rhs, start=True, stop=True, reads=(), writes=()):
            return P.op("pe", lambda e: e.matmul(out, lhsT, rhs, start=start, stop=stop), reads, writes)

        def tr(out, in_, ident, reads=(), writes=()):
            return P.op("pe", lambda e: e.transpose(out, in_, ident), reads, writes)

        def act(out, in_, func, bias=None, scale=1.0, reads=(), writes=(), accum_out=None):
            kw = {}
            if bias is not None:
                kw["bias"] = bias
            if accum_out is not None:
                kw["accum_out"] = accum_out
            return P.op("act", lambda e: e.activation(out=out, in_=in_, func=func, scale=scale, **kw), reads, writes)

        def tt(eng, out, in0, in1, op, reads=(), writes=()):
            return P.op(eng, lambda e: e.tensor_tensor(out=out, in0=in0, in1=in1, op=op), reads, writes)

        def ts(eng, out, in0, s1, op0, s2=None, op1=None, reads=(), writes=()):
            if op1 is None:
                return P.op(eng, lambda e: e.tensor_scalar(out=out, in0=in0, scalar1=s1, scalar2=None, op0=op0), reads, writes)
            return P.op(eng, lambda e: e.tensor_scalar(out=out, in0=in0, scalar1=s1, scalar2=s2, op0=op0, op1=op1), reads, writes)

        def stt(eng, out, in0, scalar, in1, op0, op1, reads=(), writes=()):
            return P.op(eng, lambda e: e.scalar_tensor_tensor(out=out, in0=in0, scalar=scalar, in1=in1, op0=op0, op1=op1), reads, writes)

        def cp_act(out, in_, reads=(), writes=()):
            return P.op("act", lambda e: e.copy(out, in_), reads, writes)

        def cp_dve(out, in_, reads=(), writes=()):
            return P.op("dve", lambda e: e.tensor_copy(out, in_), reads, writes)

        CT = A.alloc(NCONST)
        CPt = A.alloc(NCP)
        OMt = A.alloc(NCP)
        HMt = A.alloc(NCP)
        ident = CT[:, C_ID:C_ID + 128]
        blockones = CT[:, C_BO:C_BO + 128]
        blockind = CT[:, C_BI:C_BI + 2]
        P.dma("sp", CT, consts_d, writes=["CT"])
        P.dma("sp", CPt, cp_d, writes=["CP"])
        ts("dve", OMt, CPt, -1.0, ALU.mult, 1.0, ALU.add, reads=["CP"], writes=["OM"])
        ts("dve", HMt, CPt, 0.5, ALU.mult, reads=["CP"], writes=["HM"])
        ones128 = A.alloc(128)
        P.op("pool", lambda e: e.memset(ones128, 1.0), writes=["ones"])
        tiny_col = A.alloc(1)
        P.op("pool", lambda e: e.memset(tiny_col, 1e-24), writes=["tiny"])

        mixer_mark = A.mark()
        xTs = A.alloc(16 * S // 2, BF16).rearrange("p (k t) -> p k t", k=16)
        for kc in range(16):
            P.dma("pool", xTs[:, kc, :], xT_d[kc * 128:(kc + 1) * 128, :], writes=[f"xT{kc}"])
        XK = [f"xT{kc}" for kc in range(16)]

        NW = 2
        Wb = [A.alloc(16 * 128 // 2, BF16).rearrange("p (k c) -> p k c", k=16) for _ in range(NW)]
        wctr = [0]
        pbank = [0]

        def prefetch_W(c0, ncol):
            wi = wctr[0] % NW
            wctr[0] += 1
            W = Wb[wi]
            wk = f"W{wi}"
            P.dma("pool", W[:, :, 0:ncol], win_d[:, c0:c0 + ncol].rearrange("(k p) c -> p k c", p=128), writes=[wk])
            return (W, wk)

        def proj_cols(c0, ncol, evac, pre=None):
            W, wk = pre if pre is not None else prefetch_W(c0, ncol)
            for tb in range(4):
                bi = pbank[0] % 2
                pbank[0] += 1
                bk = f"B{bi}"
                ps = banks[bi][0:ncol, :]
                for kc in range(16):
                    mm(ps, W[:, kc, 0:ncol], xTs[:, kc, tb * 512:(tb + 1) * 512], start=(kc == 0), stop=(kc == 15),
                       reads=[wk, f"xT{kc}"], writes=[bk])
                evac(tb, ps, bk)

        def token_shift(raw, rk, npart, col, tmp, tk):
            tt("dve", tmp[0:npart, :], raw[0:npart, 0:S], raw[0:npart, 2:S + 2], ALU.add, reads=[rk], writes=[tk])
            act(raw[0:npart, 1:S + 1], raw[0:npart, 1:S + 1], AF.Identity, scale=OMt[0:npart, col:col + 1], reads=[rk, "OM"], writes=[rk])
            stt("dve", raw[0:npart, 1:S + 1], tmp[0:npart, :], HMt[0:npart, col:col + 1], raw[0:npart, 1:S + 1],
                ALU.mult, ALU.add, reads=[rk, tk, "HM"], writes=[rk])

        sg0 = A.alloc(S // 2, BF16)
        sg1 = A.alloc(S // 2, BF16)
        twfb = A.alloc(S // 2, BF16)
        alfb = A.alloc(S // 2, BF16)

        a2_mark = A.mark()
        uT = A.alloc(S // 2, BF16)
        Acs = [A.alloc(16 * 256 // 2, BF16).rearrange("p (c n) -> p c n", c=16) for _ in range(4)]
        dftN = A.alloc(128, BF16)
        P.dma("sp", dftN, dftn_d, writes=["dftN"])
        yfo = [A.alloc(S // 2, BF16) for _ in range(4)]
        for g in range(4):
            def evac_u(tb, ps, bk):
                cp_act(uT[:, tb * 512:(tb + 1) * 512], ps, reads=[bk], writes=["uT"])
            proj_cols(g * 128, 128, evac_u)
            for tcp in range(8):
                bi = pbank[0] % 2
                pbank[0] += 1
                bk = f"B{bi}"
                for j in range(2):
                    tc = tcp * 2 + j
                    mm(banks[bi][:, j * 256:(j + 1) * 256], uT[:, tc * 128:(tc + 1) * 128], dftN,
                       reads=["uT", "dftN"], writes=[bk])
                cp_dve(Acs[g][:, tcp * 2:tcp * 2 + 2, :], banks[bi][:, :].rearrange("p (c n) -> p c n", c=2),
                       reads=[bk], writes=[f"Acs{g}"])
        tap("acs0", Acs[0], ["Acs0"], [128, 16, 256], BF16)
        dpc = [A.alloc(4 * 512 // 2, BF16).rearrange("p (j t) -> p j t", j=4) for _ in range(2)]
        dps = [A.alloc(4 * 512 // 2, BF16).rearrange("p (j t) -> p j t", j=4) for _ in range(2)]
        pc = 0
        for tpb in range(4):
            for q in range(4):
                bsel = pc % 2
                pc += 1
                P.dma("sp", dpc[bsel], dftc_d[q * 512:(q + 1) * 512, tpb * 512:(tpb + 1) * 512].rearrange("(j p) t -> p j t", p=128),
                      writes=[f"dpc{bsel}"])
                P.dma("sp", dps[bsel], dfts_d[q * 512:(q + 1) * 512, tpb * 512:(tpb + 1) * 512].rearrange("(j p) t -> p j t", p=128),
                      writes=[f"dps{bsel}"])
                for g in range(4):
                    for j in range(4):
                        tc = q * 4 + j
                        mm(banks[2 + g][:, :], Acs[g][:, tc, 0:128], dpc[bsel][:, j, :], start=(tc == 0), stop=False,
                           reads=[f"Acs{g}", f"dpc{bsel}"], writes=[f"B{2 + g}"])
                        mm(banks[2 + g][:, :], Acs[g][:, tc, 128:256], dps[bsel][:, j, :], start=False, stop=(tc == 15),
                           reads=[f"Acs{g}", f"dps{bsel}"], writes=[f"B{2 + g}"])
            for g in range(4):
                if g % 2 == 0:
                    cp_act(yfo[g][:, tpb * 512:(tpb + 1) * 512], banks[2 + g][:, :], reads=[f"B{2 + g}"], writes=[f"yfo{g}"])
                else:
                    cp_dve(yfo[g][:, tpb * 512:(tpb + 1) * 512], banks[2 + g][:, :], reads=[f"B{2 + g}"], writes=[f"yfo{g}"])
        for g in range(4):
            P.dma("sp", ycat_d[g], yfo[g], reads=[f"yfo{g}"], writes=[f"ycat{g}"])
        tap("yfo0", yfo[0], ["yfo0"], [128, S], BF16)

        lraw = A.alloc(S + 2)
        ltmp = A.alloc(S)
        P.op("pool", lambda e: e.memset(lraw[:, 0:1], 0.0), writes=["lraw"])
        P.op("pool", lambda e: e.memset(lraw[:, S + 1:S + 2], 0.0), writes=["lraw"])
        lora_specs = [(G_OFF, 128, sg0, "sg0", AF.Sigmoid, 120), (G_OFF + 128, 96, sg1, "sg1", AF.Sigmoid, 121),
                      (WF_OFF, 128, twfb, "twfb", AF.Tanh, 122), (AF_OFF, 128, alfb, "alfb", AF.Identity, 123)]
        for (c0, ncol, dst, dk, fn, col) in lora_specs:
            def evac_l(tb, ps, bk, ncol=ncol):
                cp_act(lraw[0:ncol, 1 + tb * 512:1 + (tb + 1) * 512], ps, reads=[bk], writes=["lraw"])
            proj_cols(c0, ncol, evac_l)
            token_shift(lraw, "lraw", ncol, col, ltmp, "ltmp")
            act(dst[0:ncol, :], lraw[0:ncol, 1:S + 1], fn, reads=["lraw"], writes=[dk])
        tap("twfb", twfb, ["twfb"], [128, S], BF16)
        tap("sg1", sg1, ["sg1"], [128, S], BF16)

        if stop_after == "A1":
            P.emit(finals)
            return nc, tap_out

        P.fence()
        A.release(a2_mark)
        rawr = A.alloc(S + 2)
        rawk = A.alloc(S + 2)
        rawv = A.alloc(S + 2)
        for rw_, k_ in ((rawr, "rawr"), (rawk, "rawk"), (rawv, "rawv")):
            P.op("pool", lambda e, rw_=rw_: e.memset(rw_[:, 0:1], 0.0), writes=[k_])
            P.op("pool", lambda e, rw_=rw_: e.memset(rw_[:, S + 1:S + 2], 0.0), writes=[k_])
        VT = A.alloc(S // 2, BF16).rearrange("p (c n) -> p c n", c=16)
        YF = A.alloc(S).rearrange("p (c n) -> p c n", c=16)
        YFflat = YF.rearrange("p c n -> p (c n)")
        Sd = [A.alloc(32).rearrange("p (c h) -> p c h", c=16) for _ in range(2)]
        WCt = A.alloc(32).rearrange("p (d c) -> p d c", d=2)
        lw_hp = A.alloc(4 * 64, BF16).rearrange("p (w c) -> p w c", w=4)
        lnxb = A.alloc(256).rearrange("p (w c) -> p w c", w=2)
        ych = rawk[:, 2:2 + S // 2].bitcast(BF16)
        AR = [[A.alloc(512, BF16).rearrange("p (c n) -> p c n", c=4) for _ in range(2)] for _ in range(2)]
        BKb = [[A.alloc(512, BF16).rearrange("p (c n) -> p c n", c=4) for _ in range(2)] for _ in range(2)]
        BK = [A.alloc(1024).rearrange("p (c n) -> p c n", c=4) for _ in range(2)]
        BKT = [[A.alloc(512, BF16).rearrange("p (c n) -> p c n", c=4) for _ in range(2)] for _ in range(2)]
        NT = 9
        Tm = [A.alloc(512) for _ in range(NT)]
        SC = [[A.alloc(256, BF16) for _ in range(2)] for _ in range(4)]
        TTt = [[A.alloc(64, BF16) for _ in range(2)] for _ in range(4)]
        PXT = [[A.alloc(192, BF16) for _ in range(2)] for _ in range(4)]
        identb = A.alloc(64, BF16)
        cp_act(identb, ident, reads=["CT"], writes=["identb"])
        Zs_all = A.alloc(128, BF16)
        Zs = [Zs_all[:, ch_ * 64:(ch_ + 1) * 64] for ch_ in range(4)]
        Us_all = A.alloc(128, BF16)
        Us = [Us_all[:, ch_ * 64:(ch_ + 1) * 64] for ch_ in range(4)]
        Hb = [A.alloc(32, BF16) for _ in range(2)]
        Hst = [A.alloc(64) for _ in range(2)]
        gS = Tm[0]
        small = A.alloc(32 * 4)
        s_sum, s_nm, s_var, s_ss = (small[:, i * 32:(i + 1) * 32] for i in range(4))

        n_hp = int(stop_after[2:]) if str(stop_after).startswith("hp") else NHP
        nextW = []
        for hp in range(n_hp):
            cb = hp * 10
            col = lambda j: CPt[:, cb + j:cb + j + 1]
            hpc = slice(hp * 128, (hp + 1) * 128)
            P.dma("pool", lw_hp[:, 0, :], wupfb_d[:, hpc], writes=["lw0"])
            P.dma("pool", lw_hp[:, 1, :], aupfb_d[:, hpc], writes=["lw1"])
            P.dma("pool", lw_hp[:, 2, :], gup_d[0:128, hpc], writes=["lw2"])
            P.dma("pool", lw_hp[0:96, 3, :], gup_d[128:224, hpc], writes=["lw3"])
            P.dma("sp", lnxb, lnx_d[:, hpc].partition_broadcast(128), writes=["lnxb"])
            for j, (rw_, rk_) in enumerate(((rawr, "rawr"), (rawk, "rawk"), (rawv, "rawv"))):
                def evac_r(tb, ps, bk, rw_=rw_, rk_=rk_):
                    cp_act(rw_[:, 1 + tb * 512:1 + (tb + 1) * 512], ps, reads=[bk], writes=[rk_])
                proj_cols(R_OFF + j * RW + hp * 128, 128, evac_r, pre=(nextW.pop(0) if nextW else None))
                token_shift(rw_, rk_, 128, cb + j, YFflat, "YF")
            r_fm = rawr[:, 1:S + 1]
            k_fm = rawk[:, 1:S + 1]
            v_fm = rawv[:, 1:S + 1]
            if hp == 0:
                tap("r0", r_fm, ["rawr"], [128, S])
                tap("v0", v_fm, ["rawv"], [128, S])
            for cq in range(4):
                bi = pbank[0] % 2
                pbank[0] += 1
                bk = f"B{bi}"
                for j in range(4):
                    c = cq * 4 + j
                    tr(banks[bi][:, j * 128:(j + 1) * 128], v_fm[:, c * 128:(c + 1) * 128], ident, reads=["rawv", "CT"], writes=[bk])
                cp_act(VT[:, cq * 4:cq * 4 + 4, :], banks[bi][:, :].rearrange("p (c n) -> p c n", c=4), reads=[bk], writes=["VT"])

            cut("c_proj")
            P.op("pool", lambda e: e.memset(Hst[0], 0.0), writes=["H0"])
            P.op("pool", lambda e: e.memset(Hst[1], 0.0), writes=["H1"])

            def precompute(it, dr):
                pb = it % 2
                tb = it if dr == 0 else 3 - it
                bsl = slice(tb * 512, (tb + 1) * 512)
                dsl = slice(dr * 64, (dr + 1) * 64)
                sgm, cs, csx, e1, e2, e3, asg, kkr, sq = Tm
                t1 = csx
                k_sgm, k_cs, k_csx, k_e1, k_e2, k_e3, k_asg, k_kkr, k_sq = [f"T{i}" for i in range(9)]
                k_t1 = k_csx
                ARk, BKk, BKTk = f"AR{dr}_{pb}", f"BK{dr}", f"BKT{dr}_{pb}"
                v4 = lambda a: a.rearrange("p (c t) -> p c t", c=4)
                mm(banks[0][:, :], lw_hp[dsl, 0, :], twfb[dsl, bsl], reads=["lw0", "twfb"], writes=["B0"])
                act(sgm, banks[0][:, :], AF.Sigmoid, bias=col(3 + dr), reads=["B0", "CP"], writes=[k_sgm])
                mm(banks[0][:, :], lw_hp[dsl, 1, :], alfb[dsl, bsl], reads=["lw1", "alfb"], writes=["B0"])
                act(asg, banks[0][:, :], AF.Sigmoid, bias=col(5 + dr), reads=["B0", "CP"], writes=[k_asg])
                ts("dve", kkr, k_fm[:, bsl], col(7), ALU.mult, reads=["rawk", "CP"], writes=[k_kkr])
                yield
                tt("pool", sq, kkr, kkr, ALU.mult, reads=[k_kkr], writes=[k_sq])
                for c in range(4):
                    P.op("dve", lambda e, c=c: e.tensor_tensor_scan(cs[:, c * 128:(c + 1) * 128], ones128,
                                                                    sgm[:, c * 128:(c + 1) * 128], 0.0, ALU.mult, ALU.add),
                         reads=[k_sgm, "ones"], writes=[k_cs])
                    if c == 1:
                        yield
                yield
                mm(banks[0][:, :], blockones, sq, reads=["CT", k_sq], writes=["B0"])
                tt("pool", csx, cs, sgm, ALU.subtract, reads=[k_cs, k_sgm], writes=[k_csx])
                act(WCt[:, dr, tb * 4:tb * 4 + 4], cs[:, 127:512:128], AF.Exp, scale=-CDEC, reads=[k_cs], writes=["WC"])
                yield
                act(sq, banks[0][:, :], AF.Ln, bias=tiny_col, reads=["B0", "tiny"], writes=[k_sq])
                act(sq, sq, AF.Exp, scale=-0.5, reads=[k_sq], writes=[k_sq])
                yield
                if dr == 0:
                    act(e1, cs, AF.Exp, scale=-CDEC, reads=[k_cs], writes=[k_e1])
                    act(e2, csx, AF.Exp, scale=-CDEC, reads=[k_csx], writes=[k_e2])
                    act(e3, cs, AF.Exp, scale=CDEC, reads=[k_cs], writes=[k_e3])
                else:
                    act(e1, csx, AF.Exp, scale=CDEC, reads=[k_csx], writes=[k_e1])
                    act(e2, cs, AF.Exp, scale=CDEC, reads=[k_cs], writes=[k_e2])
                    act(e3, csx, AF.Exp, scale=-CDEC, reads=[k_csx], writes=[k_e3])
                tt("pool", kkr, kkr, sq, ALU.mult, reads=[k_kkr, k_sq], writes=[k_kkr])
                yield
                ts("dve", t1, asg, col(8), ALU.mult, OMt[:, cb + 8:cb + 9], ALU.add, reads=[k_asg, "CP", "OM", k_csx], writes=[k_t1])
                yield
                tt("pool", t1, t1, k_fm[:, bsl], ALU.mult, reads=[k_t1, "rawk"], writes=[k_t1])
                tt("dve", AR[dr][pb][:, :, 128:256], v4(r_fm[:, bsl]), v4(e1), ALU.mult, reads=["rawr", k_e1], writes=[ARk])
                yield
                tt("pool", asg, asg, kkr, ALU.mult, reads=[k_asg, k_kkr], writes=[k_asg])
                stt("dve", AR[dr][pb][:, :, 0:128], v4(kkr), -1.0, v4(e2), ALU.mult, ALU.mult, reads=[k_kkr, k_e2], writes=[ARk])
                yield
                tt("pool", BK[dr][:, :, 0:128], v4(asg), v4(e3), ALU.mult, reads=[k_asg, k_e3], writes=[BKk])
                stt("dve", e1, r_fm[:, bsl], col(9), t1, ALU.mult, ALU.mult, reads=["rawr", "CP", k_t1, k_e1, ARk], writes=[k_e1])
                yield
                tt("pool", BK[dr][:, :, 128:256], v4(t1), v4(e3), ALU.mult, reads=[k_t1, k_e3], writes=[BKk])
                for c in range(4):
                    mm(banks[0][:, c * 2:c * 2 + 2], e1[:, c * 128:(c + 1) * 128], blockind, reads=[k_e1, "CT"], writes=["B0"])
                cp_act(Sd[dr][:, tb * 4:tb * 4 + 4, :], banks[0][:, 0:8].rearrange("p (c h) -> p c h", c=4), reads=["B0"], writes=[f"Sd{dr}"])
                yield
                cp_act(BKb[dr][pb], BK[dr], reads=[BKk], writes=[f"BKb{dr}_{pb}"])
                for cpair in range(2):
                    bi = 0
                    bk = f"B{bi}"
                    for j in range(2):
                        c = cpair * 2 + j
                        tr(banks[bi][:, j * 256:j * 256 + 128], BK[dr][:, c, 0:128], ident, reads=[BKk, "CT"], writes=[bk])
                        tr(banks[bi][:, j * 256 + 128:j * 256 + 256], BK[dr][:, c, 128:256], ident, reads=[BKk, "CT"], writes=[bk])
                    cp_act(BKT[dr][pb][:, cpair * 2:cpair * 2 + 2, :], banks[bi][:, :].rearrange("p (c n) -> p c n", c=2), reads=[bk], writes=[BKTk])
                    yield
                if hp == 0 and it == 0:
                    tap(f"AR{dr}", AR[dr][pb], [ARk], [128, 4, 256], BF16)
                    tap(f"BK{dr}", BK[dr], [BKk], [128, 4, 256])
                    tap(f"BKT{dr}", BKT[dr][pb], [BKTk], [128, 4, 256], BF16)

            def chains_of(it, cl):
                chains = []
                for dr in range(2):
                    tb = it if dr == 0 else 3 - it
                    lc = cl if dr == 0 else 3 - cl
                    chunk = tb * 4 + lc
                    for h in range(2):
                        chains.append((dr, h, lc, chunk, dr * 2 + h))
                return chains

            def S_stages(it, cl):
                chains = chains_of(it, cl)
                pb = it % 2
                par = (it * 4 + cl) % 2
                st = []

                def scores():
                    for (dr, h, lc, chunk, ch) in chains:
                        hs = slice(h * 64, (h + 1) * 64)
                        ARc = AR[dr][pb][hs, lc, :]
                        BKc = BKb[dr][pb][hs, lc, :]
                        ARk, BKk = f"AR{dr}_{pb}", f"BKb{dr}_{pb}"
                        Qb = banks[2 + ch]
                        qk = f"B{2 + ch}"
                        sck = f"SC{ch}_{par}"
                        sbank = banks[6] if ch % 2 == 0 else banks[1]
                        sbk = "B6" if ch % 2 == 0 else "B1"
                        mm(sbank[:, 0:256], BKc[:, 0:128], ARc, reads=[BKk, ARk], writes=[sbk])
                        mm(sbank[:, 256:512], BKc[:, 128:256], ARc, reads=[BKk, ARk], writes=[sbk])
                        mm(Qb[:, 384:512], ARc[:, 0:128], BKc[:, 0:128], reads=[BKk, ARk], writes=[qk])
                        m4 = CT[:, C_M4F:C_M4F + 512] if dr == 0 else CT[:, C_M4B:C_M4B + 512]
                        mt = CT[:, C_MTF:C_MTF + 128] if dr == 0 else CT[:, C_MTB:C_MTB + 128]
                        tt("dve", SC[ch][par], sbank[:, :], m4, ALU.mult, reads=[sbk, "CT"], writes=[sck])
                        px = PXT[ch][0]
                        pk = f"PX{ch}_0"
                        tt("dve", px[:, 256:384], Qb[:, 384:512], mt, ALU.mult, reads=[qk, "CT"], writes=[pk])
                st.append(scores)

                def level(lvl):
                    def f():
                        for (dr, h, lc, chunk, ch) in chains:
                            Qb = banks[2 + ch]
                            qk = f"B{2 + ch}"
                            src = PXT[ch][lvl % 2]
                            dst = PXT[ch][(lvl + 1) % 2]
                            sk = f"PX{ch}_{lvl % 2}"
                            dk = f"PX{ch}_{(lvl + 1) % 2}"
                            if lvl == 0:
                                p0 = SC[ch][par][:, 0:128]
                                sck = f"SC{ch}_{par}"
                                mm(Qb[:, 256:384], p0, src[:, 256:384], reads=[sk, sck], writes=[qk])
                                mm(Qb[:, 0:128], src[:, 256:384], p0, reads=[sk, sck], writes=[qk])
                                mm(Qb[:, 128:256], identb, p0, start=True, stop=False, reads=[sck, "identb"], writes=[qk])
                                mm(Qb[:, 128:256], identb, identb, start=False, stop=True, reads=["identb"], writes=[qk])
                                if ch % 2 == 0:
                                    cp_act(dst, Qb[:, 0:384], reads=[qk], writes=[dk])
                                else:
                                    cp_dve(dst, Qb[:, 0:384], reads=[qk], writes=[dk])
                                continue
                            if lvl < 6:
                                mm(Qb[:, 256:384], src[:, 0:128], src[:, 256:384], reads=[sk], writes=[qk])
                                mm(Qb[:, 0:256], src[:, 256:384], src[:, 0:256], start=True, stop=False, reads=[sk], writes=[qk])
                            else:
                                mm(Qb[:, 128:256], src[:, 256:384], src[:, 128:256], start=True, stop=False, reads=[sk], writes=[qk])
                            mm(Qb[:, 128:256], identb, src[:, 128:256], start=False, stop=True, reads=[sk, "identb"], writes=[qk])
                            if lvl < 6:
                                if ch % 2 == 0:
                                    cp_act(dst, Qb[:, 0:384], reads=[qk], writes=[dk])
                                else:
                                    cp_dve(dst, Qb[:, 0:384], reads=[qk], writes=[dk])
                            else:
                                if ch % 2 == 0:
                                    cp_act(TTt[ch][par], Qb[:, 128:256], reads=[qk], writes=[f"TT{ch}_{par}"])
                                else:
                                    cp_dve(TTt[ch][par], Qb[:, 128:256], reads=[qk], writes=[f"TT{ch}_{par}"])
                    return f
                for lvl in range(7):
                    st.append(level(lvl))
                return st

            def R_stages(it, cl):
                chains = chains_of(it, cl)
                pb = it % 2
                par = (it * 4 + cl) % 2
                st = []

                import os
                KM = "r0,r2"

                def r0():
                    if "r0" not in KM:
                        for (dr, h, lc, chunk, ch) in chains:
                            hs = slice(h * 64, (h + 1) * 64)
                            wci = chunk if dr == 1 else max(chunk - 1, 0)
                            ts("pool", Hb[dr][hs, :], Hst[dr][hs, :], WCt[hs, dr, wci:wci + 1], ALU.mult, reads=[f"H{dr}", "WC"], writes=[f"Hb{dr}_{h}"])
                        return
                    for dr in range(2):
                        chunk = [c_ for (d_, h_, l_, c_, ch_) in chains if d_ == dr][0]
                        wci = chunk if dr == 1 else max(chunk - 1, 0)
                        act(Hb[dr], Hst[dr], AF.Copy, scale=WCt[:, dr, wci:wci + 1], reads=[f"H{dr}", "WC"], writes=[f"Hb{dr}_0", f"Hb{dr}_1"])

                def r1():
                    for (dr, h, lc, chunk, ch) in chains:
                        hs = slice(h * 64, (h + 1) * 64)
                        vsl = VT[:, chunk, h * 64:(h + 1) * 64]
                        za = banks[7][:, ch * 128:ch * 128 + 64]
                        mm(za, AR[dr][pb][hs, lc, 0:128], Hb[dr][hs, :], start=True, stop=False, reads=[f"AR{dr}_{pb}", f"Hb{dr}_{h}"], writes=["B7"])
                        mm(za, SC[ch][par][:, 256:384], vsl, start=False, stop=True, reads=[f"SC{ch}_{par}", "VT"], writes=["B7"])

                def r2():
                    if "r2" not in KM:
                        for (dr, h, lc, chunk, ch) in chains:
                            cp_act(Zs[ch], banks[7][:, ch * 128:ch * 128 + 64], reads=["B7"], writes=[f"Zs{ch}"])
                        return
                    cp_act(Zs_all.rearrange("p (c n) -> p c n", c=4), banks[7][:, :].rearrange("p (c n) -> p c n", c=4)[:, :, 0:64],
                           reads=["B7"], writes=[f"Zs{c_}" for c_ in range(4)])

                def r3():
                    for (dr, h, lc, chunk, ch) in chains:
                        mm(banks[7][:, ch * 128 + 64:ch * 128 + 128], TTt[ch][par], Zs[ch], reads=[f"TT{ch}_{par}", f"Zs{ch}"], writes=["B7"])

                def r4():
                    if "r2" not in KM:
                        for (dr, h, lc, chunk, ch) in chains:
                            cp_act(Us[ch], banks[7][:, ch * 128 + 64:ch * 128 + 128], reads=["B7"], writes=[f"Us{ch}"])
                        return
                    cp_act(Us_all.rearrange("p (c n) -> p c n", c=4), banks[7][:, :].rearrange("p (c n) -> p c n", c=4)[:, :, 64:128],
                           reads=["B7"], writes=[f"Us{c_}" for c_ in range(4)])

                def r5():
                    for (dr, h, lc, chunk, ch) in chains:
                        hs = slice(h * 64, (h + 1) * 64)
                        vsl = VT[:, chunk, h * 64:(h + 1) * 64]
                        ya = banks[7][:, ch * 128:ch * 128 + 64]
                        hb = banks[7][:, ch * 128 + 64:ch * 128 + 128]
                        mm(ya, AR[dr][pb][hs, lc, 128:256], Hb[dr][hs, :], start=True, stop=False, reads=[f"AR{dr}_{pb}", f"Hb{dr}_{h}"], writes=["B7"])
                        mm(ya, SC[ch][par][:, 128:256], Us[ch], start=False, stop=False, reads=[f"SC{ch}_{par}", f"Us{ch}"], writes=["B7"])
                        mm(ya, SC[ch][par][:, 384:512], vsl, start=False, stop=True, reads=[f"SC{ch}_{par}", "VT"], writes=["B7"])
                        mm(hb, BKT[dr][pb][:, lc, 0:128], Us[ch], start=True, stop=False, reads=[f"BKT{dr}_{pb}", f"Us{ch}"], writes=["B7"])
                        mm(hb, BKT[dr][pb][:, lc, 128:256], vsl, start=False, stop=True, reads=[f"BKT{dr}_{pb}", "VT"], writes=["B7"])

                def r6():
                    if "r6" not in KM:
                        for (dr, h, lc, chunk, ch) in chains:
                            ya = banks[7][:, ch * 128:ch * 128 + 64]
                            yk = f"YF{chunk}_{h}"
                            if it < 2:
                                cp_act(YF[:, chunk, h * 64:(h + 1) * 64], ya, reads=["B7", "YF"], writes=[yk])
                            else:
                                tt("dve", YF[:, chunk, h * 64:(h + 1) * 64], YF[:, chunk, h * 64:(h + 1) * 64], ya, ALU.add, reads=["B7", yk], writes=[yk])
                    for dr in (range(2) if "r6" in KM else []):
                        chunk = [c_ for (d_, h_, l_, c_, ch_) in chains if d_ == dr][0]
                        src = banks[7][:, dr * 256:(dr + 1) * 256].rearrange("p (c n) -> p c n", c=2)[:, :, 0:64]
                        dst = YF[:, chunk, :].rearrange("p (c n) -> p c n", c=2)
                        yk2 = [f"YF{chunk}_0", f"YF{chunk}_1"]
                        if it < 2:
                            cp_act(dst, src, reads=["B7", "YF"], writes=yk2)
                        else:
                            tt("dve", dst, dst, src, ALU.add, reads=["B7"] + yk2, writes=yk2)
                    for (dr, h, lc, chunk, ch) in chains:
                        hs = slice(h * 64, (h + 1) * 64)
                        hb = banks[7][hs, ch * 128 + 64:ch * 128 + 128]
                        wci = chunk if dr == 1 else max(chunk - 1, 0)
                        stt("dve", Hst[dr][hs, :], Hst[dr][hs, :], WCt[hs, dr, wci:wci + 1], hb, ALU.mult, ALU.add,
                            reads=[f"H{dr}", "WC", "B7", f"Hb{dr}_{h}"], writes=[f"H{dr}"])
                return [r0, r1, r2, r3, r4, r5, r6]

            for g_ in (precompute(0, 0), precompute(0, 1)):
                for _ in g_:
                    pass
            for f in S_stages(0, 0):
                f()
            for it in range(4):
                gens = [precompute(it + 1, 0), precompute(it + 1, 1)] if it < 3 else []
                for cl in range(4):
                    nxt = (it, cl + 1) if cl < 3 else ((it + 1, 0) if it < 3 else None)
                    if cl == 3:
                        for g_ in gens:
                            for _ in g_:
                                pass
                        gens = []
                    if it == 2 and cl == 0 and hp + 1 < n_hp:
                        nextW.append(prefetch_W(R_OFF + 0 * RW + (hp + 1) * 128, 128))
                        nextW.append(prefetch_W(R_OFF + 1 * RW + (hp + 1) * 128, 128))
                    rs = R_stages(it, cl)
                    ss = S_stages(*nxt) if nxt is not None else []
                    for k in range(max(len(rs), len(ss))):
                        if k < len(ss):
                            ss[k]()
                        if k < len(rs):
                            rs[k]()
                        if gens:
                            if next(gens[0], "done") == "done":
                                gens.pop(0)
            cut("c_rec")
            yks = [f"YF{c}_{h}" for c in range(16) for h in range(2)]
            Y3 = YFflat.rearrange("p (g n) -> p g n", g=32)
            gtmp = rawv[:, 1:S + 1]
            G3 = gtmp.rearrange("p (g n) -> p g n", g=32)
            V3 = VT.rearrange("p c (h n) -> p (c h) n", h=2)
            bc = lambda a: a.unsqueeze(2).to_broadcast([128, 32, 64])
            if hp == 0:
                tap("yscan", YFflat, yks, [128, S])
            tt("pool", s_ss, Sd[0].rearrange("p c h -> p (c h)"), Sd[1].rearrange("p c h -> p (c h)"), ALU.add, reads=["Sd0", "Sd1"], writes=["s_ss"])
            P.op("dve", lambda e: e.tensor_reduce(out=s_sum, in_=Y3, axis=AX.X, op=ALU.add), reads=yks, writes=["s_sum"])
            ts("dve", s_nm, s_sum, -1.0 / 64, ALU.mult, reads=["s_sum"], writes=["s_nm"])
            tt("dve", Y3, Y3, bc(s_nm), ALU.add, reads=yks + ["s_nm"], writes=["YF"])
            act(G3, Y3, AF.Square, reads=["YF", "rawv", "VT"], writes=["rawv"])
            P.op("dve", lambda e: e.tensor_reduce(out=s_var, in_=G3, axis=AX.X, op=ALU.add), reads=["rawv"], writes=["s_var"])
            ts("dve", s_var, s_var, 1.0 / 64, ALU.mult, GN_EPS, ALU.add, reads=["s_var"], writes=["s_var"])
            act(s_var, s_var, AF.Sqrt, reads=["s_var"], writes=["s_var"])
            P.op("dve", lambda e: e.reciprocal(s_var, s_var), reads=["s_var"], writes=["s_var"])
            tt("pool", G3, V3, bc(s_ss), ALU.mult, reads=["VT", "s_ss", "rawv"], writes=["rawv"])
            tt("dve", Y3, Y3, bc(s_var), ALU.mult, reads=["YF", "s_var"], writes=["YF"])
            lg = lnxb[:, 0, :].unsqueeze(1).to_broadcast([128, 16, 128])
            lb = lnxb[:, 1, :].unsqueeze(1).to_broadcast([128, 16, 128])
            tt("dve", YF, YF, lg, ALU.mult, reads=["YF", "lnxb"], writes=["YF"])
            tt("pool", YF, YF, lb, ALU.add, reads=["YF", "lnxb"], writes=["YF"])
            tt("dve", YFflat, YFflat, gtmp, ALU.add, reads=["YF", "rawv"], writes=["YF"])
            if hp == 0:
                tap("ypre", YFflat, ["YF"], [128, S])
            for tb in range(4):
                bsl = slice(tb * 512, (tb + 1) * 512)
                bi = pbank[0] % 2
                pbank[0] += 1
                bk = f"B{bi}"
                mm(banks[bi][:, :], lw_hp[:, 2, :], sg0[:, bsl], start=True, stop=False, reads=["lw2", "sg0"], writes=[bk])
                mm(banks[bi][:, :], lw_hp[0:96, 3, :], sg1[0:96, bsl], start=False, stop=True, reads=["lw3", "sg1"], writes=[bk])
                cp_act(gS, banks[bi][:, :], reads=[bk], writes=["T0"])
                bi = pbank[0] % 2
                pbank[0] += 1
                bk = f"B{bi}"
                for j in range(4):
                    c = tb * 4 + j
                    tr(banks[bi][:, j * 128:(j + 1) * 128], YF[:, c, :], ident, reads=["YF", "CT"], writes=[bk])
                tt("dve", ych[:, bsl], banks[bi][:, :], gS, ALU.mult, reads=[bk, "T0"], writes=["rawk"])
            P.dma("sp", ycat_d[4 + hp], ych, reads=["rawk"], writes=[f"ycat{4 + hp}"])
            if hp == 0:
                tap("ych0", ych, ["rawk"], [128, S], BF16)

        if stop_after is not None and str(stop_after).startswith("hp"):
            P.emit(finals)
            return nc, tap_out

        P.fence()
        A.release(mixer_mark)
        hB = A.alloc(4 * D).rearrange("p (t d) -> p t d", t=4)
        hTs = A.alloc(16 * 512 // 2, BF16).rearrange("p (k t) -> p k t", k=16)
        ffT = A.alloc(44 * 512 // 2, BF16).rearrange("p (f t) -> p f t", f=44)
        ycS = ffT[:, 0:16, :]
        lnp = A.alloc(4 * D).rearrange("p (w d) -> p w d", w=4)
        P.dma("sp", lnp, ln12_d.partition_broadcast(128), writes=["lnp"])
        xres = [A.alloc(512) for _ in range(2)]
        wpc = [A.alloc(4 * 512 // 2, BF16).rearrange("p (j d) -> p j d", j=4) for _ in range(3)]
        gpc = [A.alloc(16 * 256 // 2, BF16).rearrange("p (k c) -> p k c", k=16) for _ in range(2)]
        upc = [A.alloc(16 * 256 // 2, BF16).rearrange("p (k c) -> p k c", k=16) for _ in range(2)]
        sgt = [A.alloc(512) for _ in range(2)]
        junk = A.alloc(D)
        st_ = A.alloc(8)
        wctr2 = [0]
        xctr = [0]
        gctr = [0]

        def layer_norm(tile2d, key, gi):
            P.op("dve", lambda e: e.tensor_reduce(out=st_[:, 0:1], in_=tile2d, axis=AX.X, op=ALU.add), reads=[key], writes=["st0"])
            act(junk, tile2d, AF.Square, reads=[key], writes=["junk"])
            P.op("dve", lambda e: e.tensor_reduce(out=st_[:, 2:3], in_=junk, axis=AX.X, op=ALU.add), reads=["junk"], writes=["st2"])
            ts("dve", st_[:, 1:2], st_[:, 0:1], 1.0 / D, ALU.mult, reads=["st0"], writes=["st1"])
            tt("dve", st_[:, 5:6], st_[:, 1:2], st_[:, 1:2], ALU.mult, reads=["st1"], writes=["st5"])
            stt("dve", st_[:, 3:4], st_[:, 2:3], 1.0 / D, st_[:, 5:6], ALU.mult, ALU.subtract, reads=["st2", "st5"], writes=["st3"])
            ts("dve", st_[:, 3:4], st_[:, 3:4], LN_EPS, ALU.add, reads=["st3"], writes=["st3"])
            act(st_[:, 4:5], st_[:, 3:4], AF.Sqrt, reads=["st3"], writes=["st4"])
            P.op("dve", lambda e: e.reciprocal(st_[:, 4:5], st_[:, 4:5]), reads=["st4"], writes=["st4"])
            stt("dve", st_[:, 6:7], st_[:, 1:2], -1.0, st_[:, 4:5], ALU.mult, ALU.mult, reads=["st1", "st4"], writes=["st6"])
            act(tile2d, tile2d, AF.Identity, bias=st_[:, 6:7], scale=st_[:, 4:5], reads=[key, "st4", "st6"], writes=[key])
            tt("pool", tile2d, tile2d, lnp[:, gi, :], ALU.mult, reads=[key, "lnp"], writes=[key])
            tt("dve", tile2d, tile2d, lnp[:, gi + 1, :], ALU.add, reads=[key, "lnp"], writes=[key])

        n_tb = int(stop_after[2:]) if str(stop_after).startswith("tb") else 4
        for tb in range(n_tb):
            t0 = tb * 512
            for cc in range(16):
                P.dma("sp", ycS[:, cc, :], ycat_d[cc, :, t0:t0 + 512], reads=[f"ycat{cc}"], writes=[f"ffT{cc}"])
            for db in range(4):
                for ccg in range(4):
                    wi = wctr2[0] % 3
                    wctr2[0] += 1
                    P.dma("pool", wpc[wi], wout_d[ccg * 512:(ccg + 1) * 512, db * 512:(db + 1) * 512].rearrange("(j p) d -> p j d", p=128),
                          writes=[f"wpc{wi}"])
                    for j in range(4):
                        cc = ccg * 4 + j
                        for t4 in range(4):
                            mm(banks[t4][:, :], ycS[:, cc, t4 * 128:(t4 + 1) * 128], wpc[wi][:, j, :], start=(cc == 0), stop=(cc == 15),
                               reads=[f"ffT{cc}", f"wpc{wi}"], writes=[f"B{t4}"])
                for t4 in range(4):
                    xi = xctr[0] % 2
                    xctr[0] += 1
                    P.dma("sp", xres[xi], x_d[t0 + t4 * 128:t0 + (t4 + 1) * 128, db * 512:(db + 1) * 512], writes=[f"xres{xi}"])
                    stt("dve", hB[:, t4, db * 512:(db + 1) * 512], xres[xi], ALPHA, banks[t4][:, :], ALU.mult, ALU.add,
                        reads=[f"xres{xi}", f"B{t4}"], writes=[f"hB{t4}"])
            for t4 in range(4):
                layer_norm(hB[:, t4, :], f"hB{t4}", 0)
            if tb == 0:
                tap("h0", hB[:, 0, :], ["hB0"], [128, D])
            for kc in range(16):
                bi = 4 + kc % 2
                bk = f"B{bi}"
                for t4 in range(4):
                    tr(banks[bi][:, t4 * 128:(t4 + 1) * 128], hB[:, t4, kc * 128:(kc + 1) * 128], ident, reads=[f"hB{t4}", "CT"], writes=[bk])
                if kc % 2 == 0:
                    cp_act(hTs[:, kc, :], banks[bi][:, :], reads=[bk], writes=["hTs"])
                else:
                    cp_dve(hTs[:, kc, :], banks[bi][:, :], reads=[bk], writes=["hTs"])
            for fp in range(22):
                gi = gctr[0] % 2
                gctr[0] += 1
                P.dma("pool", gpc[gi], wg_d[:, fp * 256:(fp + 1) * 256].rearrange("(k p) c -> p k c", p=128), writes=[f"gpc{gi}"])
                P.dma("pool", upc[gi], wu_d[:, fp * 256:(fp + 1) * 256].rearrange("(k p) c -> p k c", p=128), writes=[f"upc{gi}"])
                for j in range(2):
                    fc = fp * 2 + j
                    for kc in range(16):
                        mm(banks[6][:, :], gpc[gi][:, kc, j * 128:(j + 1) * 128], hTs[:, kc, :], start=(kc == 0), stop=(kc == 15),
                           reads=[f"gpc{gi}", "hTs"], writes=["B6"])
                    for kc in range(16):
                        mm(banks[7][:, :], upc[gi][:, kc, j * 128:(j + 1) * 128], hTs[:, kc, :], start=(kc == 0), stop=(kc == 15),
                           reads=[f"upc{gi}", "hTs"], writes=["B7"])
                    si = fc % 2
                    act(sgt[si], banks[6][:, :], AF.Silu, reads=["B6"], writes=[f"sgt{si}"])
                    tt("dve", ffT[:, fc, :], sgt[si], banks[7][:, :], ALU.mult, reads=[f"sgt{si}", "B7"], writes=[f"ffT{fc}"])
            for db in range(4):
                for fg in range(11):
                    wi = wctr2[0] % 3
                    wctr2[0] += 1
                    P.dma("pool", wpc[wi], wd_d[fg * 512:(fg + 1) * 512, db * 512:(db + 1) * 512].rearrange("(j p) d -> p j d", p=128),
                          writes=[f"wpc{wi}"])
                    for j in range(4):
                        fc = fg * 4 + j
                        for t4 in range(4):
                            mm(banks[t4][:, :], ffT[:, fc, t4 * 128:(t4 + 1) * 128], wpc[wi][:, j, :], start=(fc == 0), stop=(fc == 43),
                               reads=[f"ffT{fc}", f"wpc{wi}"], writes=[f"B{t4}"])
                for t4 in range(4):
                    stt("dve", hB[:, t4, db * 512:(db + 1) * 512], hB[:, t4, db * 512:(db + 1) * 512], ALPHA, banks[t4][:, :], ALU.mult, ALU.add,
                        reads=[f"hB{t4}", f"B{t4}"], writes=[f"hB{t4}"])
            for t4 in range(4):
                layer_norm(hB[:, t4, :], f"hB{t4}", 2)
                finals.append(P.dma("sp", out_d[t0 + t4 * 128:t0 + (t4 + 1) * 128, :], hB[:, t4, :], reads=[f"hB{t4}"]))
        P.emit(finals)
        build_program.peak = A.peak
    return nc, tap_out


def _host_inputs(inp):
    f = lambda a: np.ascontiguousarray(np.asarray(a, dtype=np.float32))
    cp = np.zeros((128, NCP), np.float32)
    mu = f(inp["mu_shift"])[0]
    o = lambda off: off - R_OFF
    for hp in range(NHP):
        sl = slice(hp * 128, (hp + 1) * 128)
        cb = hp * 10
        cp[:, cb + 0] = mu[o(R_OFF) + hp * 128: o(R_OFF) + (hp + 1) * 128]
        cp[:, cb + 1] = mu[o(K_OFF) + hp * 128: o(K_OFF) + (hp + 1) * 128]
        cp[:, cb + 2] = mu[o(V_OFF) + hp * 128: o(V_OFF) + (hp + 1) * 128]
        cp[:, cb + 3] = f(inp["w0_f"])[0][sl]
        cp[:, cb + 4] = f(inp["w0_b"])[0][sl]
        cp[:, cb + 5] = f(inp["a0_f"])[0][sl]
        cp[:, cb + 6] = f(inp["a0_b"])[0][sl]
        cp[:, cb + 7] = f(inp["k_k"])[0][sl]
        cp[:, cb + 8] = f(inp["k_a"])[0][sl]
        cp[:, cb + 9] = f(inp["r_k"])[0].reshape(-1)[sl]
    cp[:, 120] = mu[o(G_OFF):o(G_OFF) + 128]
    cp[:96, 121] = mu[o(G_OFF) + 128:o(G_OFF) + 224]
    cp[:, 122] = mu[o(WF_OFF):o(WF_OFF) + 128]
    cp[:, 123] = mu[o(AF_OFF):o(AF_OFF) + 128]
    cs, ss, dn = _host_dft()
    shared = {
        "w_in": f(inp["w_in"])[0], "w_out": f(inp["w_out"])[0], "w_gate": f(inp["w_ffn_gate"])[0],
        "w_up": f(inp["w_ffn_up"])[0], "w_down": f(inp["w_ffn_down"])[0],
        "wupfb": np.ascontiguousarray(np.concatenate([f(inp["w_up_f"])[0], f(inp["w_up_b"])[0]], 0)),
        "aupfb": np.ascontiguousarray(np.concatenate([f(inp["a_up_f"])[0], f(inp["a_up_b"])[0]], 0)),
        "g_up": f(inp["g_up"])[0], "cp": cp,
        "lnx": np.ascontiguousarray(np.stack([f(inp["lnx_g"])[0], f(inp["lnx_b"])[0]], 0)),
        "ln12": np.ascontiguousarray(np.stack([f(inp["ln1_g"])[0], f(inp["ln1_b"])[0], f(inp["ln2_g"])[0], f(inp["ln2_b"])[0]], 0)),
        "consts": _host_consts(), "dftC": cs, "dftS": ss, "dftN": dn,
    }
    return shared


def kernel(**inputs):
    x = np.asarray(inputs["x"], dtype=np.float32)
    shared = _host_inputs(inputs)
    nc, _ = build_program()
    in_maps = []
    for b in range(8):
        m = dict(shared)
        m["x"] = np.ascontiguousarray(x[b])
        m["xT"] = np.ascontiguousarray(x[b].T)
        in_maps.append(m)
    res = run_bass_kernel_spmd(nc, in_maps, core_ids=list(range(8)))
    return np.stack([np.asarray(r["out"], dtype=np.float32) for r in res.results], 0)
```

```python
import contextlib
import numpy as np
import ml_dtypes
import concourse.bass as bass
import concourse.mybir as mybir
from concourse.bass_utils import run_bass_kernel_spmd

F32 = mybir.dt.float32
BF16 = mybir.dt.bfloat16
AF = mybir.ActivationFunctionType
ALU = mybir.AluOpType
AX = mybir.AxisListType

D = 2048
S = 2048
FW = 512
RW = 1536
NH = 24
N = 64
R_OFF = 512
K_OFF = R_OFF + RW
V_OFF = K_OFF + RW
G_OFF = V_OFF + RW
WF_OFF = G_OFF + 224
AF_OFF = WF_OFF + 128
IN_COLS = 5600
DFF = 5632
ALPHA = 2.0 ** 0.25
LN_EPS = 1e-5
GN_EPS = 64e-5
CDEC = float(np.exp(-0.5))
NHP = 12
NCP = 124

ENGS = ("pe", "act", "dve", "pool", "sp")
EPOCH = 12000
NDSEM = 16


class Op:
    __slots__ = ("eng", "idx", "fn", "deps", "is_dma", "needs_inc", "semval", "dma_slot", "dma_val")

    def __init__(self, eng, idx, fn, is_dma):
        self.eng = eng
        self.idx = idx
        self.fn = fn
        self.deps = []
        self.is_dma = is_dma
        self.needs_inc = False
        self.semval = None
        self.dma_slot = None
        self.dma_val = None


class Prog:
    def __init__(self, nc):
        self.nc = nc
        self.ops = {e: [] for e in ENGS}
        self.last_w = {}
        self.readers = {}
        self.ndma = {e: 0 for e in ENGS}
        self.dma_ring = {e: [None] * NDSEM for e in ENGS}
        self.fence_ops = []

    def _add(self, eng, fn, reads, writes, is_dma):
        op = Op(eng, len(self.ops[eng]), fn, is_dma)
        deps = list(self.fence_ops)
        xr = [k for k in reads if len(k) == 2 and k[0] == "B"]
        if xr:
            reads = [k for k in reads if k not in xr]
            writes = list(writes) + xr
        for k in reads:
            w = self.last_w.get(k)
            if w is not None:
                deps.append(w)
        for k in writes:
            w = self.last_w.get(k)
            if w is not None:
                deps.append(w)
            deps.extend(self.readers.get(k, ()))
        if is_dma:
            slot = self.ndma[eng] % NDSEM
            prev = self.dma_ring[eng][slot]
            if prev is not None:
                deps.append(prev)
            op.dma_slot = slot
            op.dma_val = 16 * (self.ndma[eng] // NDSEM + 1)
            self.dma_ring[eng][slot] = op
            self.ndma[eng] += 1
        op.deps = deps
        for k in reads:
            self.readers.setdefault(k, []).append(op)
        for k in writes:
            self.last_w[k] = op
            self.readers[k] = []
        self.ops[eng].append(op)
        return op

    def op(self, eng, fn, reads=(), writes=()):
        return self._add(eng, fn, reads, writes, False)

    def dma(self, eng, out, in_, reads=(), writes=()):
        def fn(e):
            return e.dma_start(out=out, in_=in_)
        return self._add(eng, fn, reads, writes, True)

    def fence(self):
        f = []
        for e in ENGS:
            if self.ops[e]:
                last_c = None
                for o in reversed(self.ops[e]):
                    if not o.is_dma:
                        last_c = o
                        break
                if last_c is not None:
                    f.append(last_c)
            for o in self.dma_ring[e]:
                if o is not None:
                    f.append(o)
        self.fence_ops = f
        self.last_w = {}
        self.readers = {}

    def emit(self, final_wait_ops=()):
        nc = self.nc
        for e in ENGS:
            for op in self.ops[e]:
                nd = []
                seen = set()
                for d in op.deps:
                    if d is op or id(d) in seen:
                        continue
                    seen.add(id(d))
                    if d.is_dma:
                        nd.append(d)
                        continue
                    if d.eng == op.eng and not op.is_dma:
                        if d.eng in ("pe", "sp"):
                            continue
                        if d.idx < op.idx - 2:
                            continue
                    d.needs_inc = True
                    nd.append(d)
                op.deps = nd
        for o in final_wait_ops:
            if not o.is_dma:
                o.needs_inc = True
        nep = {}
        for e in ENGS:
            c = 0
            for op in self.ops[e]:
                if not op.is_dma and op.needs_inc:
                    op.semval = c
                    c += 1
            nep[e] = max(1, -(-c // EPOCH))
        with contextlib.ExitStack() as st:
            csem = {e: [st.enter_context(nc.semaphore(f"c_{e}_{i}")) for i in range(nep[e])] for e in ENGS}
            dsem = {e: [st.enter_context(nc.semaphore(f"d_{e}_{i}")) for i in range(NDSEM)]
                    for e in ENGS if self.ndma[e] > 0}
            block = st.enter_context(nc.Block())

            def run(e, eng):
                waited = {}
                for op in self.ops[e]:
                    for d in op.deps:
                        if d.is_dma:
                            key = ("d", d.eng, d.dma_slot)
                            val = d.dma_val
                            sem = dsem[d.eng][d.dma_slot]
                        else:
                            ep, v = divmod(d.semval, EPOCH)
                            key = ("c", d.eng, ep)
                            val = v + 1
                            sem = csem[d.eng][ep]
                        if waited.get(key, 0) >= val:
                            continue
                        waited[key] = val
                        eng.wait_ge(sem, val)
                    inst = op.fn(eng)
                    if op.is_dma:
                        inst.then_inc(dsem[e][op.dma_slot], 16)
                    elif op.needs_inc:
                        ep, v = divmod(op.semval, EPOCH)
                        inst.then_inc(csem[e][ep], 1)
                if e == "sp":
                    for o in final_wait_ops:
                        if o.is_dma:
                            eng.wait_ge(dsem[o.eng][o.dma_slot], o.dma_val)
                        else:
                            ep, v = divmod(o.semval, EPOCH)
                            eng.wait_ge(csem[o.eng][ep], v + 1)

            @block.tensor
            def _(eng):
                run("pe", eng)

            @block.scalar
            def _(eng):
                run("act", eng)

            @block.vector
            def _(eng):
                run("dve", eng)

            @block.gpsimd
            def _(eng):
                run("pool", eng)

            @block.sync
            def _(eng):
                run("sp", eng)


class Arena:
    def __init__(self, ap, ncols):
        self.ap = ap
        self.n = ncols
        self.top = 0
        self.peak = 0

    def alloc(self, cols, dtype=F32):
        a = self.ap[:, self.top:self.top + cols]
        self.top += cols
        self.peak = max(self.peak, self.top)
        assert self.top <= self.n, f"SBUF arena overflow {self.top} > {self.n}"
        if dtype == BF16:
            a = a.bitcast(BF16)
        return a

    def mark(self):
        return self.top

    def release(self, m):
        self.top = m


C_M4F = 0
C_M4B = 512
C_MTF = 1024
C_MTB = 1152
C_ID = 1280
C_BO = 1408
C_BI = 1536
NCONST = 1538


def _host_consts():
    idx = np.arange(128)
    sf = (idx[:, None] < idx[None, :]).astype(np.float32)
    inf_ = (idx[:, None] <= idx[None, :]).astype(np.float32)
    sb = (idx[:, None] > idx[None, :]).astype(np.float32)
    inb = (idx[:, None] >= idx[None, :]).astype(np.float32)
    c = np.zeros((128, NCONST), np.float32)
    c[:, C_M4F:C_M4F + 512] = np.concatenate([sf, inf_, sf, inf_], 1)
    c[:, C_M4B:C_M4B + 512] = np.concatenate([sb, inb, sb, inb], 1)
    c[:, C_MTF:C_MTF + 128] = sf.T
    c[:, C_MTB:C_MTB + 128] = sb.T
    c[:, C_ID:C_ID + 128] = np.eye(128, dtype=np.float32)
    bo = np.zeros((128, 128), np.float32)
    bo[:64, :64] = 1
    bo[64:, 64:] = 1
    c[:, C_BO:C_BO + 128] = bo
    c[:64, C_BI] = 1
    c[64:, C_BI + 1] = 1
    return c


def _host_dft():
    t = np.arange(S, dtype=np.int64)
    m = (t[:, None] * t[None, :]) % S
    ang = 2.0 * np.pi * m.astype(np.float64) / S
    cs = np.cos(ang).astype(ml_dtypes.bfloat16)
    ss = np.sin(ang).astype(ml_dtypes.bfloat16)
    n = np.arange(128, dtype=np.int64)
    mn = (n[:, None] * n[None, :]) % 128
    an = 2.0 * np.pi * mn.astype(np.float64) / 128
    norm = 1.0 / np.sqrt(S * 128.0)
    dn = np.concatenate([np.cos(an) * norm, -np.sin(an) * norm], 1).astype(ml_dtypes.bfloat16)
    return cs, ss, dn


class _Stop(Exception):
    pass


def build_program(taps=None, stop_after=None):
    try:
        return _build_program(taps, stop_after)
    except _Stop as e:
        return e.args[0]


def _build_program(taps=None, stop_after=None):
    taps = taps or {}
    nc = bass.Bass("TRN2", target_bir_lowering=False)
    din = lambda n, s, d=F32: nc.dram_tensor(n, list(s), d, kind="ExternalInput").ap()
    xT_d = din("xT", [D, S])
    x_d = din("x", [S, D])
    win_d = din("w_in", [D, IN_COLS])
    wout_d = din("w_out", [D, D])
    wg_d = din("w_gate", [D, DFF])
    wu_d = din("w_up", [D, DFF])
    wd_d = din("w_down", [DFF, D])
    wupfb_d = din("wupfb", [128, RW])
    aupfb_d = din("aupfb", [128, RW])
    gup_d = din("g_up", [224, RW])
    cp_d = din("cp", [128, NCP])
    lnx_d = din("lnx", [2, RW])
    ln12_d = din("ln12", [4, D])
    consts_d = din("consts", [128, NCONST])
    dftc_d = din("dftC", [S, S], BF16)
    dfts_d = din("dftS", [S, S], BF16)
    dftn_d = din("dftN", [128, 256], BF16)
    out_d = nc.dram_tensor("out", [S, D], F32, kind="ExternalOutput").ap()
    ycat_d = nc.dram_tensor("ycat_scr", [16, 128, S], BF16).ap()
    tap_out = {}

    with contextlib.ExitStack() as top:
        ARENA_COLS = 52600
        arena_t = top.enter_context(nc.sbuf_tensor("arena", [128, ARENA_COLS], F32))
        A = Arena(arena_t, ARENA_COLS)
        banks = [top.enter_context(nc.psum_tensor(f"bank{i}", [128, 512], F32)) for i in range(8)]
        P = Prog(nc)
        finals = []

        def tap(name, ap, keys, shape, dtype=F32):
            if name not in taps:
                return
            t = nc.dram_tensor("tap_" + name, list(shape), dtype, kind="ExternalOutput").ap()
            tap_out[name] = t
            sel = taps[name]
            finals.append(P.dma("sp", sel(t) if callable(sel) else t, ap, reads=keys))

        def cut(name):
            if stop_after == name:
                for e_ in ("pe", "act", "dve", "pool"):
                    for o_ in reversed(P.ops[e_]):
                        if not o_.is_dma:
                            finals.append(o_)
                            break
                P.emit(finals)
                raise _Stop((nc, tap_out))

        def mm(out, lhsT, rhs, start=True, stop=True, reads=(), writes=()):
            return P.op("pe", lambda e: e.matmul(out, lhsT, rhs, start=start, stop=stop), reads, writes)

        def tr(out, in_, ident, reads=(), writes=()):
            return P.op("pe", lambda e: e.transpose(out, in_, ident), reads, writes)

        def act(out, in_, func, bias=None, scale=1.0, reads=(), writes=(), accum_out=None):
            kw = {}
            if bias is not None:
                kw["bias"] = bias
            if accum_out is not None:
                kw["accum_out"] = accum_out
            return P.op("act", lambda e: e.activation(out=out, in_=in_, func=func, scale=scale, **kw), reads, writes)

        def tt(eng, out, in0, in1, op, reads=(), writes=()):
            return P.op(eng, lambda e: e.tensor_tensor(out=out, in0=in0, in1=in1, op=op), reads, writes)

        def ts(eng, out, in0, s1, op0, s2=None, op1=None, reads=(), writes=()):
            if op1 is None:
                return P.op(eng, lambda e: e.tensor_scalar(out=out, in0=in0, scalar1=s1, scalar2=None, op0=op0), reads, writes)
            return P.op(eng, lambda e: e.tensor_scalar(out=out, in0=in0, scalar1=s1, scalar2=s2, op0=op0, op1=op1), reads, writes)

        def stt(eng, out, in0, scalar, in1, op0, op1, reads=(), writes=()):
            return P.op(eng, lambda e: e.scalar_tensor_tensor(out=out, in0=in0, scalar=scalar, in1=in1, op0=op0, op1=op1), reads, writes)

        def cp_act(out, in_, reads=(), writes=()):
            return P.op("act", lambda e: e.copy(out, in_), reads, writes)

        def cp_dve(out, in_, reads=(), writes=()):
            return P.op("dve", lambda e: e.tensor_copy(out, in_), reads, writes)

        CT = A.alloc(NCONST)
        CPt = A.alloc(NCP)
        OMt = A.alloc(NCP)
        HMt = A.alloc(NCP)
        ident = CT[:, C_ID:C_ID + 128]
        blockones = CT[:, C_BO:C_BO + 128]
        blockind = CT[:, C_BI:C_BI + 2]
        P.dma("sp", CT, consts_d, writes=["CT"])
        P.dma("sp", CPt, cp_d, writes=["CP"])
        ts("dve", OMt, CPt, -1.0, ALU.mult, 1.0, ALU.add, reads=["CP"], writes=["OM"])
        ts("dve", HMt, CPt, 0.5, ALU.mult, reads=["CP"], writes=["HM"])
        ones128 = A.alloc(128)
        P.op("pool", lambda e: e.memset(ones128, 1.0), writes=["ones"])
        tiny_col = A.alloc(1)
        P.op("pool", lambda e: e.memset(tiny_col, 1e-24), writes=["tiny"])

        mixer_mark = A.mark()
        xTs = A.alloc(16 * S // 2, BF16).rearrange("p (k t) -> p k t", k=16)
        for kc in range(16):
            P.dma("pool", xTs[:, kc, :], xT_d[kc * 128:(kc + 1) * 128, :], writes=[f"xT{kc}"])
        XK = [f"xT{kc}" for kc in range(16)]

        NW = 2
        Wb = [A.alloc(16 * 128 // 2, BF16).rearrange("p (k c) -> p k c", k=16) for _ in range(NW)]
        wctr = [0]
        pbank = [0]

        def prefetch_W(c0, ncol):
            wi = wctr[0] % NW
            wctr[0] += 1
            W = Wb[wi]
            wk = f"W{wi}"
            P.dma("pool", W[:, :, 0:ncol], win_d[:, c0:c0 + ncol].rearrange("(k p) c -> p k c", p=128), writes=[wk])
            return (W, wk)

        def proj_cols(c0, ncol, evac, pre=None):
            W, wk = pre if pre is not None else prefetch_W(c0, ncol)
            for tb in range(4):
                bi = pbank[0] % 2
                pbank[0] += 1
                bk = f"B{bi}"
                ps = banks[bi][0:ncol, :]
                for kc in range(16):
                    mm(ps, W[:, kc, 0:ncol], xTs[:, kc, tb * 512:(tb + 1) * 512], start=(kc == 0), stop=(kc == 15),
                       reads=[wk, f"xT{kc}"], writes=[bk])
                evac(tb, ps, bk)

        def token_shift(raw, rk, npart, col, tmp, tk):
            tt("dve", tmp[0:npart, :], raw[0:npart, 0:S], raw[0:npart, 2:S + 2], ALU.add, reads=[rk], writes=[tk])
            act(raw[0:npart, 1:S + 1], raw[0:npart, 1:S + 1], AF.Identity, scale=OMt[0:npart, col:col + 1], reads=[rk, "OM"], writes=[rk])
            stt("dve", raw[0:npart, 1:S + 1], tmp[0:npart, :], HMt[0:npart, col:col + 1], raw[0:npart, 1:S + 1],
                ALU.mult, ALU.add, reads=[rk, tk, "HM"], writes=[rk])

        sg0 = A.alloc(S // 2, BF16)
        sg1 = A.alloc(S // 2, BF16)
        twfb = A.alloc(S // 2, BF16)
        alfb = A.alloc(S // 2, BF16)

        a2_mark = A.mark()
        uT = A.alloc(S // 2, BF16)
        Acs = [A.alloc(16 * 256 // 2, BF16).rearrange("p (c n) -> p c n", c=16) for _ in range(4)]
        dftN = A.alloc(128, BF16)
        P.dma("sp", dftN, dftn_d, writes=["dftN"])
        yfo = [A.alloc(S // 2, BF16) for _ in range(4)]
        for g in range(4):
            def evac_u(tb, ps, bk):
                cp_act(uT[:, tb * 512:(tb + 1) * 512], ps, reads=[bk], writes=["uT"])
            proj_cols(g * 128, 128, evac_u)
            for tcp in range(8):
                bi = pbank[0] % 2
                pbank[0] += 1
                bk = f"B{bi}"
                for j in range(2):
                    tc = tcp * 2 + j
                    mm(banks[bi][:, j * 256:(j + 1) * 256], uT[:, tc * 128:(tc + 1) * 128], dftN,
                       reads=["uT", "dftN"], writes=[bk])
                cp_dve(Acs[g][:, tcp * 2:tcp * 2 + 2, :], banks[bi][:, :].rearrange("p (c n) -> p c n", c=2),
                       reads=[bk], writes=[f"Acs{g}"])
        tap("acs0", Acs[0], ["Acs0"], [128, 16, 256], BF16)
        dpc = [A.alloc(4 * 512 // 2, BF16).rearrange("p (j t) -> p j t", j=4) for _ in range(2)]
        dps = [A.alloc(4 * 512 // 2, BF16).rearrange("p (j t) -> p j t", j=4) for _ in range(2)]
        pc = 0
        for tpb in range(4):
            for q in range(4):
                bsel = pc % 2
                pc += 1
                P.dma("sp", dpc[bsel], dftc_d[q * 512:(q + 1) * 512, tpb * 512:(tpb + 1) * 512].rearrange("(j p) t -> p j t", p=128),
                      writes=[f"dpc{bsel}"])
                P.dma("sp", dps[bsel], dfts_d[q * 512:(q + 1) * 512, tpb * 512:(tpb + 1) * 512].rearrange("(j p) t -> p j t", p=128),
                      writes=[f"dps{bsel}"])
                for g in range(4):
                    for j in range(4):
                        tc = q * 4 + j
                        mm(banks[2 + g][:, :], Acs[g][:, tc, 0:128], dpc[bsel][:, j, :], start=(tc == 0), stop=False,
                           reads=[f"Acs{g}", f"dpc{bsel}"], writes=[f"B{2 + g}"])
                        mm(banks[2 + g][:, :], Acs[g][:, tc, 128:256], dps[bsel][:, j, :], start=False, stop=(tc == 15),
                           reads=[f"Acs{g}", f"dps{bsel}"], writes=[f"B{2 + g}"])
            for g in range(4):
                if g % 2 == 0:
                    cp_act(yfo[g][:, tpb * 512:(tpb + 1) * 512], banks[2 + g][:, :], reads=[f"B{2 + g}"], writes=[f"yfo{g}"])
                else:
                    cp_dve(yfo[g][:, tpb * 512:(tpb + 1) * 512], banks[2 + g][:, :], reads=[f"B{2 + g}"], writes=[f"yfo{g}"])
        for g in range(4):
            P.dma("sp", ycat_d[g], yfo[g], reads=[f"yfo{g}"], writes=[f"ycat{g}"])
        tap("yfo0", yfo[0], ["yfo0"], [128, S], BF16)

        lraw = A.alloc(S + 2)
        ltmp = A.alloc(S)
        P.op("pool", lambda e: e.memset(lraw[:, 0:1], 0.0), writes=["lraw"])
        P.op("pool", lambda e: e.memset(lraw[:, S + 1:S + 2], 0.0), writes=["lraw"])
        lora_specs = [(G_OFF, 128, sg0, "sg0", AF.Sigmoid, 120), (G_OFF + 128, 96, sg1, "sg1", AF.Sigmoid, 121),
                      (WF_OFF, 128, twfb, "twfb", AF.Tanh, 122), (AF_OFF, 128, alfb, "alfb", AF.Identity, 123)]
        for (c0, ncol, dst, dk, fn, col) in lora_specs:
            def evac_l(tb, ps, bk, ncol=ncol):
                cp_act(lraw[0:ncol, 1 + tb * 512:1 + (tb + 1) * 512], ps, reads=[bk], writes=["lraw"])
            proj_cols(c0, ncol, evac_l)
            token_shift(lraw, "lraw", ncol, col, ltmp, "ltmp")
            act(dst[0:ncol, :], lraw[0:ncol, 1:S + 1], fn, reads=["lraw"], writes=[dk])
        tap("twfb", twfb, ["twfb"], [128, S], BF16)
        tap("sg1", sg1, ["sg1"], [128, S], BF16)

        if stop_after == "A1":
            P.emit(finals)
            return nc, tap_out

        P.fence()
        A.release(a2_mark)
        rawr = A.alloc(S + 2)
        rawk = A.alloc(S + 2)
        rawv = A.alloc(S + 2)
        for rw_, k_ in ((rawr, "rawr"), (rawk, "rawk"), (rawv, "rawv")):
            P.op("pool", lambda e, rw_=rw_: e.memset(rw_[:, 0:1], 0.0), writes=[k_])
            P.op("pool", lambda e, rw_=rw_: e.memset(rw_[:, S + 1:S + 2], 0.0), writes=[k_])
        VT = A.alloc(S // 2, BF16).rearrange("p (c n) -> p c n", c=16)
        YF = A.alloc(S).rearrange("p (c n) -> p c n", c=16)
        YFflat = YF.rearrange("p c n -> p (c n)")
        Sd = [A.alloc(32).rearrange("p (c h) -> p c h", c=16) for _ in range(2)]
        WCt = A.alloc(32).rearrange("p (d c) -> p d c", d=2)
        lw_hp = A.alloc(4 * 64, BF16).rearrange("p (w c) -> p w c", w=4)
        lnxb = A.alloc(256).rearrange("p (w c) -> p w c", w=2)
        ych = rawk[:, 2:2 + S // 2].bitcast(BF16)
        AR = [[A.alloc(512, BF16).rearrange("p (c n) -> p c n", c=4) for _ in range(2)] for _ in range(2)]
        BKb = [[A.alloc(512, BF16).rearrange("p (c n) -> p c n", c=4) for _ in range(2)] for _ in range(2)]
        BK = [A.alloc(1024).rearrange("p (c n) -> p c n", c=4) for _ in range(2)]
        BKT = [[A.alloc(512, BF16).rearrange("p (c n) -> p c n", c=4) for _ in range(2)] for _ in range(2)]
        NT = 9
        Tm = [A.alloc(512) for _ in range(NT)]
        SC = [[A.alloc(256, BF16) for _ in range(2)] for _ in range(4)]
        TTt = [[A.alloc(64, BF16) for _ in range(2)] for _ in range(4)]
        PXT = [[A.alloc(192, BF16) for _ in range(2)] for _ in range(4)]
        identb = A.alloc(64, BF16)
        cp_act(identb, ident, reads=["CT"], writes=["identb"])
        Zs_all = A.alloc(128, BF16)
        Zs = [Zs_all[:, ch_ * 64:(ch_ + 1) * 64] for ch_ in range(4)]
        Us_all = A.alloc(128, BF16)
        Us = [Us_all[:, ch_ * 64:(ch_ + 1) * 64] for ch_ in range(4)]
        Hb = [A.alloc(32, BF16) for _ in range(2)]
        Hst = [A.alloc(64) for _ in range(2)]
        gS = Tm[0]
        small = A.alloc(32 * 4)
        s_sum, s_nm, s_var, s_ss = (small[:, i * 32:(i + 1) * 32] for i in range(4))

        n_hp = int(stop_after[2:]) if str(stop_after).startswith("hp") else NHP
        nextW = []
        for hp in range(n_hp):
            cb = hp * 10
            col = lambda j: CPt[:, cb + j:cb + j + 1]
            hpc = slice(hp * 128, (hp + 1) * 128)
            P.dma("pool", lw_hp[:, 0, :], wupfb_d[:, hpc], writes=["lw0"])
            P.dma("pool", lw_hp[:, 1, :], aupfb_d[:, hpc], writes=["lw1"])
            P.dma("pool", lw_hp[:, 2, :], gup_d[0:128, hpc], writes=["lw2"])
            P.dma("pool", lw_hp[0:96, 3, :], gup_d[128:224, hpc], writes=["lw3"])
            P.dma("sp", lnxb, lnx_d[:, hpc].partition_broadcast(128), writes=["lnxb"])
            for j, (rw_, rk_) in enumerate(((rawr, "rawr"), (rawk, "rawk"), (rawv, "rawv"))):
                def evac_r(tb, ps, bk, rw_=rw_, rk_=rk_):
                    cp_act(rw_[:, 1 + tb * 512:1 + (tb + 1) * 512], ps, reads=[bk], writes=[rk_])
                proj_cols(R_OFF + j * RW + hp * 128, 128, evac_r, pre=(nextW.pop(0) if nextW else None))
                token_shift(rw_, rk_, 128, cb + j, YFflat, "YF")
            r_fm = rawr[:, 1:S + 1]
            k_fm = rawk[:, 1:S + 1]
            v_fm = rawv[:, 1:S + 1]
            if hp == 0:
                tap("r0", r_fm, ["rawr"], [128, S])
                tap("v0", v_fm, ["rawv"], [128, S])
            for cq in range(4):
                bi = pbank[0] % 2
                pbank[0] += 1
                bk = f"B{bi}"
                for j in range(4):
                    c = cq * 4 + j
                    tr(banks[bi][:, j * 128:(j + 1) * 128], v_fm[:, c * 128:(c + 1) * 128], ident, reads=["rawv", "CT"], writes=[bk])
                cp_act(VT[:, cq * 4:cq * 4 + 4, :], banks[bi][:, :].rearrange("p (c n) -> p c n", c=4), reads=[bk], writes=["VT"])

            cut("c_proj")
            P.op("pool", lambda e: e.memset(Hst[0], 0.0), writes=["H0"])
            P.op("pool", lambda e: e.memset(Hst[1], 0.0), writes=["H1"])

            def precompute(it, dr):
                pb = it % 2
                tb = it if dr == 0 else 3 - it
                bsl = slice(tb * 512, (tb + 1) * 512)
                dsl = slice(dr * 64, (dr + 1) * 64)
                sgm, cs, csx, e1, e2, e3, asg, kkr, sq = Tm
                t1 = csx
                k_sgm, k_cs, k_csx, k_e1, k_e2, k_e3, k_asg, k_kkr, k_sq = [f"T{i}" for i in range(9)]
                k_t1 = k_csx
                ARk, BKk, BKTk = f"AR{dr}_{pb}", f"BK{dr}", f"BKT{dr}_{pb}"
                v4 = lambda a: a.rearrange("p (c t) -> p c t", c=4)
                mm(banks[0][:, :], lw_hp[dsl, 0, :], twfb[dsl, bsl], reads=["lw0", "twfb"], writes=["B0"])
                act(sgm, banks[0][:, :], AF.Sigmoid, bias=col(3 + dr), reads=["B0", "CP"], writes=[k_sgm])
                mm(banks[0][:, :], lw_hp[dsl, 1, :], alfb[dsl, bsl], reads=["lw1", "alfb"], writes=["B0"])
                act(asg, banks[0][:, :], AF.Sigmoid, bias=col(5 + dr), reads=["B0", "CP"], writes=[k_asg])
                ts("dve", kkr, k_fm[:, bsl], col(7), ALU.mult, reads=["rawk", "CP"], writes=[k_kkr])
                yield
                tt("pool", sq, kkr, kkr, ALU.mult, reads=[k_kkr], writes=[k_sq])
                for c in range(4):
                    P.op("dve", lambda e, c=c: e.tensor_tensor_scan(cs[:, c * 128:(c + 1) * 128], ones128,
                                                                    sgm[:, c * 128:(c + 1) * 128], 0.0, ALU.mult, ALU.add),
                         reads=[k_sgm, "ones"], writes=[k_cs])
                    if c == 1:
                        yield
                yield
                mm(banks[0][:, :], blockones, sq, reads=["CT", k_sq], writes=["B0"])
                tt("pool", csx, cs, sgm, ALU.subtract, reads=[k_cs, k_sgm], writes=[k_csx])
                act(WCt[:, dr, tb * 4:tb * 4 + 4], cs[:, 127:512:128], AF.Exp, scale=-CDEC, reads=[k_cs], writes=["WC"])
                yield
                act(sq, banks[0][:, :], AF.Ln, bias=tiny_col, reads=["B0", "tiny"], writes=[k_sq])
                act(sq, sq, AF.Exp, scale=-0.5, reads=[k_sq], writes=[k_sq])
                yield
                if dr == 0:
                    act(e1, cs, AF.Exp, scale=-CDEC, reads=[k_cs], writes=[k_e1])
                    act(e2, csx, AF.Exp, scale=-CDEC, reads=[k_csx], writes=[k_e2])
                    act(e3, cs, AF.Exp, scale=CDEC, reads=[k_cs], writes=[k_e3])
                else:
                    act(e1, csx, AF.Exp, scale=CDEC, reads=[k_csx], writes=[k_e1])
                    act(e2, cs, AF.Exp, scale=CDEC, reads=[k_cs], writes=[k_e2])
                    act(e3, csx, AF.Exp, scale=-CDEC, reads=[k_csx], writes=[k_e3])
                tt("pool", kkr, kkr, sq, ALU.mult, reads=[k_kkr, k_sq], writes=[k_kkr])
                yield
                ts("dve", t1, asg, col(8), ALU.mult, OMt[:, cb + 8:cb + 9], ALU.add, reads=[k_asg, "CP", "OM", k_csx], writes=[k_t1])
                yield
                tt("pool", t1, t1, k_fm[:, bsl], ALU.mult, reads=[k_t1, "rawk"], writes=[k_t1])
                tt("dve", AR[dr][pb][:, :, 128:256], v4(r_fm[:, bsl]), v4(e1), ALU.mult, reads=["rawr", k_e1], writes=[ARk])
                yield
                tt("pool", asg, asg, kkr, ALU.mult, reads=[k_asg, k_kkr], writes=[k_asg])
                stt("dve", AR[dr][pb][:, :, 0:128], v4(kkr), -1.0, v4(e2), ALU.mult, ALU.mult, reads=[k_kkr, k_e2], writes=[ARk])
                yield
                tt("pool", BK[dr][:, :, 0:128], v4(asg), v4(e3), ALU.mult, reads=[k_asg, k_e3], writes=[BKk])
                stt("dve", e1, r_fm[:, bsl], col(9), t1, ALU.mult, ALU.mult, reads=["rawr", "CP", k_t1, k_e1, ARk], writes=[k_e1])
                yield
                tt("pool", BK[dr][:, :, 128:256], v4(t1), v4(e3), ALU.mult, reads=[k_t1, k_e3], writes=[BKk])
                for c in range(4):
                    mm(banks[0][:, c * 2:c * 2 + 2], e1[:, c * 128:(c + 1) * 128], blockind, reads=[k_e1, "CT"], writes=["B0"])
                cp_act(Sd[dr][:, tb * 4:tb * 4 + 4, :], banks[0][:, 0:8].rearrange("p (c h) -> p c h", c=4), reads=["B0"], writes=[f"Sd{dr}"])
                yield
                cp_act(BKb[dr][pb], BK[dr], reads=[BKk], writes=[f"BKb{dr}_{pb}"])
                for cpair in range(2):
                    bi = 0
                    bk = f"B{bi}"
                    for j in range(2):
                        c = cpair * 2 + j
                        tr(banks[bi][:, j * 256:j * 256 + 128], BK[dr][:, c, 0:128], ident, reads=[BKk, "CT"], writes=[bk])
                        tr(banks[bi][:, j * 256 + 128:j * 256 + 256], BK[dr][:, c, 128:256], ident, reads=[BKk, "CT"], writes=[bk])
                    cp_act(BKT[dr][pb][:, cpair * 2:cpair * 2 + 2, :], banks[bi][:, :].rearrange("p (c n) -> p c n", c=2), reads=[bk], writes=[BKTk])
                    yield
                if hp == 0 and it == 0:
                    tap(f"AR{dr}", AR[dr][pb], [ARk], [128, 4, 256], BF16)
                    tap(f"BK{dr}", BK[dr], [BKk], [128, 4, 256])
                    tap(f"BKT{dr}", BKT[dr][pb], [BKTk], [128, 4, 256], BF16)

            def chains_of(it, cl):
                chains = []
                for dr in range(2):
                    tb = it if dr == 0 else 3 - it
                    lc = cl if dr == 0 else 3 - cl
                    chunk = tb * 4 + lc
                    for h in range(2):
                        chains.append((dr, h, lc, chunk, dr * 2 + h))
                return chains

            def S_stages(it, cl):
                chains = chains_of(it, cl)
                pb = it % 2
                par = (it * 4 + cl) % 2
                st = []

                def scores():
                    for (dr, h, lc, chunk, ch) in chains:
                        hs = slice(h * 64, (h + 1) * 64)
                        ARc = AR[dr][pb][hs, lc, :]
                        BKc = BKb[dr][pb][hs, lc, :]
                        ARk, BKk = f"AR{dr}_{pb}", f"BKb{dr}_{pb}"
                        Qb = banks[2 + ch]
                        qk = f"B{2 + ch}"
                        sck = f"SC{ch}_{par}"
                        sbank = banks[6] if ch % 2 == 0 else banks[1]
                        sbk = "B6" if ch % 2 == 0 else "B1"
                        mm(sbank[:, 0:256], BKc[:, 0:128], ARc, reads=[BKk, ARk], writes=[sbk])
                        mm(sbank[:, 256:512], BKc[:, 128:256], ARc, reads=[BKk, ARk], writes=[sbk])
                        mm(Qb[:, 384:512], ARc[:, 0:128], BKc[:, 0:128], reads=[BKk, ARk], writes=[qk])
                        m4 = CT[:, C_M4F:C_M4F + 512] if dr == 0 else CT[:, C_M4B:C_M4B + 512]
                        mt = CT[:, C_MTF:C_MTF + 128] if dr == 0 else CT[:, C_MTB:C_MTB + 128]
                        tt("dve", SC[ch][par], sbank[:, :], m4, ALU.mult, reads=[sbk, "CT"], writes=[sck])
                        px = PXT[ch][0]
                        pk = f"PX{ch}_0"
                        tt("dve", px[:, 256:384], Qb[:, 384:512], mt, ALU.mult, reads=[qk, "CT"], writes=[pk])
                st.append(scores)

                def level(lvl):
                    def f():
                        for (dr, h, lc, chunk, ch) in chains:
                            Qb = banks[2 + ch]
                            qk = f"B{2 + ch}"
                            src = PXT[ch][lvl % 2]
                            dst = PXT[ch][(lvl + 1) % 2]
                            sk = f"PX{ch}_{lvl % 2}"
                            dk = f"PX{ch}_{(lvl + 1) % 2}"
                            if lvl == 0:
                                p0 = SC[ch][par][:, 0:128]
                                sck = f"SC{ch}_{par}"
                                mm(Qb[:, 256:384], p0, src[:, 256:384], reads=[sk, sck], writes=[qk])
                                mm(Qb[:, 0:128], src[:, 256:384], p0, reads=[sk, sck], writes=[qk])
                                mm(Qb[:, 128:256], identb, p0, start=True, stop=False, reads=[sck, "identb"], writes=[qk])
                                mm(Qb[:, 128:256], identb, identb, start=False, stop=True, reads=["identb"], writes=[qk])
                                if ch % 2 == 0:
                                    cp_act(dst, Qb[:, 0:384], reads=[qk], writes=[dk])
                                else:
                                    cp_dve(dst, Qb[:, 0:384], reads=[qk], writes=[dk])
                                continue
                            if lvl < 6:
                                mm(Qb[:, 256:384], src[:, 0:128], src[:, 256:384], reads=[sk], writes=[qk])
                                mm(Qb[:, 0:256], src[:, 256:384], src[:, 0:256], start=True, stop=False, reads=[sk], writes=[qk])
                            else:
                                mm(Qb[:, 128:256], src[:, 256:384], src[:, 128:256], start=True, stop=False, reads=[sk], writes=[qk])
                            mm(Qb[:, 128:256], identb, src[:, 128:256], start=False, stop=True, reads=[sk, "identb"], writes=[qk])
                            if lvl < 6:
                                if ch % 2 == 0:
                                    cp_act(dst, Qb[:, 0:384], reads=[qk], writes=[dk])
                                else:
                                    cp_dve(dst, Qb[:, 0:384], reads=[qk], writes=[dk])
                            else:
                                if ch % 2 == 0:
                                    cp_act(TTt[ch][par], Qb[:, 128:256], reads=[qk], writes=[f"TT{ch}_{par}"])
                                else:
                                    cp_dve(TTt[ch][par], Qb[:, 128:256], reads=[qk], writes=[f"TT{ch}_{par}"])
                    return f
                for lvl in range(7):
                    st.append(level(lvl))
                return st

            def R_stages(it, cl):
                chains = chains_of(it, cl)
                pb = it % 2
                par = (it * 4 + cl) % 2
                st = []

                import os
                KM = "r0,r2"

                def r0():
                    if "r0" not in KM:
                        for (dr, h, lc, chunk, ch) in chains:
                            hs = slice(h * 64, (h + 1) * 64)
                            wci = chunk if dr == 1 else max(chunk - 1, 0)
                            ts("pool", Hb[dr][hs, :], Hst[dr][hs, :], WCt[hs, dr, wci:wci + 1], ALU.mult, reads=[f"H{dr}", "WC"], writes=[f"Hb{dr}_{h}"])
                        return
                    for dr in range(2):
                        chunk = [c_ for (d_, h_, l_, c_, ch_) in chains if d_ == dr][0]
                        wci = chunk if dr == 1 else max(chunk - 1, 0)
                        act(Hb[dr], Hst[dr], AF.Copy, scale=WCt[:, dr, wci:wci + 1], reads=[f"H{dr}", "WC"], writes=[f"Hb{dr}_0", f"Hb{dr}_1"])

                def r1():
                    for (dr, h, lc, chunk, ch) in chains:
                        hs = slice(h * 64, (h + 1) * 64)
                        vsl = VT[:, chunk, h * 64:(h + 1) * 64]
                        za = banks[7][:, ch * 128:ch * 128 + 64]
                        mm(za, AR[dr][pb][hs, lc, 0:128], Hb[dr][hs, :], start=True, stop=False, reads=[f"AR{dr}_{pb}", f"Hb{dr}_{h}"], writes=["B7"])
                        mm(za, SC[ch][par][:, 256:384], vsl, start=False, stop=True, reads=[f"SC{ch}_{par}", "VT"], writes=["B7"])

                def r2():
                    if "r2" not in KM:
                        for (dr, h, lc, chunk, ch) in chains:
                            cp_act(Zs[ch], banks[7][:, ch * 128:ch * 128 + 64], reads=["B7"], writes=[f"Zs{ch}"])
                        return
                    cp_dve(Zs_all.rearrange("p (c n) -> p c n", c=4), banks[7][:, :].rearrange("p (c n) -> p c n", c=4)[:, :, 0:64],
                           reads=["B7"], writes=[f"Zs{c_}" for c_ in range(4)])

                def r3():
                    for (dr, h, lc, chunk, ch) in chains:
                        mm(banks[7][:, ch * 128 + 64:ch * 128 + 128], TTt[ch][par], Zs[ch], reads=[f"TT{ch}_{par}", f"Zs{ch}"], writes=["B7"])

                def r4():
                    if "r2" not in KM:
                        for (dr, h, lc, chunk, ch) in chains:
                            cp_act(Us[ch], banks[7][:, ch * 128 + 64:ch * 128 + 128], reads=["B7"], writes=[f"Us{ch}"])
                        return
                    cp_act(Us_all.rearrange("p (c n) -> p c n", c=4), banks[7][:, :].rearrange("p (c n) -> p c n", c=4)[:, :, 64:128],
                           reads=["B7"], writes=[f"Us{c_}" for c_ in range(4)])

                def r5():
                    for (dr, h, lc, chunk, ch) in chains:
                        hs = slice(h * 64, (h + 1) * 64)
                        vsl = VT[:, chunk, h * 64:(h + 1) * 64]
                        ya = banks[7][:, ch * 128:ch * 128 + 64]
                        hb = banks[7][:, ch * 128 + 64:ch * 128 + 128]
                        mm(ya, AR[dr][pb][hs, lc, 128:256], Hb[dr][hs, :], start=True, stop=False, reads=[f"AR{dr}_{pb}", f"Hb{dr}_{h}"], writes=["B7"])
                        mm(ya, SC[ch][par][:, 128:256], Us[ch], start=False, stop=False, reads=[f"SC{ch}_{par}", f"Us{ch}"], writes=["B7"])
                        mm(ya, SC[ch][par][:, 384:512], vsl, start=False, stop=True, reads=[f"SC{ch}_{par}", "VT"], writes=["B7"])
                        mm(hb, BKT[dr][pb][:, lc, 0:128], Us[ch], start=True, stop=False, reads=[f"BKT{dr}_{pb}", f"Us{ch}"], writes=["B7"])
                        mm(hb, BKT[dr][pb][:, lc, 128:256], vsl, start=False, stop=True, reads=[f"BKT{dr}_{pb}", "VT"], writes=["B7"])

                def r6():
                    if "r6" not in KM:
                        for (dr, h, lc, chunk, ch) in chains:
                            ya = banks[7][:, ch * 128:ch * 128 + 64]
                            yk = f"YF{chunk}_{h}"
                            if it < 2:
                                cp_act(YF[:, chunk, h * 64:(h + 1) * 64], ya, reads=["B7", "YF"], writes=[yk])
                            else:
                                tt("dve", YF[:, chunk, h * 64:(h + 1) * 64], YF[:, chunk, h * 64:(h + 1) * 64], ya, ALU.add, reads=["B7", yk], writes=[yk])
                    for dr in (range(2) if "r6" in KM else []):
                        chunk = [c_ for (d_, h_, l_, c_, ch_) in chains if d_ == dr][0]
                        src = banks[7][:, dr * 256:(dr + 1) * 256].rearrange("p (c n) -> p c n", c=2)[:, :, 0:64]
                        dst = YF[:, chunk, :].rearrange("p (c n) -> p c n", c=2)
                        yk2 = [f"YF{chunk}_0", f"YF{chunk}_1"]
                        if it < 2:
                            cp_act(dst, src, reads=["B7", "YF"], writes=yk2)
                        else:
                            tt("dve", dst, dst, src, ALU.add, reads=["B7"] + yk2, writes=yk2)
                    for (dr, h, lc, chunk, ch) in chains:
                        hs = slice(h * 64, (h + 1) * 64)
                        hb = banks[7][hs, ch * 128 + 64:ch * 128 + 128]
                        wci = chunk if dr == 1 else max(chunk - 1, 0)
                        stt("dve", Hst[dr][hs, :], Hst[dr][hs, :], WCt[hs, dr, wci:wci + 1], hb, ALU.mult, ALU.add,
                            reads=[f"H{dr}", "WC", "B7", f"Hb{dr}_{h}"], writes=[f"H{dr}"])
                return [r0, r1, r2, r3, r4, r5, r6]

            for g_ in (precompute(0, 0), precompute(0, 1)):
                for _ in g_:
                    pass
            for f in S_stages(0, 0):
                f()
            for it in range(4):
                gens = [precompute(it + 1, 0), precompute(it + 1, 1)] if it < 3 else []
                for cl in range(4):
                    nxt = (it, cl + 1) if cl < 3 else ((it + 1, 0) if it < 3 else None)
                    if cl == 3:
                        for g_ in gens:
                            for _ in g_:
                                pass
                        gens = []
                    if it == 2 and cl == 0 and hp + 1 < n_hp:
                        nextW.append(prefetch_W(R_OFF + 0 * RW + (hp + 1) * 128, 128))
                        nextW.append(prefetch_W(R_OFF + 1 * RW + (hp + 1) * 128, 128))
                    rs = R_stages(it, cl)
                    ss = S_stages(*nxt) if nxt is not None else []
                    for k in range(max(len(rs), len(ss))):
                        if k < len(ss):
                            ss[k]()
                        if k < len(rs):
                            rs[k]()
                        if gens:
                            if next(gens[0], "done") == "done":
                                gens.pop(0)
            cut("c_rec")
            yks = [f"YF{c}_{h}" for c in range(16) for h in range(2)]
            Y3 = YFflat.rearrange("p (g n) -> p g n", g=32)
            gtmp = rawv[:, 1:S + 1]
            G3 = gtmp.rearrange("p (g n) -> p g n", g=32)
            V3 = VT.rearrange("p c (h n) -> p (c h) n", h=2)
            bc = lambda a: a.unsqueeze(2).to_broadcast([128, 32, 64])
            if hp == 0:
                tap("yscan", YFflat, yks, [128, S])
            tt("pool", s_ss, Sd[0].rearrange("p c h -> p (c h)"), Sd[1].rearrange("p c h -> p (c h)"), ALU.add, reads=["Sd0", "Sd1"], writes=["s_ss"])
            P.op("dve", lambda e: e.tensor_reduce(out=s_sum, in_=Y3, axis=AX.X, op=ALU.add), reads=yks, writes=["s_sum"])
            ts("dve", s_nm, s_sum, -1.0 / 64, ALU.mult, reads=["s_sum"], writes=["s_nm"])
            tt("dve", Y3, Y3, bc(s_nm), ALU.add, reads=yks + ["s_nm"], writes=["YF"])
            act(G3, Y3, AF.Square, reads=["YF", "rawv", "VT"], writes=["rawv"])
            P.op("dve", lambda e: e.tensor_reduce(out=s_var, in_=G3, axis=AX.X, op=ALU.add), reads=["rawv"], writes=["s_var"])
            ts("dve", s_var, s_var, 1.0 / 64, ALU.mult, GN_EPS, ALU.add, reads=["s_var"], writes=["s_var"])
            act(s_var, s_var, AF.Sqrt, reads=["s_var"], writes=["s_var"])
            P.op("dve", lambda e: e.reciprocal(s_var, s_var), reads=["s_var"], writes=["s_var"])
            tt("pool", G3, V3, bc(s_ss), ALU.mult, reads=["VT", "s_ss", "rawv"], writes=["rawv"])
            tt("dve", Y3, Y3, bc(s_var), ALU.mult, reads=["YF", "s_var"], writes=["YF"])
            lg = lnxb[:, 0, :].unsqueeze(1).to_broadcast([128, 16, 128])
            lb = lnxb[:, 1, :].unsqueeze(1).to_broadcast([128, 16, 128])
            tt("dve", YF, YF, lg, ALU.mult, reads=["YF", "lnxb"], writes=["YF"])
            tt("pool", YF, YF, lb, ALU.add, reads=["YF", "lnxb"], writes=["YF"])
            tt("dve", YFflat, YFflat, gtmp, ALU.add, reads=["YF", "rawv"], writes=["YF"])
            if hp == 0:
                tap("ypre", YFflat, ["YF"], [128, S])
            for tb in range(4):
                bsl = slice(tb * 512, (tb + 1) * 512)
                bi = pbank[0] % 2
                pbank[0] += 1
                bk = f"B{bi}"
                mm(banks[bi][:, :], lw_hp[:, 2, :], sg0[:, bsl], start=True, stop=False, reads=["lw2", "sg0"], writes=[bk])
                mm(banks[bi][:, :], lw_hp[0:96, 3, :], sg1[0:96, bsl], start=False, stop=True, reads=["lw3", "sg1"], writes=[bk])
                cp_act(gS, banks[bi][:, :], reads=[bk], writes=["T0"])
                bi = pbank[0] % 2
                pbank[0] += 1
                bk = f"B{bi}"
                for j in range(4):
                    c = tb * 4 + j
                    tr(banks[bi][:, j * 128:(j + 1) * 128], YF[:, c, :], ident, reads=["YF", "CT"], writes=[bk])
                tt("dve", ych[:, bsl], banks[bi][:, :], gS, ALU.mult, reads=[bk, "T0"], writes=["rawk"])
            P.dma("sp", ycat_d[4 + hp], ych, reads=["rawk"], writes=[f"ycat{4 + hp}"])
            if hp == 0:
                tap("ych0", ych, ["rawk"], [128, S], BF16)

        if stop_after is not None and str(stop_after).startswith("hp"):
            P.emit(finals)
            return nc, tap_out

        P.fence()
        A.release(mixer_mark)
        hB = A.alloc(4 * D).rearrange("p (t d) -> p t d", t=4)
        hTs = A.alloc(16 * 512 // 2, BF16).rearrange("p (k t) -> p k t", k=16)
        ffT = A.alloc(44 * 512 // 2, BF16).rearrange("p (f t) -> p f t", f=44)
        ycS = ffT[:, 0:16, :]
        lnp = A.alloc(4 * D).rearrange("p (w d) -> p w d", w=4)
        P.dma("sp", lnp, ln12_d.partition_broadcast(128), writes=["lnp"])
        xres = [A.alloc(512) for _ in range(2)]
        wpc = [A.alloc(4 * 512 // 2, BF16).rearrange("p (j d) -> p j d", j=4) for _ in range(3)]
        gpc = [A.alloc(16 * 256 // 2, BF16).rearrange("p (k c) -> p k c", k=16) for _ in range(2)]
        upc = [A.alloc(16 * 256 // 2, BF16).rearrange("p (k c) -> p k c", k=16) for _ in range(2)]
        sgt = [A.alloc(512) for _ in range(2)]
        junk = A.alloc(D)
        st_ = A.alloc(8)
        wctr2 = [0]
        xctr = [0]
        gctr = [0]

        def layer_norm(tile2d, key, gi):
            P.op("dve", lambda e: e.tensor_reduce(out=st_[:, 0:1], in_=tile2d, axis=AX.X, op=ALU.add), reads=[key], writes=["st0"])
            act(junk, tile2d, AF.Square, reads=[key], writes=["junk"])
            P.op("dve", lambda e: e.tensor_reduce(out=st_[:, 2:3], in_=junk, axis=AX.X, op=ALU.add), reads=["junk"], writes=["st2"])
            ts("dve", st_[:, 1:2], st_[:, 0:1], 1.0 / D, ALU.mult, reads=["st0"], writes=["st1"])
            tt("dve", st_[:, 5:6], st_[:, 1:2], st_[:, 1:2], ALU.mult, reads=["st1"], writes=["st5"])
            stt("dve", st_[:, 3:4], st_[:, 2:3], 1.0 / D, st_[:, 5:6], ALU.mult, ALU.subtract, reads=["st2", "st5"], writes=["st3"])
            ts("dve", st_[:, 3:4], st_[:, 3:4], LN_EPS, ALU.add, reads=["st3"], writes=["st3"])
            act(st_[:, 4:5], st_[:, 3:4], AF.Sqrt, reads=["st3"], writes=["st4"])
            P.op("dve", lambda e: e.reciprocal(st_[:, 4:5], st_[:, 4:5]), reads=["st4"], writes=["st4"])
            stt("dve", st_[:, 6:7], st_[:, 1:2], -1.0, st_[:, 4:5], ALU.mult, ALU.mult, reads=["st1", "st4"], writes=["st6"])
            act(tile2d, tile2d, AF.Identity, bias=st_[:, 6:7], scale=st_[:, 4:5], reads=[key, "st4", "st6"], writes=[key])
            tt("pool", tile2d, tile2d, lnp[:, gi, :], ALU.mult, reads=[key, "lnp"], writes=[key])
            tt("dve", tile2d, tile2d, lnp[:, gi + 1, :], ALU.add, reads=[key, "lnp"], writes=[key])

        n_tb = int(stop_after[2:]) if str(stop_after).startswith("tb") else 4
        for tb in range(n_tb):
            t0 = tb * 512
            for cc in range(16):
                P.dma("sp", ycS[:, cc, :], ycat_d[cc, :, t0:t0 + 512], reads=[f"ycat{cc}"], writes=[f"ffT{cc}"])
            for db in range(4):
                for ccg in range(4):
                    wi = wctr2[0] % 3
                    wctr2[0] += 1
                    P.dma("pool", wpc[wi], wout_d[ccg * 512:(ccg + 1) * 512, db * 512:(db + 1) * 512].rearrange("(j p) d -> p j d", p=128),
                          writes=[f"wpc{wi}"])
                    for j in range(4):
                        cc = ccg * 4 + j
                        for t4 in range(4):
                            mm(banks[t4][:, :], ycS[:, cc, t4 * 128:(t4 + 1) * 128], wpc[wi][:, j, :], start=(cc == 0), stop=(cc == 15),
                               reads=[f"ffT{cc}", f"wpc{wi}"], writes=[f"B{t4}"])
                for t4 in range(4):
                    xi = xctr[0] % 2
                    xctr[0] += 1
                    P.dma("sp", xres[xi], x_d[t0 + t4 * 128:t0 + (t4 + 1) * 128, db * 512:(db + 1) * 512], writes=[f"xres{xi}"])
                    stt("dve", hB[:, t4, db * 512:(db + 1) * 512], xres[xi], ALPHA, banks[t4][:, :], ALU.mult, ALU.add,
                        reads=[f"xres{xi}", f"B{t4}"], writes=[f"hB{t4}"])
            for t4 in range(4):
                layer_norm(hB[:, t4, :], f"hB{t4}", 0)
            if tb == 0:
                tap("h0", hB[:, 0, :], ["hB0"], [128, D])
            for kc in range(16):
                bi = 4 + kc % 2
                bk = f"B{bi}"
                for t4 in range(4):
                    tr(banks[bi][:, t4 * 128:(t4 + 1) * 128], hB[:, t4, kc * 128:(kc + 1) * 128], ident, reads=[f"hB{t4}", "CT"], writes=[bk])
                if kc % 2 == 0:
                    cp_act(hTs[:, kc, :], banks[bi][:, :], reads=[bk], writes=["hTs"])
                else:
                    cp_dve(hTs[:, kc, :], banks[bi][:, :], reads=[bk], writes=["hTs"])
            for fp in range(22):
                gi = gctr[0] % 2
                gctr[0] += 1
                P.dma("pool", gpc[gi], wg_d[:, fp * 256:(fp + 1) * 256].rearrange("(k p) c -> p k c", p=128), writes=[f"gpc{gi}"])
                P.dma("pool", upc[gi], wu_d[:, fp * 256:(fp + 1) * 256].rearrange("(k p) c -> p k c", p=128), writes=[f"upc{gi}"])
                for j in range(2):
                    fc = fp * 2 + j
                    for kc in range(16):
                        mm(banks[6][:, :], gpc[gi][:, kc, j * 128:(j + 1) * 128], hTs[:, kc, :], start=(kc == 0), stop=(kc == 15),
                           reads=[f"gpc{gi}", "hTs"], writes=["B6"])
                    for kc in range(16):
                        mm(banks[7][:, :], upc[gi][:, kc, j * 128:(j + 1) * 128], hTs[:, kc, :], start=(kc == 0), stop=(kc == 15),
                           reads=[f"upc{gi}", "hTs"], writes=["B7"])
                    si = fc % 2
                    act(sgt[si], banks[6][:, :], AF.Silu, reads=["B6"], writes=[f"sgt{si}"])
                    tt("dve", ffT[:, fc, :], sgt[si], banks[7][:, :], ALU.mult, reads=[f"sgt{si}", "B7"], writes=[f"ffT{fc}"])
            for db in range(4):
                for fg in range(11):
                    wi = wctr2[0] % 3
                    wctr2[0] += 1
                    P.dma("pool", wpc[wi], wd_d[fg * 512:(fg + 1) * 512, db * 512:(db + 1) * 512].rearrange("(j p) d -> p j d", p=128),
                          writes=[f"wpc{wi}"])
                    for j in range(4):
                        fc = fg * 4 + j
                        for t4 in range(4):
                            mm(banks[t4][:, :], ffT[:, fc, t4 * 128:(t4 + 1) * 128], wpc[wi][:, j, :], start=(fc == 0), stop=(fc == 43),
                               reads=[f"ffT{fc}", f"wpc{wi}"], writes=[f"B{t4}"])
                for t4 in range(4):
                    stt("dve", hB[:, t4, db * 512:(db + 1) * 512], hB[:, t4, db * 512:(db + 1) * 512], ALPHA, banks[t4][:, :], ALU.mult, ALU.add,
                        reads=[f"hB{t4}", f"B{t4}"], writes=[f"hB{t4}"])
            for t4 in range(4):
                layer_norm(hB[:, t4, :], f"hB{t4}", 2)
                finals.append(P.dma("sp", out_d[t0 + t4 * 128:t0 + (t4 + 1) * 128, :], hB[:, t4, :], reads=[f"hB{t4}"]))
        P.emit(finals)
        build_program.peak = A.peak
    return nc, tap_out


def _host_inputs(inp):
    f = lambda a: np.ascontiguousarray(np.asarray(a, dtype=np.float32))
    cp = np.zeros((128, NCP), np.float32)
    mu = f(inp["mu_shift"])[0]
    o = lambda off: off - R_OFF
    for hp in range(NHP):
        sl = slice(hp * 128, (hp + 1) * 128)
        cb = hp * 10
        cp[:, cb + 0] = mu[o(R_OFF) + hp * 128: o(R_OFF) + (hp + 1) * 128]
        cp[:, cb + 1] = mu[o(K_OFF) + hp * 128: o(K_OFF) + (hp + 1) * 128]
        cp[:, cb + 2] = mu[o(V_OFF) + hp * 128: o(V_OFF) + (hp + 1) * 128]
        cp[:, cb + 3] = f(inp["w0_f"])[0][sl]
        cp[:, cb + 4] = f(inp["w0_b"])[0][sl]
        cp[:, cb + 5] = f(inp["a0_f"])[0][sl]
        cp[:, cb + 6] = f(inp["a0_b"])[0][sl]
        cp[:, cb + 7] = f(inp["k_k"])[0][sl]
        cp[:, cb + 8] = f(inp["k_a"])[0][sl]
        cp[:, cb + 9] = f(inp["r_k"])[0].reshape(-1)[sl]
    cp[:, 120] = mu[o(G_OFF):o(G_OFF) + 128]
    cp[:96, 121] = mu[o(G_OFF) + 128:o(G_OFF) + 224]
    cp[:, 122] = mu[o(WF_OFF):o(WF_OFF) + 128]
    cp[:, 123] = mu[o(AF_OFF):o(AF_OFF) + 128]
    cs, ss, dn = _host_dft()
    shared = {
        "w_in": f(inp["w_in"])[0], "w_out": f(inp["w_out"])[0], "w_gate": f(inp["w_ffn_gate"])[0],
        "w_up": f(inp["w_ffn_up"])[0], "w_down": f(inp["w_ffn_down"])[0],
        "wupfb": np.ascontiguousarray(np.concatenate([f(inp["w_up_f"])[0], f(inp["w_up_b"])[0]], 0)),
        "aupfb": np.ascontiguousarray(np.concatenate([f(inp["a_up_f"])[0], f(inp["a_up_b"])[0]], 0)),
        "g_up": f(inp["g_up"])[0], "cp": cp,
        "lnx": np.ascontiguousarray(np.stack([f(inp["lnx_g"])[0], f(inp["lnx_b"])[0]], 0)),
        "ln12": np.ascontiguousarray(np.stack([f(inp["ln1_g"])[0], f(inp["ln1_b"])[0], f(inp["ln2_g"])[0], f(inp["ln2_b"])[0]], 0)),
        "consts": _host_consts(), "dftC": cs, "dftS": ss, "dftN": dn,
    }
    return shared


def kernel(**inputs):
    x = np.asarray(inputs["x"], dtype=np.float32)
    shared = _host_inputs(inputs)
    nc, _ = build_program()
    in_maps = []
    for b in range(8):
        m = dict(shared)
        m["x"] = np.ascontiguousarray(x[b])
        m["xT"] = np.ascontiguousarray(x[b].T)
        in_maps.append(m)
    res = run_bass_kernel_spmd(nc, in_maps, core_ids=list(range(8)))
    return np.stack([np.asarray(r["out"], dtype=np.float32) for r in res.results], 0)
```
